# Optimizing a Trainium2 kernel written in Bass

```python
import functools
import jax, jax.numpy as jnp
from jax import lax
import numpy as np

D_MODEL = 1024
BATCH = 16
SEQ = 256
DEPTH = 1
DEC_BATCH = 2
DEC_SEQ = 2048
PAST_LEN = 512

GRID_W = 64
HEAD_DIM = 64
N_ATT_HEADS = 8
N_RWKV_HEADS = 8
ATT_WIDTH = N_ATT_HEADS * HEAD_DIM
RWKV_WIDTH = N_RWKV_HEADS * HEAD_DIM
MIX_WIDTH = ATT_WIDTH + RWKV_WIDTH
NA_ROWS = 8
NA_COLS = 16
N_DIRS = 2
DECAY_LORA = 64
AAA_LORA = 64
GATE_LORA = 128
SHIFT_WIDTH = 3
FFN_CONV_WIDTH = 3
D_FF = 2816
EPS = 1e-6
GN_EPS = 64e-5
ATT_SCALE = HEAD_DIM ** -0.5
RWKV_COLS = 3 * RWKV_WIDTH + N_DIRS * (DECAY_LORA + AAA_LORA) + GATE_LORA
IN_WIDTH = 3 * ATT_WIDTH + RWKV_COLS

kernel_name = 'hymba_natten_rwkv7_dit_step'


def _rmsnorm(x, g):
    xf = x.astype(jnp.float32)
    y = xf * lax.rsqrt(jnp.mean(xf * xf, axis=-1, keepdims=True) + EPS)
    return (y * g.astype(jnp.float32)).astype(x.dtype)


def _dwconv(x, w):
    K = w.shape[0]
    pad = K // 2
    T = x.shape[1]
    xp = jnp.pad(x, ((0, 0), (pad, pad), (0, 0)))
    return sum(xp[:, j:j + T] * w[j] for j in range(K))


def _modulation(cond, w_ada, b_ada):
    m = jax.nn.silu(cond) @ w_ada + b_ada
    return jnp.split(m[..., None, :], 6, axis=-1)


def _heads(x, n):
    return x.reshape(x.shape[:-1] + (n, HEAD_DIM))


def _context_attention(q, k, v):
    B, L = q.shape[:2]
    s = jnp.einsum('bqhd,bkhd->bhqk', q, k).astype(jnp.float32) * ATT_SCALE
    p = jax.nn.softmax(s, axis=-1).astype(v.dtype)
    return jnp.einsum('bhqk,bkhd->bqhd', p, v).reshape(B, L, ATT_WIDTH)


def _neighbourhood_attention(q, k, v, k_ctx, v_ctx, rpb):
    B, T, H, Dh = q.shape
    R = T // GRID_W
    KH = min(NA_ROWS, R)
    KW = NA_COLS
    qg = q.reshape(B, R, GRID_W, H, Dh)
    kg = k.reshape(B, R, GRID_W, H, Dh)
    vg = v.reshape(B, R, GRID_W, H, Dh)
    k_ctx = k_ctx.astype(q.dtype)
    v_ctx = v_ctx.astype(v.dtype)
    cols = jnp.arange(GRID_W)
    col_start = jnp.clip(cols - KW // 2, 0, GRID_W - KW)
    col_idx = col_start[:, None] + jnp.arange(KW)[None, :]
    col_off = col_idx - cols[:, None] + (NA_COLS - 1)
    rpb_cols = rpb[:, :, col_off].astype(jnp.float32)

    def row_block(i):
        si = jnp.clip(i - KH // 2, 0, R - KH)
        q_i = lax.dynamic_index_in_dim(qg, i, axis=1, keepdims=False)
        k_win = lax.dynamic_slice_in_dim(kg, si, KH, axis=1)[:, :, col_idx]
        v_win = lax.dynamic_slice_in_dim(vg, si, KH, axis=1)[:, :, col_idx]
        row_off = si + jnp.arange(KH) - i + (NA_ROWS - 1)
        bias = jnp.take(rpb_cols, row_off, axis=1)
        s_loc = (jnp.einsum('bwhd,bawkhd->bhwak', q_i, k_win).astype(jnp.float32) * ATT_SCALE
                 + jnp.transpose(bias, (0, 2, 1, 3))[None])
        s_loc = s_loc.reshape(B, H, GRID_W, KH * KW)
        s_ctx = jnp.einsum('bwhd,blhd->bhwl', q_i, k_ctx).astype(jnp.float32) * ATT_SCALE
        p = jax.nn.softmax(jnp.concatenate([s_loc, s_ctx], axis=-1), axis=-1).astype(v.dtype)
        p_loc = p[..., :KH * KW].reshape(B, H, GRID_W, KH, KW)
        p_ctx = p[..., KH * KW:]
        return (jnp.einsum('bhwak,bawkhd->bwhd', p_loc, v_win)
                + jnp.einsum('bhwl,blhd->bwhd', p_ctx, v_ctx))

    o = lax.map(row_block, jnp.arange(R))
    return jnp.moveaxis(o, 0, 1).reshape(B, T, H * Dh)


def _wkv_scan(S0, r, decay, kk, kka, v, k, reverse):
    def step(S, inp):
        r_t, w_t, kk_t, b_t, v_t, k_t = inp
        sa = jnp.einsum('bhvk,bhk->bhv', S, kk_t)
        S = S * w_t[:, :, None, :] - sa[..., None] * b_t[:, :, None, :] + v_t[..., None] * k_t[:, :, None, :]
        return S, jnp.einsum('bhvk,bhk->bhv', S, r_t)
    return lax.scan(step, S0, (r, decay, kk, kka, v, k), reverse=reverse)


def _rwkv_mix(u, S0, w0, w2, a0, a2, g2, k_k, k_a, r_k, ln_g, ln_b):
    f32 = jnp.float32
    u = u.astype(f32)
    B, T, _ = u.shape
    H = N_RWKV_HEADS
    r = u[..., :RWKV_WIDTH]
    kr = u[..., RWKV_WIDTH:2 * RWKV_WIDTH]
    v = u[..., 2 * RWKV_WIDTH:3 * RWKV_WIDTH]
    o = 3 * RWKV_WIDTH
    xw = u[..., o:o + N_DIRS * DECAY_LORA].reshape(B, T, N_DIRS, DECAY_LORA)
    o += N_DIRS * DECAY_LORA
    xa = u[..., o:o + N_DIRS * AAA_LORA].reshape(B, T, N_DIRS, AAA_LORA)
    o += N_DIRS * AAA_LORA
    xg = u[..., o:o + GATE_LORA]
    w_log = -jax.nn.softplus(-(w0.astype(f32) + jnp.einsum('btnl,nlc->btnc', jnp.tanh(xw), w2.astype(f32)))) - 0.5
    decay = jnp.exp(-jnp.exp(w_log))
    a = jax.nn.sigmoid(a0.astype(f32) + jnp.einsum('btnl,nlc->btnc', xa, a2.astype(f32)))
    g = jax.nn.sigmoid(xg) @ g2.astype(f32)
    kk = _heads(kr * k_k.astype(f32), H)
    kk = kk * lax.rsqrt(jnp.sum(kk * kk, axis=-1, keepdims=True) + 1e-12)
    kd = kr[:, :, None] * (1.0 + (a - 1.0) * k_a.astype(f32))
    r_h = _heads(r, H)
    v_h = _heads(v, H)
    decay_h = _heads(decay, H)
    a_h = _heads(a, H)
    kd_h = _heads(kd, H)
    tm = lambda t: jnp.moveaxis(t, 1, 0)
    S_f, y_f = _wkv_scan(S0[:, 0].astype(f32), tm(r_h), tm(decay_h[:, :, 0]), tm(kk),
                         tm(kk * a_h[:, :, 0]), tm(v_h), tm(kd_h[:, :, 0]), reverse=False)
    S_b, y_b = _wkv_scan(S0[:, 1].astype(f32), tm(r_h), tm(decay_h[:, :, 1]), tm(kk),
                         tm(kk * a_h[:, :, 1]), tm(v_h), tm(kd_h[:, :, 1]), reverse=True)
    y = jnp.moveaxis(y_f + y_b, 0, 1)
    mu = jnp.mean(y, axis=-1, keepdims=True)
    var = jnp.mean(jnp.square(y - mu), axis=-1, keepdims=True)
    y = ((y - mu) * lax.rsqrt(var + GN_EPS)).reshape(B, T, RWKV_WIDTH) * ln_g.astype(f32) + ln_b.astype(f32)
    bonus = jnp.sum(jnp.sum(r_h[:, :, None] * kd_h * r_k.astype(f32), axis=-1, keepdims=True) * v_h[:, :, None], axis=2)
    out = (y + bonus.reshape(B, T, RWKV_WIDTH)) * g
    return out, jnp.stack([S_f, S_b], axis=1)


def _conv_ffn(h, w1, w3, wc, w2):
    return (jax.nn.silu(_dwconv(h @ w1, wc)) * (h @ w3)) @ w2


def _layer(x, mod, attend, S0, w_in, w_ts, rp, w_out, norm1, norm2, fp):
    sh1, sc1, g1, sh2, sc2, g2 = mod
    h = _rmsnorm(x, norm1) * (1 + sc1) + sh1
    z = h @ w_in
    q = _heads(z[..., :ATT_WIDTH], N_ATT_HEADS)
    k = _heads(z[..., ATT_WIDTH:2 * ATT_WIDTH], N_ATT_HEADS)
    v = _heads(z[..., 2 * ATT_WIDTH:3 * ATT_WIDTH], N_ATT_HEADS)
    a_out = attend(q, k, v)
    u = _dwconv(z[..., 3 * ATT_WIDTH:], w_ts)
    r_out, S = _rwkv_mix(u, S0, *rp)
    x = x + g1 * (jnp.concatenate([a_out, r_out.astype(x.dtype)], axis=-1) @ w_out)
    h2 = _rmsnorm(x, norm2) * (1 + sc2) + sh2
    x = x + g2 * _conv_ffn(h2, *fp)
    return x, k, v, S


def setup_inputs(seed: int = 0) -> dict:
    key = jax.random.key(seed)
    ks = jax.random.split(key, 32)
    n = lambda i, shape, s: jax.random.normal(ks[i], shape, jnp.float32) * s
    D = D_MODEL
    return {
        'x_prompt': n(0, (BATCH, SEQ, D), 1.0),
        'x_sample': n(1, (DEC_BATCH, DEC_SEQ, D), 1.0),
        'cache_k': n(2, (DEC_BATCH, DEPTH, PAST_LEN, N_ATT_HEADS, HEAD_DIM), 1.0),
        'cache_v': n(3, (DEC_BATCH, DEPTH, PAST_LEN, N_ATT_HEADS, HEAD_DIM), 1.0),
        'state_rwkv': n(4, (DEC_BATCH, DEPTH, N_DIRS, N_RWKV_HEADS, HEAD_DIM, HEAD_DIM), 0.3),
        'c': n(5, (DEC_BATCH, D), 1.0),
        'c_ctx': n(6, (D,), 1.0),
        'w_ada': n(7, (DEPTH, D, 6 * D), 0.5 * D ** -0.5),
        'b_ada': n(8, (DEPTH, 6 * D), 0.02),
        'norm1': 1.0 + n(9, (DEPTH, D), 0.05),
        'norm2': 1.0 + n(10, (DEPTH, D), 0.05),
        'w_in': n(11, (DEPTH, D, IN_WIDTH), D ** -0.5),
        'w_ts': n(12, (DEPTH, SHIFT_WIDTH, RWKV_COLS), SHIFT_WIDTH ** -0.5),
        'w0': n(13, (DEPTH, N_DIRS, RWKV_WIDTH), 0.5),
        'w2': n(14, (DEPTH, N_DIRS, DECAY_LORA, RWKV_WIDTH), 0.5 * DECAY_LORA ** -0.5),
        'a0': n(15, (DEPTH, N_DIRS, RWKV_WIDTH), 0.5),
        'a2': n(16, (DEPTH, N_DIRS, AAA_LORA, RWKV_WIDTH), AAA_LORA ** -0.5),
        'g2': n(17, (DEPTH, GATE_LORA, RWKV_WIDTH), GATE_LORA ** -0.5),
        'k_k': 0.85 + n(18, (DEPTH, RWKV_WIDTH), 0.05),
        'k_a': 1.0 + n(19, (DEPTH, RWKV_WIDTH), 0.05),
        'r_k': n(20, (DEPTH, N_RWKV_HEADS, HEAD_DIM), 0.1),
        'ln_x_g': 1.0 + n(21, (DEPTH, RWKV_WIDTH), 0.05),
        'ln_x_b': n(22, (DEPTH, RWKV_WIDTH), 0.02),
        'rpb': n(23, (DEPTH, N_ATT_HEADS, 2 * NA_ROWS - 1, 2 * NA_COLS - 1), 0.1),
        'w_out': n(24, (DEPTH, MIX_WIDTH, D), MIX_WIDTH ** -0.5),
        'w_ffn1': n(25, (DEPTH, D, D_FF), D ** -0.5),
        'w_ffn3': n(26, (DEPTH, D, D_FF), D ** -0.5),
        'w_ffn_conv': n(27, (DEPTH, FFN_CONV_WIDTH, D_FF), FFN_CONV_WIDTH ** -0.5),
        'w_ffn2': n(28, (DEPTH, D_FF, D), D_FF ** -0.5),
        'norm_f': 1.0 + n(29, (D,), 0.05),
    }


def reference(x_prompt, x_sample, cache_k, cache_v, state_rwkv, c, c_ctx, w_ada, b_ada, norm1, norm2,
              w_in, w_ts, w0, w2, a0, a2, g2, k_k, k_a, r_k, ln_x_g, ln_x_b, rpb, w_out,
              w_ffn1, w_ffn3, w_ffn_conv, w_ffn2, norm_f):
    xp = x_prompt
    xs = x_sample
    new_k, new_v, new_s = [], [], []
    zero_state = jnp.zeros((xp.shape[0], N_DIRS, N_RWKV_HEADS, HEAD_DIM, HEAD_DIM), jnp.float32)
    for l in range(DEPTH):
        rp = (w0[l], w2[l], a0[l], a2[l], g2[l], k_k[l], k_a[l], r_k[l], ln_x_g[l], ln_x_b[l])
        fp = (w_ffn1[l], w_ffn3[l], w_ffn_conv[l], w_ffn2[l])
        mod_ctx = _modulation(c_ctx[None, :], w_ada[l], b_ada[l])
        xp, kc, vc, sc = _layer(xp, mod_ctx, _context_attention, zero_state, w_in[l], w_ts[l], rp,
                                w_out[l], norm1[l], norm2[l], fp)
        new_k.append(kc)
        new_v.append(vc)
        new_s.append(sc)
        mod_lat = _modulation(c, w_ada[l], b_ada[l])
        attend = functools.partial(_neighbourhood_attention, k_ctx=cache_k[:, l], v_ctx=cache_v[:, l], rpb=rpb[l])
        xs, _, _, _ = _layer(xs, mod_lat, attend, state_rwkv[:, l], w_in[l], w_ts[l], rp,
                             w_out[l], norm1[l], norm2[l], fp)
    y_prompt = _rmsnorm(xp, norm_f)
    y_sample = _rmsnorm(xs, norm_f)
    new_cache_k = jnp.stack(new_k, axis=1)
    new_cache_v = jnp.stack(new_v, axis=1)
    new_state_rwkv = jnp.stack(new_s, axis=1)
    return (y_prompt, y_sample, new_cache_k, new_cache_v, new_state_rwkv)
```

```python
import numpy as np
from contextlib import ExitStack
import concourse.bass as bass
import concourse.mybir as mybir
from concourse.bass_utils import run_bass_kernel_spmd

F32 = mybir.dt.float32
BF16 = mybir.dt.bfloat16
AF = mybir.ActivationFunctionType
ALU = mybir.AluOpType

D = 1024
NEG = -30000.0
EPS = 1e-6
GN_EPS = 64e-5

PV_NORM1, PV_NORM2, PV_NORMF, PV_BADA, PV_WTS, PV_W0, PV_A0 = 0, 8, 16, 24, 72, 117, 125
PV_KK, PV_KA, PV_RK, PV_LNG, PV_LNB, PV_WC, PV_C = 133, 137, 141, 145, 149, 153, 219
PV_ROWS = 256


class _Eng:
    def __init__(self, name, handle, sem):
        self.name = name
        self.h = handle
        self.sem = sem
        self.count = 0
        self.waited = {}
        self.dma_rr = 0


class FW:
    def __init__(self, nc, es, n_dma_sems=10):
        self.nc = nc
        mk = lambda n: es.enter_context(nc.semaphore(n))
        self.mk = mk
        self.E = {
            'pe': _Eng('pe', nc.tensor, mk('s_pe')),
            'act': _Eng('act', nc.scalar, mk('s_act')),
            'dve': _Eng('dve', nc.vector, mk('s_dve')),
            'pool': _Eng('pool', nc.gpsimd, mk('s_pool')),
            'sp': _Eng('sp', nc.sync, mk('s_sp')),
        }
        self.dma_sems = {}
        for q in ('sp', 'pool'):
            self.dma_sems[q] = [[mk('d_%s%d' % (q, i)), 0] for i in range(n_dma_sems)]
        self.last_write = {}
        self.readers = {}
        self.semobj = {}
        self.nops = 0
        self.bank_rr = 0

    def _need(self, needed, tok):
        if tok is None:
            return
        sem, val, owner = tok
        k = id(sem)
        self.semobj[k] = sem
        if needed.get(k, (0, None))[0] < val:
            needed[k] = (val, owner)

    def _deps(self, eng, reads, writes):
        needed = {}
        for r in reads:
            self._need(needed, self.last_write.get(r))
        for w in writes:
            self._need(needed, self.last_write.get(w))
            for t in self.readers.get(w, {}).values():
                self._need(needed, t)
        e = self.E[eng]
        for k, (val, owner) in needed.items():
            if owner == 'pe' and eng == 'pe':
                continue
            if e.waited.get(k, 0) < val:
                e.h.wait_ge(self.semobj[k], val)
                e.waited[k] = val

    def _track(self, tok, reads, writes):
        owner = tok[2]
        for w in writes:
            self.last_write[w] = tok
            self.readers[w] = {}
        for r in reads:
            if r in writes:
                continue
            self.readers.setdefault(r, {})[owner] = tok

    capture = None

    def replay(self, lst, n):
        for _ in range(min(n, len(lst))):
            eng, fn, reads, writes = lst.pop(0)
            self.op(eng, fn, reads, writes)

    def async_op(self, eng, fn, reads=(), writes=()):
        e = self.E[eng]
        self._deps(eng, reads, writes)
        sem = self.mk('cc_sem%d' % self.nops)
        ins = fn(e.h)
        ins.then_inc(sem, 1)
        self._track((sem, 1, 'cc%d' % self.nops), reads, writes)
        self.nops += 1
        return ins

    def op(self, eng, fn, reads=(), writes=()):
        if self.capture is not None:
            self.capture.append((eng, fn, list(reads), list(writes)))
            return None
        reads = [r.key if isinstance(r, LazyBank) else r for r in reads]
        writes = [r.key if isinstance(r, LazyBank) else r for r in writes]
        e = self.E[eng]
        bk = [r for r in reads if isinstance(r, str) and r.startswith('bank') and r not in writes]
        if bk:
            writes = list(writes) + bk
        self._deps(eng, reads, writes)
        ins = fn(e.h)
        e.count += 1
        ins.then_inc(e.sem, 1)
        self._track((e.sem, e.count, eng), reads, writes)
        self.nops += 1
        return ins

    def dma(self, q, out, in_, reads=(), writes=(), **kw):
        e = self.E[q]
        self._deps(q, reads, writes)
        slots = self.dma_sems[q]
        i = e.dma_rr % len(slots)
        e.dma_rr += 1
        sem, val = slots[i]
        k = id(sem)
        self.semobj[k] = sem
        if val > 0 and e.waited.get(k, 0) < val:
            e.h.wait_ge(sem, val)
            e.waited[k] = val
        ins = e.h.dma_start(out=out, in_=in_, **kw)
        ins.then_inc(sem, 16)
        slots[i][1] = val + 16
        self._track((sem, val + 16, 'dma_%s_%d' % (q, i)), reads, writes)
        self.nops += 1
        return ins

    def finish(self, eng='sp'):
        e = self.E[eng]
        for q, slots in self.dma_sems.items():
            for sem, val in slots:
                if val > 0 and e.waited.get(id(sem), 0) < val:
                    e.h.wait_ge(sem, val)
                    e.waited[id(sem)] = val
        for n, o in self.E.items():
            if n != eng and o.count > 0 and e.waited.get(id(o.sem), 0) < o.count:
                e.h.wait_ge(o.sem, o.count)
                e.waited[id(o.sem)] = o.count


class Ctx:
    pass


class LazyBank:
    def __init__(self, K):
        self.K = K
        self.v = None

    def get(self):
        if self.v is None:
            self.v = bank(self.K)
        return self.v

    @property
    def ps(self):
        return self.get()[0]

    @property
    def key(self):
        return self.get()[1]


def build_nc(stage=99, dbg=()):
    nc = bass.Bass("TRN2", target_bir_lowering=False)
    K = Ctx()
    K.nc = nc
    K.dbg = set(dbg)
    K.dbg_specs = {}

    def din(name, shape):
        return nc.dram_tensor(name, list(shape), F32, kind="ExternalInput").ap()

    def dout(name, shape):
        return nc.dram_tensor(name, list(shape), F32, kind="ExternalOutput").ap()

    I = Ctx()
    I.xp = din("xp", [512, D]); I.xs = din("xs", [512, D]); I.xh = din("xh", [448, D])
    I.ck = din("ck", [512, 512]); I.cv = din("cv", [512, 512]); I.s0 = din("s0", [2, 8, 64, 64])
    I.pvec = din("pvec", [PV_ROWS, 128])
    I.w_ada_sh = din("w_ada_sh", [D, 1536]); I.w_in = din("w_in", [D, 3456])
    I.w2 = din("w2", [2, 64, 512]); I.a2 = din("a2", [2, 64, 512]); I.g2 = din("g2", [128, 512])
    I.w_out = din("w_out", [D, D]); I.w1 = din("w1", [D, 2816]); I.w3 = din("w3", [D, 2816])
    I.wf2 = din("wf2", [2816, D])
    I.bt = din("bt", [128, 2 * 8 * 8 * 64]); I.rbp = din("rbp", [128, 64])
    I.cst = din("cst", [128, 16]); I.ident = din("ident", [128, 128])
    I.mq = din("mq", [128, 2 * 512]); I.mb = din("mb", [128, 2 * 128]); I.bd1 = din("bd1", [128, 128])
    I.rst = din("rst", [128, 512]); I.xi = din("xi", [128, 128])
    O = Ctx()
    O.yp = dout("yp", [512, D]); O.ys = dout("ys", [512, D]); O.nk = dout("nk", [512, 512])
    O.nv = dout("nv", [512, 512]); O.ns = dout("ns", [2, 2, 8, 64, 64])
    K.I = I; K.O = O

    with ExitStack() as es:
        K.es = es
        fw = FW(nc, es)
        K.fw = fw
        K.cur_es = es
        K.sb = lambda n, s, d=F32: K.cur_es.enter_context(nc.sbuf_tensor("sb_" + n, list(s), d))
        K.banks = [es.enter_context(nc.psum_tensor("bank%d" % i, [128, 512], F32)) for i in range(8)]
        emit_program(K, stage)
        fw.finish('sp')
    K.nops = fw.nops
    return nc, K


def bank(K):
    fw = K.fw
    i = fw.bank_rr % 8
    fw.bank_rr += 1
    return K.banks[i], 'bank%d' % i


def dump(K, name, ap, shape, reads):
    if name not in K.dbg:
        return
    t = K.nc.dram_tensor("dbg_" + name, list(shape), ap.dtype, kind="ExternalOutput").ap()
    K.fw.dma('sp', t, ap, reads=reads)
    K.dbg_specs[name] = shape


def emit_program(K, stage):
    nc, fw, I, O, sb = K.nc, K.fw, K.I, K.O, K.sb
    ident_f = sb("ident_f", [128, 128]); ident_b = sb("ident_b", [128, 128], BF16)
    ones_b = sb("ones_b", [128, 128], BF16)
    bd1_b = sb("bd1_b", [128, 128], BF16)
    cst = sb("cst", [128, 16])
    PV = sb("PV", [128, PV_ROWS])
    K.ident_f, K.ident_b, K.ones_b, K.bd1_b, K.cst, K.PV = ident_f, ident_b, ones_b, bd1_b, cst, PV
    fw.dma('sp', ident_f[:], I.ident[:, :], writes=['ident_f'])
    fw.dma('pool', ident_b[:], I.ident[:, :], writes=['ident_b'])
    fw.dma('pool', bd1_b[:], I.bd1[:, :], writes=['bd1_b'])
    fw.dma('sp', cst[:], I.cst[:, :], writes=['cst'])
    fw.op('pool', lambda e: e.memset(ones_b[:], 1.0), writes=['ones_b'])
    epsc = sb("epsc", [128, 2])
    fw.op('pool', lambda e: e.memset(epsc[:, 0:1], EPS), writes=['epsc'])
    fw.op('pool', lambda e: e.memset(epsc[:, 1:2], GN_EPS), writes=['epsc'])
    K.epsc = epsc

    pv_rows = sb("pv_rows", [128, 2, 128])
    fw.dma('sp', pv_rows[:], I.pvec.rearrange("(a p) f -> p a f", p=128), writes=['pv_rows'])
    for a in range(2):
        ps, pk = bank(K)
        fw.op('pe', lambda t, a=a, ps=ps: t.transpose(ps[:, 0:128], pv_rows[:, a, :], ident_f[:]),
              reads=['pv_rows', 'ident_f'], writes=[pk])
        fw.op('dve', lambda e, a=a, ps=ps: e.tensor_copy(out=PV[:, a * 128:(a + 1) * 128], in_=ps[:, 0:128]),
              reads=[pk], writes=['PV'])

    MOD = sb("MOD", [128, 48, 2])
    cs = sb("cs", [128, 24], BF16)
    K.MOD = MOD
    fw.op('act', lambda e: e.activation(out=cs[:], in_=PV[:, PV_C:PV_C + 24], func=AF.Silu), reads=['PV'], writes=['cs'])
    K.wslab_rr = 0
    K.AB = sb("AB", [128, 2, 8, 2])
    K.cs = cs
    if stage <= 0:
        return
    emit_front(K, stage)


def barrier(K):
    fw = K.fw
    for n, e in fw.E.items():
        for q, slots in fw.dma_sems.items():
            for sem, val in slots:
                if val > 0 and e.waited.get(id(sem), 0) < val:
                    e.h.wait_ge(sem, val)
                    e.waited[id(sem)] = val
        for n2, o in fw.E.items():
            if n2 != n and o.count > 0 and e.waited.get(id(o.sem), 0) < o.count:
                e.h.wait_ge(o.sem, o.count)
                e.waited[id(o.sem)] = o.count


class Scope:
    def __init__(self, K):
        self.K = K

    def __enter__(self):
        self.prev = self.K.cur_es
        self.es = ExitStack()
        self.es.__enter__()
        self.K.cur_es = self.es
        return self

    def __exit__(self, *a):
        barrier(self.K)
        self.K.cur_es = self.prev
        return self.es.__exit__(*a)


def evac(K, eng, out, in_, reads, writes, scale=None):
    if eng == 'act':
        if scale is None:
            return K.fw.op('act', lambda e: e.activation(out=out, in_=in_, func=AF.Copy), reads=reads, writes=writes)
        return K.fw.op('act', lambda e: e.activation(out=out, in_=in_, func=AF.Identity, scale=scale), reads=reads, writes=writes)
    if scale is None:
        return K.fw.op(eng, lambda e: e.tensor_copy(out=out, in_=in_), reads=reads, writes=writes)
    return K.fw.op(eng, lambda e: e.tensor_scalar(out=out, in0=in_, scalar1=scale, scalar2=None, op0=ALU.mult), reads=reads, writes=writes)


def load_w(K, parts):
    si = K.wslab_rr % 2
    K.wslab_rr += 1
    slab = K.wslab[si]
    key = 'wslab%d' % (si if K.wslab[0] is not K.wslab[1] else 0)
    c = 0
    for ap, n in parts:
        K.fw.dma('pool', slab[:, :, c:c + n], ap.rearrange("(k p) c -> p k c", p=128), writes=[key])
        c += n
    return slab, key


def emit_mod_load(K):
    nc, fw, I, sb = K.nc, K.fw, K.I, K.sb
    PV, cst, MOD, cs = K.PV, K.cst, K.MOD, K.cs
    md_in = nc.dram_tensor("md_in", [128, 48], F32)
    md_out = nc.dram_tensor("md_out", [512, 48], F32)
    wslab = [sb("wslab%d" % i, [128, 8, 768], BF16) for i in range(2)]
    K.wslab = wslab
    MP = sb("MP", [128, 12, 4]); MG = sb("MG", [128, 48, 4])
    fw.op('pool', lambda e: e.memset(MP[:].rearrange("p a b -> p (a b)"), 0.0), writes=['MP'])
    for g in range(2):
        fw.dma('pool', wslab[g][:], I.w_ada_sh[:, g * 768:(g + 1) * 768].rearrange("(k p) c -> p k c", p=128), writes=['wslab%d' % g])
    K.mod_tiles = (wslab, MP, MG, md_in, md_out)


def emit_mod_compute(K):
    nc, fw, I, sb = K.nc, K.fw, K.I, K.sb
    PV, cst, MOD, cs = K.PV, K.cst, K.MOD, K.cs
    wslab, MP, MG, md_in, md_out = K.mod_tiles
    modps, modk = bank(K)
    for g in range(2):
        slab = wslab[g]
        for m in range(6):
            mm = g * 6 + m
            for k in range(8):
                fw.op('pe', lambda t, m=m, k=k, mm=mm, slab=slab: t.matmul(modps[:, 3 * mm:3 * mm + 3], slab[:, k, m * 128:(m + 1) * 128], cs[:, k:24:8],
                                                                         start=(k == 0), stop=(k == 7)), reads=['wslab%d' % g, 'cs'], writes=[modk])
    fw.op('dve', lambda e: e.tensor_tensor(out=MP[:, :, 0:3], in0=modps[:, 0:36].rearrange("p (m c) -> p m c", c=3),
                                           in1=PV[:, PV_BADA:PV_BADA + 12].unsqueeze(2).broadcast_to([128, 12, 3]), op=ALU.add),
          reads=[modk, 'PV'], writes=['MP'])
    fw.dma('pool', md_in.ap(), MP[:].rearrange("p a b -> p (a b)"), reads=['MP'], writes=['md_in'])
    fw.async_op('pool', lambda g: g.collective_compute("AllGather", ALU.bypass, replica_groups=[[0, 1, 2, 3], [4, 5, 6, 7]],
                                                 ins=[md_in.ap().opt()], outs=[md_out.ap().opt()]), reads=['md_in'], writes=['md_out'])
    fw.dma('pool', MG[:].rearrange("p (r m) c -> p r (m c)", r=4), md_out.ap().rearrange("(r p) f -> p r f", p=128), reads=['md_out'], writes=['MG'])
    fw.op('dve', lambda e: e.tensor_copy(out=MOD[:, :, 0], in_=MG[:, :, 0]), reads=['MG'], writes=['MOD'])
    fw.op('dve', lambda e: e.tensor_scalar(out=MOD[:, :, 1], in0=MG[:, :, 1], scalar1=cst[:, 14:15], scalar2=None, op0=ALU.mult), reads=['MG', 'cst'], writes=['MOD'])
    fw.op('dve', lambda e: e.scalar_tensor_tensor(out=MOD[:, :, 1], in0=MG[:, :, 2], scalar=cst[:, 15:16], in1=MOD[:, :, 1], op0=ALU.mult, op1=ALU.add),
          reads=['MG', 'cst', 'MOD'], writes=['MOD'])
    AB = K.AB
    for which, (sc_m, nrow) in enumerate(((8, PV_NORM1), (32, PV_NORM2))):
        for cond in range(2):
            fw.op('dve', lambda e, which=which, sc_m=sc_m, nrow=nrow, cond=cond: e.scalar_tensor_tensor(
                out=AB[:, which, :, cond], in0=MOD[:, sc_m:sc_m + 8, cond], scalar=1.0,
                in1=PV[:, nrow:nrow + 8], op0=ALU.add, op1=ALU.mult),
                reads=['MOD', 'PV'], writes=['AB'])
    dump(K, "MOD", MOD[:], [128, 48, 2], ['MOD'])


def emit_front(K, stage):
    nc, fw, I, O, sb = K.nc, K.fw, K.I, K.O, K.sb
    xT = sb("xT", [128, 8, 1024])
    mixT = sb("mixT", [128, 8, 1024], BF16)
    K.xT, K.mixT = xT, mixT
    xkeys = [('xT', c) for c in range(0, 1024, 128)]
    K.xkeys = xkeys
    with Scope(K):
        hT = sb("hT", [128, 8, 1026], BF16)
        K.hT = hT
        hscope = Scope(K)
        hscope.__enter__()
        hTh = sb("hTh", [128, 8, 448], BF16)
        K.hTh = hTh
        K.vctx = sb("vctx", [128, 4, 512], BF16)
        K.MB = sb("MB", [128, 2, 8, 8, 64], BF16)
        K.RB = sb("RB", [128, 64])
        K.cktm = [sb("cktm%d" % i, [128, 512]) for i in range(4)]
        K.wslabA = [sb("wslabA%d" % i, [128, 8, 768], BF16) for i in range(2)]
        with Scope(K):
            xTh = sb("xTh", [128, 8, 448])
            xtm = [sb("xtm%d" % i, [128, 1024]) for i in range(3)]
            emit_mod_load(K)
            blocks = []
            for b in range(4):
                blocks.append((I.xp[b * 128:(b + 1) * 128, :], 128, xT, b * 128, 'xT'))
            for b in range(4):
                blocks.append((I.xs[b * 128:(b + 1) * 128, :], 128, xT, 512 + b * 128, 'xT'))
            for b in range(4):
                n = 128 if b < 3 else 64
                blocks.append((I.xh[b * 128:b * 128 + n, :], n, xTh, b * 128, 'xTh'))
            for bi, (src, n, dst, col, dk) in enumerate(blocks):
                t = xtm[bi % 3]; tk = 'xtm%d' % (bi % 3)
                fw.dma('sp', t[0:n, :], src, writes=[tk])
                for half in range(2):
                    ps, pk = bank(K)
                    for c4 in range(4):
                        c = half * 4 + c4
                        fw.op('pe', lambda tt, t=t, n=n, c=c, c4=c4, ps=ps: tt.transpose(
                            ps[:, c4 * 128:c4 * 128 + n], t[0:n, c * 128:(c + 1) * 128], K.ident_f[0:n, 0:n]),
                            reads=[tk, 'ident_f'], writes=[pk])
                    evac(K, 'act' if (bi + half) % 2 == 0 else 'dve', dst[:, half * 4:half * 4 + 4, col:col + n],
                         ps[:].rearrange("p (c t) -> p c t", c=4)[:, :, 0:n], [pk], [(dk, col)])
            dump(K, "xT", xT[:], [128, 8, 1024], xkeys)

            sq = sb("sq", [128, 8, 512], BF16)
            emit_mod_compute(K)
            fw.dma('pool', K.vctx[:], I.cv.rearrange("(b p) c -> p b c", p=128), writes=['vctx'])
            fw.dma('pool', K.MB[:].rearrange("p a b h j -> p (a b h j)"), I.bt[:, :], writes=['MB'])
            fw.dma('sp', K.RB[:], I.rbp[:, :], writes=['RB'])
            for kb in range(4):
                fw.dma('sp', K.cktm[kb][:, :], I.ck[kb * 128:(kb + 1) * 128, :], writes=['cktm%d' % kb])
            K.wslab = K.wslabA
            K.pre_slab = load_w(K, [(I.w_in[:, 0:256], 256), (I.w_in[:, 512:768], 256), (I.w_in[:, 1024:1280], 256)])
            rstds = [sb("rstdF%d" % i, [128, 512]) for i in range(3)]
            tmpn = sb("tmpn", [128, 2, 512])
            hkeys = [('xTh', c) for c in range(0, 512, 128)]
            nblocks = ((xT, xkeys, 0, 512, hT, 0, 'hT0', 0), (xT, xkeys, 512, 512, hT, 512, 'hT512', 1), (xTh, hkeys, 0, 448, hTh, 0, 'hT1024', 1))
            for bi_, (src, skeys, scol, n, dst, dcol, okey, cond) in enumerate(nblocks):
                rms_stats(K, sq, rstds[bi_], src, skeys, scol, n, 'rstd%d' % bi_)
            for bi_, (src, skeys, scol, n, dst, dcol, okey, cond) in enumerate(nblocks):
                rms_apply(K, rstds[bi_], tmpn, src, skeys, scol, n, dst, dcol, [okey],
                          lambda c, cond=cond: K.AB[:, 0, c, cond:cond + 1],
                          lambda c, cond=cond: K.MOD[:, c, cond:cond + 1], 'rstd%d' % bi_)
            fw.op('dve', lambda e: e.tensor_copy(out=hT[:, :, 1024:1026], in_=hTh[:, :, 255:257]), reads=['hT1024'], writes=['hTx'])
        with Scope(K):
            emit_attention(K, stage)
        hscope.__exit__(None, None, None)
        with Scope(K):
            emit_rwkv(K, stage)
    if stage <= 3:
        return
    emit_back(K, stage)


def rmsnorm_block(K, sq, rstd, tmpn, src, src_keys, scol, n, dst, dcol, out_keys, Asc, Bsc, rkey='rstd'):
    rms_stats(K, sq, rstd, src, src_keys, scol, n, rkey)
    rms_apply(K, rstd, tmpn, src, src_keys, scol, n, dst, dcol, out_keys, Asc, Bsc, rkey)


def rms_stats(K, sq, rstd, src, src_keys, scol, n, rkey='rstd'):
    fw = K.fw
    for c in range(8):
        fw.op('act', lambda e, c=c: e.activation(out=sq[:, c, 0:n], in_=src[:, c, scol:scol + n], func=AF.Square),
              reads=src_keys, writes=[('sq', c)])
    ps, pk = bank(K)
    for c in range(8):
        fw.op('pe', lambda t, c=c, ps=ps: t.matmul(ps[:, 0:n], K.ones_b[:], sq[:, c, 0:n], start=(c == 0), stop=(c == 7)),
              reads=[('sq', c), 'ones_b'], writes=[pk])
    fw.op('act', lambda e, ps=ps: e.activation(out=rstd[:, 0:n], in_=ps[:, 0:n], func=AF.Ln,
                                               bias=K.epsc[:, 0:1], scale=1.0 / D),
          reads=[pk, 'epsc'], writes=[rkey])
    fw.op('act', lambda e: e.activation(out=rstd[:, 0:n], in_=rstd[:, 0:n], func=AF.Exp, scale=-0.5), reads=[rkey], writes=[rkey])


def rms_apply(K, rstd, tmpn, src, src_keys, scol, n, dst, dcol, out_keys, Asc, Bsc, rkey='rstd'):
    fw = K.fw
    for c in range(8):
        fw.op('dve', lambda e, c=c: e.tensor_tensor(out=tmpn[:, c % 2, 0:n], in0=src[:, c, scol:scol + n], in1=rstd[:, 0:n], op=ALU.mult),
              reads=src_keys + [rkey], writes=[('tmpn', c % 2)])
        if Bsc is not None:
            fw.op('act', lambda e, c=c: e.activation(out=dst[:, c, dcol:dcol + n], in_=tmpn[:, c % 2, 0:n], func=AF.Identity,
                                                     bias=Bsc(c), scale=Asc(c)),
                  reads=[('tmpn', c % 2), 'MOD', 'AB', 'PV'], writes=out_keys)
        else:
            fw.op('act', lambda e, c=c: e.activation(out=dst[:, c, dcol:dcol + n], in_=tmpn[:, c % 2, 0:n], func=AF.Identity,
                                                     scale=Asc(c)),
                  reads=[('tmpn', c % 2), 'MOD', 'AB', 'PV'], writes=out_keys)


def emit_attention(K, stage=99):
    nc, fw, I, O, sb = K.nc, K.fw, K.I, K.O, K.sb
    hT, mixT = K.hT, K.mixT
    hsel = lambda k, col0, n: hT[:, k, col0:col0 + n] if col0 < 1024 else K.hTh[:, k, col0 - 1024:col0 - 1024 + n]
    kctx = sb("kctx", [128, 4, 512], BF16)
    cktm = K.cktm
    vctx, MB, RB = K.vctx, K.MB, K.RB
    K.wslab = K.wslabA
    for kb in range(4):
        t = cktm[kb]; tk = 'cktm%d' % kb
        ps, pk = bank(K)
        for p in range(4):
            fw.op('pe', lambda tt, t=t, p=p, ps=ps: tt.transpose(ps[:, p * 128:(p + 1) * 128], t[:, p * 128:(p + 1) * 128], K.ident_f[:]),
                  reads=[tk, 'ident_f'], writes=[pk])
        evac(K, 'act' if kb % 2 == 0 else 'dve', kctx[:, :, kb * 128:(kb + 1) * 128],
             ps[:].rearrange("p (c t) -> p c t", c=4), [pk], ['kctx'])
    HK = ['hT0', 'hT512', 'hT1024']
    QP = sb("QP", [128, 2, 2, 2, 256], BF16)
    QS = sb("QS", [128, 2, 8, 2, 64], BF16)
    kP = sb("kP", [128, 2, 512], BF16)
    kS = sb("kS", [128, 2, 1536], BF16)
    vP = sb("vP", [128, 4, 256], BF16)
    vS = sb("vS", [128, 12, 256], BF16)
    E = [sb("E%d" % i, [128, 12, 256], BF16) for i in range(2)]
    EP = sb("EP", [128, 2, 2, 512], BF16)
    rec = sb("rec", [128, 512])
    recS = [sb("recS%d" % i, [128, 256]) for i in range(2)]
    stkv = [sb("stkv%d" % i, [128, 512]) for i in range(2)]
    fw.op('act', lambda e: e.activation(out=MB[:].rearrange("p a b h j -> p (a b h j)"),
                                        in_=MB[:].rearrange("p a b h j -> p (a b h j)"), func=AF.Exp),
          reads=['MB'], writes=['MB'])
    fw.op('pool', lambda e: e.memset(QP[:].rearrange("p a b c d -> p (a b c d)"), 0.0), writes=['QP'])
    fw.op('pool', lambda e: e.memset(QS[:].rearrange("p a b c d -> p (a b c d)"), 0.0), writes=['QS'])
    fw.op('pool', lambda e: e.memset(kS[:].rearrange("p a b -> p (a b)"), 0.0), writes=['kS'])
    fw.op('pool', lambda e: e.memset(vS[:].rearrange("p a b -> p (a b)"), 0.0), writes=['vS'])
    ecount = 0
    if stage < 1.15:
        return
    for hh in range(2):
        if hh == 0:
            slab, sk = K.pre_slab
        else:
            slab, sk = load_w(K, [(I.w_in[:, hh * 256:hh * 256 + 256], 256),
                                  (I.w_in[:, 512 + hh * 256:512 + hh * 256 + 256], 256),
                                  (I.w_in[:, 1024 + hh * 256:1024 + hh * 256 + 256], 256)])
        for pi in range(2):
            for (col0, isS) in ((0, False), (512, True)):
                ps, pk = bank(K)
                for k in range(8):
                    fw.op('pe', lambda t, k=k, ps=ps, pi=pi, col0=col0: t.matmul(
                        ps[:, 0:512], slab[:, k, pi * 128:(pi + 1) * 128], hT[:, k, col0:col0 + 512],
                        start=(k == 0), stop=(k == 7)), reads=[sk] + HK, writes=[pk])
                for h2 in range(2):
                    pr = slice(h2 * 64, h2 * 64 + 64)
                    if not isS:
                        evac(K, 'act' if h2 == 0 else 'dve', QP[pr, pi, :, h2, :],
                             ps[pr, :].rearrange("p (s t) -> p s t", s=2), [pk], ['QP'])
                    else:
                        evac(K, 'act' if h2 == 0 else 'dve', QS[pr, pi, :, h2, :],
                             ps[pr, :].rearrange("p (l t) -> p l t", l=8), [pk], ['QS'])
        for pi in range(2):
            for (col0, n, kind) in ((0, 512, 'P'), (512, 512, 'S'), (1024, 448, 'H')):
                ps, pk = bank(K)
                for k in range(8):
                    fw.op('pe', lambda t, k=k, ps=ps, pi=pi, col0=col0, n=n: t.matmul(
                        ps[:, 0:n], slab[:, k, 256 + pi * 128:256 + (pi + 1) * 128], hsel(k, col0, n),
                        start=(k == 0), stop=(k == 7)), reads=[sk] + HK, writes=[pk])
                if kind == 'P':
                    evac(K, 'act', kP[:, pi, :], ps[:, 0:512], [pk], ['kP'])
                elif kind == 'S':
                    evac(K, 'dve', kS[:, pi, 512:1024], ps[:, 0:512], [pk], ['kS'])
                else:
                    evac(K, 'act', kS[:, pi, 256:512], ps[:, 0:256], [pk], ['kS'])
                    evac(K, 'dve', kS[:, pi, 1024:1216], ps[:, 256:448], [pk], ['kS'])
        for tb in range(4):
            ps, pk = bank(K)
            for k in range(8):
                fw.op('pe', lambda t, k=k, ps=ps, tb=tb: t.matmul(
                    ps[:, 0:512], hT[:, k, tb * 128:(tb + 1) * 128], slab[:, k, 256:768],
                    start=(k == 0), stop=(k == 7)), reads=[sk] + HK, writes=[pk])
            st = stkv[tb % 2]; stk = 'stkv%d' % (tb % 2)
            evac(K, 'act', st[:], ps[:, 0:512], [pk], [stk])
            evac(K, 'dve', vP[:, tb, :], ps[:, 256:512], [pk], ['vP'])
            fw.dma('sp', O.nk[tb * 128:(tb + 1) * 128, hh * 256:(hh + 1) * 256], st[:, 0:256], reads=[stk])
            fw.dma('sp', O.nv[tb * 128:(tb + 1) * 128, hh * 256:(hh + 1) * 256], st[:, 256:512], reads=[stk])
        vsrc = [(4 + b, 512 + b * 128, 128) for b in range(4)] + [(2, 1024, 128), (3, 1152, 128), (8, 1280, 128), (9, 1408, 64)]
        for gi in range(0, 8, 2):
            ps, pk = bank(K)
            for gj in range(2):
                blk, col0, n = vsrc[gi + gj]
                for k in range(8):
                    fw.op('pe', lambda t, k=k, ps=ps, gj=gj, col0=col0, n=n: t.matmul(
                        ps[0:n, gj * 256:(gj + 1) * 256], hsel(k, col0, n), slab[:, k, 512:768],
                        start=(k == 0), stop=(k == 7)), reads=[sk] + HK, writes=[pk])
            for gj in range(2):
                blk, col0, n = vsrc[gi + gj]
                evac(K, 'act' if gj == 0 else 'dve', vS[0:n, blk, :], ps[0:n, gj * 256:(gj + 1) * 256], [pk], ['vS'])
        if stage < 1.25:
            continue
        for seq in range(2):
            for pi in range(2):
                p = 2 * hh + pi
                for kb in range(2):
                    ps, pk = bank(K)
                    fw.op('pe', lambda t, ps=ps, pi=pi, seq=seq, kb=kb: t.matmul(
                        ps[:, 0:512], kP[:, pi, seq * 256 + kb * 128:seq * 256 + (kb + 1) * 128],
                        QP[:, pi, seq, :, :].rearrange("p a q -> p (a q)"), start=True, stop=True),
                        reads=['kP', 'QP'], writes=[pk])
                    fw.op('act', lambda e, ps=ps, pi=pi, kb=kb: e.activation(out=EP[:, pi, kb, :], in_=ps[:, 0:512], func=AF.Exp, scale=0.125),
                          reads=[pk], writes=[('EP', pi, kb)])
                psd, pdk = bank(K)
                for kb in range(2):
                    fw.op('pe', lambda t, psd=psd, pi=pi, kb=kb: t.matmul(psd[:, 0:512], K.ones_b[:], EP[:, pi, kb, :], start=(kb == 0), stop=(kb == 1)),
                          reads=[('EP', pi, kb), 'ones_b'], writes=[pdk])
                psn, pnk = bank(K)
                for kb in range(2):
                    fw.op('pe', lambda t, psn=psn, pi=pi, kb=kb, seq=seq: t.matmul(
                        psn[:, 0:512], vP[:, seq * 2 + kb, pi * 128:(pi + 1) * 128], EP[:, pi, kb, :], start=(kb == 0), stop=(kb == 1)),
                        reads=[('EP', pi, kb), 'vP'], writes=[pnk])
                fw.op('act', lambda e, psd=psd: e.activation(out=rec[:, 0:512], in_=psd[:, 0:512], func=AF.Ln), reads=[pdk], writes=['rec'])
                fw.op('act', lambda e: e.activation(out=rec[:, 0:512], in_=rec[:, 0:512], func=AF.Exp, scale=-1.0), reads=['rec'], writes=['rec'])
                for h2 in range(2):
                    pr = slice(h2 * 64, h2 * 64 + 64)
                    fw.op('dve', lambda e, psn=psn, pr=pr, h2=h2, p=p, seq=seq: e.tensor_tensor(
                        out=mixT[pr, p, seq * 256:(seq + 1) * 256], in0=psn[pr, h2 * 256:(h2 + 1) * 256],
                        in1=rec[pr, h2 * 256:(h2 + 1) * 256], op=ALU.mult),
                        reads=[pnk, 'rec'], writes=[('mixT', p)])
        if stage < 1.35:
            continue
        for l in range(8):
            Et = E[ecount % 2]; ek = 'E%d' % (ecount % 2); ecount += 1
            par = l % 2
            b0 = (l + 1) // 2
            for g in range(6):
                ps, pk = bank(K)
                for gj in range(2):
                    blk = 2 * g + gj
                    for pi in range(2):
                        if blk < 8:
                            lhs = kS[:, pi, (b0 + blk) * 128:(b0 + blk + 1) * 128]
                            rk = 'kS'
                        else:
                            lhs = kctx[:, 2 * hh + pi, (blk - 8) * 128:(blk - 7) * 128]
                            rk = 'kctx'
                        fw.op('pe', lambda t, ps=ps, lhs=lhs, gj=gj, pi=pi, l=l: t.matmul(
                            ps[:, gj * 256 + pi * 128:gj * 256 + (pi + 1) * 128], lhs,
                            QS[:, pi, l, :, :].rearrange("p a q -> p (a q)"), start=True, stop=True),
                            reads=[rk, 'QS'], writes=[pk])
                for gj in range(2):
                    blk = 2 * g + gj
                    if blk < 8:
                        fw.op('act', lambda e, ps=ps, gj=gj, blk=blk, l=l, Et=Et: e.activation(
                            out=Et[:, blk, :], in_=ps[:, gj * 256:(gj + 1) * 256], func=AF.Exp, scale=0.125,
                            bias=RB[:, l * 8 + blk:l * 8 + blk + 1]), reads=[pk, 'RB'], writes=[(ek, blk)])
                    else:
                        fw.op('act', lambda e, ps=ps, gj=gj, blk=blk, Et=Et: e.activation(
                            out=Et[:, blk, :], in_=ps[:, gj * 256:(gj + 1) * 256], func=AF.Exp, scale=0.125),
                            reads=[pk], writes=[(ek, blk)])
            ekeys = [(ek, b) for b in range(12)]
            fw.op('dve', lambda e, Et=Et, par=par, hh=hh: e.tensor_tensor(
                out=Et[:, 0:8, :], in0=Et[:, 0:8, :],
                in1=MB[:, par, :, hh * 4:hh * 4 + 4, :].rearrange("p b h j -> p b (h j)"), op=ALU.mult),
                reads=ekeys[0:8] + ['MB'], writes=ekeys[0:8])
            rq = recS[(ecount - 1) % 2]; rqk = 'recS%d' % ((ecount - 1) % 2)
            psd, pdk = bank(K)
            for blk in range(12):
                fw.op('pe', lambda t, psd=psd, blk=blk, Et=Et: t.matmul(psd[:, 0:256], K.ones_b[:], Et[:, blk, :], start=(blk == 0), stop=(blk == 11)),
                      reads=[(ek, blk), 'ones_b'], writes=[pdk])
            psn, pnk = bank(K)
            for pi in range(2):
                for blk in range(12):
                    if blk < 8:
                        lhs = vS[:, b0 + blk, pi * 128:(pi + 1) * 128]; rk = 'vS'
                    else:
                        lhs = vctx[:, blk - 8, hh * 256 + pi * 128:hh * 256 + (pi + 1) * 128]; rk = 'vctx'
                    fw.op('pe', lambda t, psn=psn, lhs=lhs, blk=blk, pi=pi, Et=Et: t.matmul(
                        psn[:, pi * 128:(pi + 1) * 128], lhs, Et[:, blk, pi * 128:(pi + 1) * 128],
                        start=(blk == 0), stop=(blk == 11)), reads=[(ek, blk), rk], writes=[pnk])
            fw.op('act', lambda e, psd=psd: e.activation(out=rq[:, 0:256], in_=psd[:, 0:256], func=AF.Ln), reads=[pdk], writes=[rqk])
            fw.op('act', lambda e: e.activation(out=rq[:, 0:256], in_=rq[:, 0:256], func=AF.Exp, scale=-1.0), reads=[rqk], writes=[rqk])
            for h2 in range(2):
                pr = slice(h2 * 64, h2 * 64 + 64)
                fw.op('dve', lambda e, psn=psn, pr=pr, h2=h2, hh=hh, l=l: e.tensor_tensor(
                    out=mixT[pr, 2 * hh:2 * hh + 2, 512 + l * 64:512 + (l + 1) * 64],
                    in0=psn[pr, 0:256].rearrange("p (a b) -> p a b", a=2)[:, :, h2 * 64:(h2 + 1) * 64],
                    in1=rq[pr, 0:256].rearrange("p (a b) -> p a b", a=2)[:, :, h2 * 64:(h2 + 1) * 64], op=ALU.mult),
                    reads=[pnk, rqk], writes=[('mixT', 2 * hh), ('mixT', 2 * hh + 1)])
    if 'mixA' in K.dbg:
        md = sb("mixdbg", [128, 4, 1024])
        fw.op('dve', lambda e: e.tensor_copy(out=md[:], in_=mixT[:, 0:4, :]), reads=[('mixT', p) for p in range(4)], writes=['mixdbg'])
        dump(K, "mixA", md[:], [128, 4, 1024], ['mixdbg'])


C0 = 0.6065306597126334


def emit_rwkv(K, stage):
    nc, fw, I, O, sb = K.nc, K.fw, K.I, K.O, K.sb
    hT, mixT, PV = K.hT, K.mixT, K.PV
    HK = ['hT0', 'hT512', 'hTx']
    yT = sb("yT", [128, 4, 1024])
    YP = sb("YP", [128, 4, 2, 4, 128], BF16)
    TXW = sb("TXW", [128, 1024], BF16); UXA = sb("UXA", [128, 1024], BF16); SXG = sb("SXG", [128, 1024], BF16)
    CC = sb("CC", [128, 4, 2, 128])
    S0T = sb("S0T", [128, 4, 2, 64])
    W2b = sb("W2b", [128, 512], BF16); A2b = sb("A2b", [128, 512], BF16); G2b = sb("G2b", [128, 512], BF16)
    mqb = sb("mqb", [128, 2, 512], BF16); mbb = sb("mbb", [128, 2, 128], BF16)
    rst = sb("rst", [128, 512], BF16); xi_f = sb("xi_f", [128, 128]); bd1_f = sb("bd1_f", [128, 128])
    OMK = sb("OMK", [128, 4])
    eps12 = sb("eps12", [128, 1])
    K.wslab = [sb("wslabR0", [128, 8, 384], BF16)] * 2
    fw.dma('pool', W2b[:], I.w2.rearrange("d l c -> (d l) c"), writes=['W2b'])
    fw.dma('pool', A2b[:], I.a2.rearrange("d l c -> (d l) c"), writes=['A2b'])
    fw.dma('pool', G2b[:], I.g2[:, :], writes=['G2b'])
    fw.dma('pool', mqb[:].rearrange("p a b -> p (a b)"), I.mq[:, :], writes=['mqb'])
    fw.dma('pool', mbb[:].rearrange("p a b -> p (a b)"), I.mb[:, :], writes=['mbb'])
    fw.dma('pool', rst[:], I.rst[:, :], writes=['rst'])
    fw.dma('sp', xi_f[:], I.xi[:, :], writes=['xi_f'])
    fw.dma('sp', bd1_f[:], I.bd1[:, :], writes=['bd1_f'])
    fw.op('pool', lambda e: e.memset(eps12[:], 1e-12), writes=['eps12'])
    fw.op('dve', lambda e: e.tensor_scalar(out=OMK[:], in0=PV[:, PV_KA:PV_KA + 4], scalar1=-1.0, scalar2=1.0, op0=ALU.mult, op1=ALU.add),
          reads=['PV'], writes=['OMK'])
    with Scope(K):
        Z = [sb("Z%d" % i, [128, 1030], BF16) for i in range(3)]
        TT6 = sb("TT6", [128, 6, 512])
        T = [TT6[:, i, :] for i in range(6)]
        utmp = TT6[:, 0:2, :].rearrange("p a b -> p (a b)")
        Uset = [[sb("U%s%d" % (n, 0), [128, 1024], BF16) for n in ("r", "kr", "v")]] * 2
        RKK = sb("RKK", [128, 1024], BF16)
        Vaug = sb("Vaug", [128, 4, 2, 128], BF16)
        PSET = [dict(TL2=[sb("TL2_%d_%d" % (d, q), [128, 4, 2, 128], BF16) for d in range(2)],
                     TB=[sb("TB_%d_%d" % (d, q), [128, 512], BF16) for d in range(2)],
                     TK=[sb("TK_%d_%d" % (d, q), [128, 512], BF16) for d in range(2)],
                     TBK=[sb("TBK_%d_%d" % (d, q), [128, 4, 2, 128], BF16) for d in range(2)],
                     WC=[sb("WC_%d_%d" % (d, q), [128, 4]) for d in range(2)],
                     ) for q in range(2)]
        KKt1 = sb("KKt", [128, 512]); SQK1 = sb("SQK", [128, 512], BF16)
        for q in range(2):
            PSET[q]['KKt'] = KKt1; PSET[q]['SQK'] = SQK1
        AQ = [sb("AQ_%d" % d, [128, 8, 3, 128], BF16) for d in range(2)]
        NP = [sb("NP_%d" % d, [128, 8, 2, 128], BF16) for d in range(2)]; BB = [sb("BB_%d" % d, [128, 8, 128], BF16) for d in range(2)]
        NN = [NP[d][:, :, 0, :] for d in range(2)]
        PP = [NP[d][:, :, 1, :] for d in range(2)]
        ST32 = [sb("ST32_%d" % d, [128, 2, 128]) for d in range(2)]; STb = [sb("STb_%d" % d, [128, 2, 128], BF16) for d in range(2)]
        STW = [sb("STW_%d" % d, [128, 2, 128]) for d in range(2)]
        Xb = [sb("Xb_%d" % d, [128, 4, 128], BF16) for d in range(2)]; Ub = [sb("Ub_%d" % d, [128, 4, 128], BF16) for d in range(2)]
        stS = sb("stS", [128, 128])
        for zi in range(3):
            fw.op('pool', lambda e, zi=zi: e.memset(Z[zi][:, 0:516].rearrange("p (s t) -> p s t", s=2)[:, :, 0:258:257], 0.0), writes=['Z%d' % zi])

        def proj_mm(slab, sk, scol, zi):
            Zt = Z[zi]; zk = 'Z%d' % zi
            for (col0, kind) in ((0, 'P'), (512, 'S')):
                ps, pk = bank(K)
                for k in range(8):
                    fw.op('pe', lambda t, k=k, ps=ps, col0=col0: t.matmul(
                        ps[:, 0:512], slab[:, k, scol:scol + 128], hT[:, k, col0:col0 + 512], start=(k == 0), stop=(k == 7)),
                        reads=[sk] + HK, writes=[pk])
                if kind == 'P':
                    evac(K, 'act', Zt[:, 0:516].rearrange("p (s t) -> p s t", s=2)[:, :, 1:257],
                         ps[:, 0:512].rearrange("p (s t) -> p s t", s=2), [pk], [zk])
                else:
                    evac(K, 'act', Zt[:, 517:1029], ps[:, 0:512], [pk], [zk])
            ps, pk = bank(K)
            for k in range(8):
                fw.op('pe', lambda t, k=k, ps=ps: t.matmul(ps[:, 0:2], slab[:, k, scol:scol + 128], hT[:, k, 1024:1026], start=(k == 0), stop=(k == 7)),
                      reads=[sk] + HK, writes=[pk])
            fw.op('dve', lambda e, ps=ps: e.tensor_scalar(out=Zt[:, 516:517], in0=ps[:, 0:1], scalar1=K.cst[:, 12:13], scalar2=None, op0=ALU.mult),
                  reads=[pk, 'cst'], writes=[zk])
            fw.op('dve', lambda e, ps=ps: e.tensor_scalar(out=Zt[:, 1029:1030], in0=ps[:, 1:2], scalar1=K.cst[:, 13:14], scalar2=None, op0=ALU.mult),
                  reads=[pk, 'cst'], writes=[zk])

        def conv(zi, ci, dest, dkey, func=None):
            Zt = Z[zi]; zk = 'Z%d' % zi
            w = [PV[:, PV_WTS + j * 15 + ci:PV_WTS + j * 15 + ci + 1] for j in range(3)]
            zP = lambda off: Zt[:, 0:516].rearrange("p (s t) -> p s t", s=2)[:, :, off:off + 256]
            zS = lambda off: Zt[:, 516 + off:516 + off + 512]
            uP = utmp[:, 0:512].rearrange("p (s t) -> p s t", s=2)
            uS = utmp[:, 512:1024]
            final = dest if func is None else utmp
            dP = final[:, 0:512].rearrange("p (s t) -> p s t", s=2)
            dS = final[:, 512:1024]
            fkeys = [dkey] if func is None else ['T0', 'T1']
            for (zv, uv, dv) in ((zP, uP, dP), (zS, uS, dS)):
                fw.op('act', lambda e, zv=zv, uv=uv: e.activation(out=uv, in_=zv(0), func=AF.Identity, scale=w[0]), reads=[zk, 'PV'], writes=['T0', 'T1'])
                fw.op('dve', lambda e, zv=zv, uv=uv: e.scalar_tensor_tensor(out=uv, in0=zv(1), scalar=w[1], in1=uv, op0=ALU.mult, op1=ALU.add),
                      reads=[zk, 'PV', 'T0', 'T1'], writes=['T0', 'T1'])
                fw.op('dve', lambda e, zv=zv, uv=uv, dv=dv: e.scalar_tensor_tensor(out=dv, in0=zv(2), scalar=w[2], in1=uv, op0=ALU.mult, op1=ALU.add),
                      reads=[zk, 'PV', 'T0', 'T1'], writes=fkeys)
            if func is not None:
                fw.op('act', lambda e: e.activation(out=dest[:], in_=utmp[:], func=func), reads=['T0', 'T1'], writes=[dkey])

        slab, sk = load_w(K, [(I.w_in[:, 3072:3456], 384)])
        for zi in range(3):
            proj_mm(slab, sk, zi * 128, zi)
        conv(0, 12, TXW, 'TXW', AF.Tanh)
        conv(1, 13, UXA, 'UXA')
        conv(2, 14, SXG, 'SXG', AF.Sigmoid)

        def load_pair(p):
            return load_w(K, [(I.w_in[:, 1536 + p * 128:1536 + (p + 1) * 128], 128),
                              (I.w_in[:, 2048 + p * 128:2048 + (p + 1) * 128], 128),
                              (I.w_in[:, 2560 + p * 128:2560 + (p + 1) * 128], 128)])

        def proj_pair(slab, sk):
            for zi in range(3):
                proj_mm(slab, sk, zi * 128, zi)

        def conv_pair(p):
            us = Uset[p % 2]
            for zi, nm in enumerate(("r", "kr", "v")):
                conv(zi, zi * 4 + p, us[zi], 'U%s0' % nm)

        slab, sk = load_pair(0)
        proj_pair(slab, sk)
        conv_pair(0)
        Ur, Ukr, Uv = Uset[0]
        ukeys = {'Ur': 'Ur0', 'Ukr': 'Ukr0', 'Uv': 'Uv0'}
        base = dict(T=T, Ur=Ur, Ukr=Ukr, Uv=Uv, RKK=RKK, TXW=TXW, UXA=UXA, W2b=W2b, A2b=A2b, rst=rst, OMK=OMK, PSET=PSET, AQ=AQ, NN=NN, BB=BB, PP=PP, NP=NP,
                    ST32=ST32, STb=STb, STW=STW, Xb=Xb, Ub=Ub, Vaug=Vaug, YP=YP, yT=yT, CC=CC, mqb=mqb, mbb=mbb, xi_f=xi_f, stS=stS, eps12=eps12, ukeys=ukeys)

        def mkL(it):
            L = dict(base)
            hf = 1 - it % 2
            L.update(p=it // 2, half=hf, par=it % 2, hs=slice(hf * 512, hf * 512 + 512))
            return L

        def bonus(p):
            for hb in range(2):
                ps, pk = bank(K)
                fw.op('pe', lambda t, ps=ps, hb=hb: t.matmul(ps[:, 0:512], K.bd1_b[:], RKK[:, hb * 512:(hb + 1) * 512], start=True, stop=True),
                      reads=['RKK', 'bd1_b'], writes=[pk])
                fw.op('dve', lambda e, ps=ps, hb=hb: e.tensor_tensor(out=mixT[:, 4 + p, hb * 512:(hb + 1) * 512], in0=ps[:, 0:512],
                                                                      in1=Uv[:, hb * 512:(hb + 1) * 512], op=ALU.mult),
                      reads=[pk, 'Uv0'], writes=[('mixT', 4 + p)])

        def vaug(half):
            fw.op('pool', lambda e: e.memset(Vaug[:].rearrange("p a b c -> p (a b c)"), 0.0), writes=['Vaug'])
            ps, pk = bank(K)
            psb = ps[:].bitcast(BF16)
            for c in range(4):
                fw.op('pe', lambda t, c=c, psb=psb: t.transpose(psb[:, c * 128:(c + 1) * 128], Uv[:, half * 512 + c * 128:half * 512 + (c + 1) * 128], K.ident_b[:]),
                      reads=['Uv0', 'ident_b'], writes=[pk])
            for h2 in range(2):
                evac(K, 'act' if h2 == 0 else 'dve', Vaug[:, :, h2, h2 * 64:(h2 + 1) * 64],
                     psb[:, 0:512].rearrange("p (c f) -> p c f", c=4)[:, :, h2 * 64:(h2 + 1) * 64], [pk], ['Vaug'])

        ops0 = rwkv_prep(K, mkL(0))
        fw.replay(ops0, len(ops0))
        nslab = nsk = None
        cc_in = nc.dram_tensor("cc_in", [128, 1024], F32)
        cc_out = nc.dram_tensor("cc_out", [512, 1024], F32)
        for it in range(8):
            p, second = it // 2, it % 2
            L = mkL(it)
            if second == 0 and p < 3:
                nslab, nsk = load_pair(p + 1)
                L['hook_prep'] = lambda nslab=nslab, nsk=nsk: proj_pair(nslab, nsk)

            def hook_mid(L=L, it=it, p=p, second=second):
                if second == 1:
                    bonus(p)
                    if p < 3:
                        conv_pair(p + 1)
                return rwkv_prep(K, mkL(it + 1)) if it < 7 else []
            L['hook_mid'] = hook_mid
            L['vaug'] = vaug
            rwkv_core(K, L)
            if it == 6:
                fw.dma('pool', cc_in.ap(), CC[:].rearrange("p a b c -> p (a b c)"), reads=['CC'], writes=['cc_in'])
                fw.async_op('pool', lambda g: g.collective_compute("AllGather", ALU.bypass, replica_groups=[[0, 1, 2, 3], [4, 5, 6, 7]],
                                                             ins=[cc_in.ap().opt()], outs=[cc_out.ap().opt()]), reads=['cc_in'], writes=['cc_out'])
    if stage < 3:
        if 'yT' in K.dbg:
            dump(K, "yT", yT[:], [128, 4, 1024], [('yT%d' % p, t_) for p in range(4) for t_ in range(0, 1024, 128)])
            dump(K, "CC", CC[:], [128, 4, 2, 128], ['CC'])
        return
    rwkv_finish(K, locals())


def rwkv_prep(K, L):
    fw = K.fw
    PV = K.PV
    p, half, hs = L['p'], L['half'], L['hs']
    uk = L['ukeys']
    T, Ur, Ukr, Uv, RKK = L['T'], L['Ur'], L['Ukr'], L['Uv'], L['RKK']
    TXW, UXA, W2b, A2b, rst, OMK = L['TXW'], L['UXA'], L['W2b'], L['A2b'], L['rst'], L['OMK']
    par = L['par']
    PS = L['PSET'][par]
    TL2s, TBs, TKs, TBKs, WCs = PS['TL2'], PS['TB'], PS['TK'], PS['TBK'], PS['WC']
    AQs, NNs, BBs, PPs = L['AQ'], L['NN'], L['BB'], L['PP']
    ST32s, STbs, STWs, Xbs, Ubs = L['ST32'], L['STb'], L['STW'], L['Xb'], L['Ub']
    Vaug, YP, yT, CC, mqb, mbb, xi_f, stS = (L[k] for k in ('Vaug', 'YP', 'yT', 'CC', 'mqb', 'mbb', 'xi_f', 'stS'))
    O = K.O
    SG, LC, LX, AT, TE, T2 = T[0], T[1], T[2], T[3], T[4], T[5]
    kn = lambda base, d: ('%s_%d_%d' % (base, d, par)) if base in ('TL2', 'TB', 'TK', 'TBK', 'WC') else ('%s_%d' % (base, d))
    KKt, SQK = PS['KKt'], PS['SQK']
    kkk = 'KKt'
    eps12 = L['eps12']
    fw.capture = []

    def prep_kk():
        fw.op('dve', lambda e: e.tensor_scalar(out=KKt[:], in0=Ukr[:, hs], scalar1=PV[:, PV_KK + p:PV_KK + p + 1], scalar2=None, op0=ALU.mult),
              reads=[uk['Ukr'], 'PV'], writes=[kkk])
        fw.op('act', lambda e: e.activation(out=SQK[:], in_=KKt[:], func=AF.Square), reads=[kkk], writes=['SQK'])
        lb = LazyBank(K)
        fw.op('pe', lambda t: t.matmul(lb.ps[:, 0:512], K.bd1_b[:], SQK[:], start=True, stop=True), reads=['SQK', 'bd1_b'], writes=[lb])
        fw.op('act', lambda e: e.activation(out=T[5][:], in_=lb.ps[:, 0:512], func=AF.Ln, bias=eps12[:, 0:1], scale=1.0),
              reads=[lb, 'eps12'], writes=['T5'])
        fw.op('act', lambda e: e.activation(out=T[5][:], in_=T[5][:], func=AF.Exp, scale=-0.5), reads=['T5'], writes=['T5'])
        fw.op('dve', lambda e: e.tensor_tensor(out=KKt[:], in0=KKt[:], in1=T[5][:], op=ALU.mult), reads=[kkk, 'T5'], writes=[kkk])

    def prep_dir(d):
        dr = slice(d * 64, d * 64 + 64)
        TL2, TB, TK, WC = TL2s[d], TBs[d], TKs[d], WCs[d]
        lb = LazyBank(K)
        fw.op('pe', lambda t: t.matmul(lb.ps[:, 0:512], W2b[dr, p * 128:(p + 1) * 128], TXW[dr, hs], start=True, stop=True), reads=['W2b', 'TXW'], writes=[lb])
        fw.op('act', lambda e: e.activation(out=SG[:], in_=lb.ps[:, 0:512], func=AF.Sigmoid, bias=PV[:, PV_W0 + d * 4 + p:PV_W0 + d * 4 + p + 1], scale=1.0),
              reads=[lb, 'PV'], writes=['T0'])
        lb2 = LazyBank(K)
        fw.op('pe', lambda t: t.matmul(lb2.ps[:, 0:512], A2b[dr, p * 128:(p + 1) * 128], UXA[dr, hs], start=True, stop=True), reads=['A2b', 'UXA'], writes=[lb2])
        fw.op('act', lambda e: e.activation(out=AT[:], in_=lb2.ps[:, 0:512], func=AF.Sigmoid, bias=PV[:, PV_A0 + d * 4 + p:PV_A0 + d * 4 + p + 1], scale=1.0),
              reads=[lb2, 'PV'], writes=['T3'])
        fw.op('dve', lambda e: e.tensor_tensor_scan(out=LC[:], data0=rst[:], data1=SG[:], initial=0.0, op0=ALU.mult, op1=ALU.add),
              reads=['rst', 'T0'], writes=['T1'])
        if d == 0:
            fw.op('dve', lambda e: e.tensor_tensor(out=LX[:], in0=LC[:], in1=SG[:], op=ALU.subtract), reads=['T1', 'T0'], writes=['T2'])
            LI, lik = LC, 'T1'
        else:
            for c in range(4):
                cs = slice(c * 128, (c + 1) * 128)
                fw.op('dve', lambda e, cs=cs, c=c: e.tensor_scalar(out=LX[:, cs], in0=LC[:, cs], scalar1=-1.0, scalar2=LC[:, c * 128 + 127:c * 128 + 128],
                                                                   op0=ALU.mult, op1=ALU.add), reads=['T1'], writes=['T2'])
            fw.op('dve', lambda e: e.tensor_tensor(out=SG[:], in0=LX[:], in1=SG[:], op=ALU.add), reads=['T2', 'T0'], writes=['T0'])
            LI, lik = SG, 'T0'
        fw.op('act', lambda e: e.activation(out=TE[:], in_=LI[:], func=AF.Exp, scale=-C0), reads=[lik], writes=['T4'])
        wcol = 127 if d == 0 else 0
        fw.op('dve', lambda e: e.tensor_copy(out=WC[:], in_=TE[:, wcol:512:128]), reads=['T4'], writes=[kn('WC', d)])
        fw.op('dve', lambda e: e.tensor_tensor(out=TL2[:, :, 1, :], in0=Ur[:, hs].rearrange("p (c t) -> p c t", c=4),
                                               in1=TE[:].rearrange("p (c t) -> p c t", c=4), op=ALU.mult), reads=[uk['Ur'], 'T4'], writes=[kn('TL2', d)])
        fw.op('act', lambda e: e.activation(out=TE[:], in_=LX[:], func=AF.Exp, scale=-C0), reads=['T2'], writes=['T4'])
        fw.op('dve', lambda e: e.tensor_tensor(out=TL2[:, :, 0, :], in0=KKt[:].rearrange("p (c t) -> p c t", c=4),
                                               in1=TE[:].rearrange("p (c t) -> p c t", c=4), op=ALU.mult), reads=[kkk, 'T4'], writes=[kn('TL2', d)])
        fw.op('act', lambda e: e.activation(out=TE[:], in_=LI[:], func=AF.Exp, scale=C0), reads=[lik], writes=['T4'])
        fw.op('dve', lambda e: e.tensor_tensor(out=T2[:], in0=KKt[:], in1=AT[:], op=ALU.mult), reads=[kkk, 'T3'], writes=['T5'])
        fw.op('dve', lambda e: e.tensor_tensor(out=TB[:], in0=T2[:], in1=TE[:], op=ALU.mult), reads=['T5', 'T4'], writes=[kn('TB', d)])
        fw.op('dve', lambda e: e.tensor_scalar(out=AT[:], in0=AT[:], scalar1=PV[:, PV_KA + p:PV_KA + p + 1], scalar2=OMK[:, p:p + 1], op0=ALU.mult, op1=ALU.add),
              reads=['T3', 'PV', 'OMK'], writes=['T3'])
        fw.op('dve', lambda e: e.tensor_tensor(out=T2[:], in0=Ukr[:, hs], in1=AT[:], op=ALU.mult), reads=[uk['Ukr'], 'T3'], writes=['T5'])
        fw.op('dve', lambda e: e.tensor_tensor(out=TK[:], in0=T2[:], in1=TE[:], op=ALU.mult), reads=['T5', 'T4'], writes=[kn('TK', d)])
        if d == 0:
            fw.op('dve', lambda e: e.scalar_tensor_tensor(out=RKK[:, hs], in0=Ur[:, hs], scalar=PV[:, PV_RK + p:PV_RK + p + 1], in1=T2[:], op0=ALU.mult, op1=ALU.mult),
                  reads=[uk['Ur'], 'PV', 'T5'], writes=['RKK'])
        else:
            fw.op('dve', lambda e: e.scalar_tensor_tensor(out=T2[:], in0=Ur[:, hs], scalar=PV[:, PV_RK + p:PV_RK + p + 1], in1=T2[:], op0=ALU.mult, op1=ALU.mult),
                  reads=[uk['Ur'], 'PV', 'T5'], writes=['T5'])
            fw.op('dve', lambda e: e.tensor_tensor(out=RKK[:, hs], in0=RKK[:, hs], in1=T2[:], op=ALU.add), reads=['RKK', 'T5'], writes=['RKK'])

    prep_kk()
    for d in range(2):
        prep_dir(d)
    ops = fw.capture
    fw.capture = None
    return ops


def rwkv_prepB(K, L):
    fw = K.fw
    PV = K.PV
    p, half, hs = L['p'], L['half'], L['hs']
    uk = L['ukeys']
    T, Ur, Ukr, Uv, RKK = L['T'], L['Ur'], L['Ukr'], L['Uv'], L['RKK']
    TXW, UXA, W2b, A2b, rst, OMK = L['TXW'], L['UXA'], L['W2b'], L['A2b'], L['rst'], L['OMK']
    par = L['par']
    PS = L['PSET'][par]
    TL2s, TBs, TKs, TBKs, WCs = PS['TL2'], PS['TB'], PS['TK'], PS['TBK'], PS['WC']
    AQs, NNs, BBs, PPs = L['AQ'], L['NN'], L['BB'], L['PP']
    ST32s, STbs, STWs, Xbs, Ubs = L['ST32'], L['STb'], L['STW'], L['Xb'], L['Ub']
    Vaug, YP, yT, CC, mqb, mbb, xi_f, stS = (L[k] for k in ('Vaug', 'YP', 'yT', 'CC', 'mqb', 'mbb', 'xi_f', 'stS'))
    O = K.O
    SG, LC, LX, AT, TE, T2 = T[0], T[1], T[2], T[3], T[4], T[5]
    kn = lambda base, d: ('%s_%d_%d' % (base, d, par)) if base in ('TL2', 'TB', 'TK', 'TBK', 'WC') else ('%s_%d' % (base, d))
    for d in range(2):
        TB, TK = TBs[d], TKs[d]
        ps, pk = bank(K)
        psb = ps[:].bitcast(BF16)
        for c in range(4):
            for a, (src, skey) in enumerate(((TB, kn('TB', d)), (TK, kn('TK', d)))):
                fw.op('pe', lambda t, c=c, a=a, src=src: t.transpose(psb[:, (c * 2 + a) * 128:(c * 2 + a + 1) * 128], src[:, c * 128:(c + 1) * 128], K.ident_b[:]),
                      reads=[skey, 'ident_b'], writes=[pk])
        evac(K, 'act', TBKs[d][:].rearrange("p c a t -> p (c a t)"), psb[:, 0:1024], [pk], [kn('TBK', d)])


def rwkv_core(K, L):
    fw = K.fw
    PV = K.PV
    p, half, hs = L['p'], L['half'], L['hs']
    uk = L['ukeys']
    T, Ur, Ukr, Uv, RKK = L['T'], L['Ur'], L['Ukr'], L['Uv'], L['RKK']
    TXW, UXA, W2b, A2b, rst, OMK = L['TXW'], L['UXA'], L['W2b'], L['A2b'], L['rst'], L['OMK']
    par = L['par']
    PS = L['PSET'][par]
    TL2s, TBs, TKs, TBKs, WCs = PS['TL2'], PS['TB'], PS['TK'], PS['TBK'], PS['WC']
    AQs, NNs, BBs, PPs = L['AQ'], L['NN'], L['BB'], L['PP']
    ST32s, STbs, STWs, Xbs, Ubs = L['ST32'], L['STb'], L['STW'], L['Xb'], L['Ub']
    Vaug, YP, yT, CC, mqb, mbb, xi_f, stS = (L[k] for k in ('Vaug', 'YP', 'yT', 'CC', 'mqb', 'mbb', 'xi_f', 'stS'))
    O = K.O
    SG, LC, LX, AT, TE, T2 = T[0], T[1], T[2], T[3], T[4], T[5]
    kn = lambda base, d: ('%s_%d_%d' % (base, d, par)) if base in ('TL2', 'TB', 'TK', 'TBK', 'WC') else ('%s_%d' % (base, d))
    rwkv_prepB(K, L)
    L['vaug'](half)
    if L.get('hook_prep'):
        L['hook_prep']()
    for d in range(2):
        TL2, TB, TK, AQ, NN, BB = TL2s[d], TBs[d], TKs[d], AQs[d], NNs[d], BBs[d]
        tl = TL2[:].rearrange("p c a t -> p c (a t)")
        for c in range(4):
            for h2 in range(2):
                u = c * 2 + h2
                pr = slice(h2 * 64, h2 * 64 + 64)
                ps, pk = bank(K)
                fw.op('pe', lambda t, ps=ps, c=c, pr=pr: t.matmul(ps[:, 0:256], TB[pr, c * 128:(c + 1) * 128], tl[pr, c, :], start=True, stop=True),
                      reads=[kn('TB', d), kn('TL2', d)], writes=[pk])
                fw.op('pe', lambda t, ps=ps, c=c, pr=pr: t.matmul(ps[:, 256:512], TK[pr, c * 128:(c + 1) * 128], tl[pr, c, :], start=True, stop=True),
                      reads=[kn('TK', d), kn('TL2', d)], writes=[pk])
                fw.op('dve', lambda e, ps=ps, u=u: e.tensor_tensor(out=NN[:, u, :], in0=ps[:, 0:128], in1=mqb[:, d, 0:128], op=ALU.mult),
                      reads=[pk, 'mqb'], writes=[(kn('NN', d), u // 4)])
                fw.op('act', lambda e, ps=ps, u=u: e.activation(out=AQ[:, u, :, :].rearrange("p a t -> p (a t)"), in_=ps[:, 128:512], func=AF.Copy),
                      reads=[pk], writes=[(kn('AQ', d), u)])
                fw.op('pool', lambda e, u=u: e.tensor_tensor(out=AQ[:, u, :, :].rearrange("p a t -> p (a t)"), in0=AQ[:, u, :, :].rearrange("p a t -> p (a t)"),
                                                             in1=mqb[:, d, 128:512], op=ALU.mult),
                      reads=[(kn('AQ', d), u), 'mqb'], writes=[(kn('AQ', d), u)])
        for h2 in range(2):
            pr = slice(h2 * 64, h2 * 64 + 64)
            ps, pk = bank(K)
            for c in range(4):
                fw.op('pe', lambda t, ps=ps, c=c, pr=pr: t.matmul(ps[:, c * 128:(c + 1) * 128], TL2[pr, c, 0, :], TB[pr, c * 128:(c + 1) * 128], start=True, stop=True),
                      reads=[kn('TB', d), kn('TL2', d)], writes=[pk])
            fw.op('dve', lambda e, ps=ps, h2=h2: e.tensor_tensor(out=BB[:, h2:8:2, :], in0=ps[:, 0:512].rearrange("p (c t) -> p c t", c=4),
                                                                in1=mbb[:, d:d + 1, :].broadcast_to([128, 4, 128]), op=ALU.mult),
                  reads=[pk, 'mbb'], writes=[(kn('BB', d), 0), (kn('BB', d), 1)])
    if L.get('hook_quad'):
        L['hook_quad']()
    for d in range(2):
        for g in range(2):
            gs = slice(g * 4, g * 4 + 4)
            fw.op('dve', lambda e, gs=gs, d=d: e.tensor_tensor(out=PPs[d][:, gs, :], in0=NNs[d][:, gs, :], in1=K.ident_b[:].unsqueeze(1).broadcast_to([128, 4, 128]), op=ALU.add),
                  reads=[(kn('NN', d), g), 'ident_b'], writes=[(kn('PP', d), g)])
    NPs = L['NP']
    for lvl in range(1, 8):
        for d in range(2):
            NN, BB, PP, NPt = NNs[d], BBs[d], PPs[d], NPs[d]
            for g in range(2):
                nk_, bk_, pk_ = (kn('NN', d), g), (kn('BB', d), g), (kn('PP', d), g)
                gs = slice(g * 4, g * 4 + 4)
                if lvl == 1:
                    psn, pnk = bank(K)
                    for j in range(4):
                        u = g * 4 + j
                        fw.op('pe', lambda t, psn=psn, j=j, u=u: t.matmul(psn[:, j * 128:(j + 1) * 128], BB[:, u, :], NN[:, u, :], start=True, stop=True),
                              reads=[bk_, nk_], writes=[pnk])
                    psb_, pbk = bank(K)
                    for j in range(4):
                        u = g * 4 + j
                        fw.op('pe', lambda t, psb_=psb_, j=j, u=u: t.matmul(psb_[:, j * 128:(j + 1) * 128], NN[:, u, :], BB[:, u, :], start=True, stop=True),
                              reads=[bk_, nk_], writes=[pbk])
                    evac(K, 'act', NN[:, gs, :], psn[:, 0:512].rearrange("p (u t) -> p u t", u=4), [pnk], [nk_])
                    evac(K, 'act', BB[:, gs, :].rearrange("p u t -> p (u t)"), psb_[:, 0:512], [pbk], [bk_])
                elif lvl <= 5:
                    pm = [bank(K), bank(K)]
                    for j in range(4):
                        u = g * 4 + j
                        psm, pmk = pm[j // 2]
                        jj = j % 2
                        fw.op('pe', lambda t, psm=psm, jj=jj, u=u: t.matmul(psm[:, jj * 256:jj * 256 + 256], BB[:, u, :], NPt[:, u, :, :].rearrange("p a t -> p (a t)"),
                                                                            start=True, stop=True), reads=[bk_, nk_, pk_], writes=[pmk])
                    psb_, pbk = bank(K)
                    for j in range(4):
                        u = g * 4 + j
                        fw.op('pe', lambda t, psb_=psb_, j=j, u=u: t.matmul(psb_[:, j * 128:(j + 1) * 128], NN[:, u, :], BB[:, u, :], start=True, stop=True),
                              reads=[bk_, nk_], writes=[pbk])
                    for hb_ in range(2):
                        psm, pmk = pm[hb_]
                        u0 = g * 4 + hb_ * 2
                        v3 = psm[:, 0:512].rearrange("p (j c) -> p j c", j=2)
                        evac(K, 'act', NN[:, u0:u0 + 2, :], v3[:, :, 0:128], [pmk], [nk_])
                        fw.op('dve', lambda e, v3=v3, u0=u0: e.tensor_tensor(out=PP[:, u0:u0 + 2, :], in0=v3[:, :, 128:256], in1=PP[:, u0:u0 + 2, :], op=ALU.add),
                              reads=[pmk, pk_], writes=[pk_])
                    evac(K, 'dve' if (g + d + lvl) % 2 == 0 else 'act', BB[:, gs, :].rearrange("p u t -> p (u t)"), psb_[:, 0:512], [pbk], [bk_])
                else:
                    psp, ppk = bank(K)
                    for j in range(4):
                        u = g * 4 + j
                        fw.op('pe', lambda t, psp=psp, j=j, u=u: t.matmul(psp[:, j * 128:(j + 1) * 128], BB[:, u, :], PP[:, u, :], start=True, stop=True),
                              reads=[bk_, pk_], writes=[ppk])
                    if lvl == 6:
                        psb_, pbk = bank(K)
                        for j in range(4):
                            u = g * 4 + j
                            fw.op('pe', lambda t, psb_=psb_, j=j, u=u: t.matmul(psb_[:, j * 128:(j + 1) * 128], NN[:, u, :], BB[:, u, :], start=True, stop=True),
                                  reads=[bk_, nk_], writes=[pbk])
                    fw.op('dve', lambda e, psp=psp: e.tensor_tensor(out=PP[:, gs, :], in0=psp[:, 0:512].rearrange("p (u t) -> p u t", u=4),
                                                                   in1=PP[:, gs, :], op=ALU.add), reads=[ppk, pk_], writes=[pk_])
                    if lvl == 6:
                        evac(K, 'act', BB[:, gs, :].rearrange("p u t -> p (u t)"), psb_[:, 0:512], [pbk], [bk_])
    filler = L['hook_mid']()
    if getattr(K, 'no_interleave', False):
        fw.replay(filler, len(filler))
    if half == 0:
        base = [[0, 1], [2, 3]]
    else:
        base = [[0, 1, 2, 3]]
    seqs_d = [base, [list(reversed(x)) for x in base]]
    ns = len(base)
    nsteps = len(base[0])
    nfill = -(-len(filler) // (4 * nsteps))
    for d in range(2):
        for si in range(ns):
            fw.op('dve', lambda e, si=si, d=d: e.tensor_copy(out=ST32s[d][:, si, :], in_=xi_f[:]), reads=['xi_f'], writes=[kn('ST32', d)])
            fw.op('act', lambda e, si=si, d=d: e.activation(out=STbs[d][:, si, :], in_=xi_f[:], func=AF.Copy), reads=['xi_f'], writes=[kn('STb', d)])
    for step in range(nsteps):
        XB = {}
        for d in range(2):
            TL2, AQ, STb = TL2s[d], AQs[d], STbs[d]
            AQk = [(kn('AQ', d), u) for u in range(8)]
            XB[d] = [bank(K), bank(K)]
            for si in range(ns):
                c = seqs_d[d][si][step]
                for h2 in range(2):
                    pr = slice(h2 * 64, h2 * 64 + 64)
                    u = c * 2 + h2
                    psx, pxk = XB[d][h2]
                    fw.op('pe', lambda t, psx=psx, si=si, c=c, pr=pr: t.matmul(psx[:, si * 128:(si + 1) * 128], TL2[pr, c, 0, :], STb[pr, si, :], start=True, stop=False),
                          reads=[kn('TL2', d), kn('STb', d)], writes=[pxk])
                    fw.op('pe', lambda t, psx=psx, si=si, c=c, u=u, h2=h2: t.matmul(psx[:, si * 128:(si + 1) * 128], AQ[:, u, 1, :], Vaug[:, c, h2, :], start=False, stop=True),
                          reads=AQk + ['Vaug'], writes=[pxk])
        for d in range(2):
            for h2 in range(2):
                psx, pxk = XB[d][h2]
                fw.op('act' if d == 0 else 'dve',
                      (lambda e, psx=psx, h2=h2, d=d: e.activation(out=Xbs[d][:, h2 * ns:(h2 + 1) * ns, :].rearrange("p s t -> p (s t)"), in_=psx[:, 0:ns * 128],
                                                                   func=AF.Identity, scale=-1.0)) if d == 0 else
                      (lambda e, psx=psx, h2=h2, d=d: e.tensor_scalar(out=Xbs[d][:, h2 * ns:(h2 + 1) * ns, :].rearrange("p s t -> p (s t)"), in0=psx[:, 0:ns * 128],
                                                                      scalar1=-1.0, scalar2=None, op0=ALU.mult)),
                      reads=[pxk], writes=[kn('Xb', d)])
        fw.replay(filler, nfill)
        UB = {}
        for d in range(2):
            PP, Xb = PPs[d], Xbs[d]
            PPk = [(kn('PP', d), 0), (kn('PP', d), 1)]
            UB[d] = bank(K)
            psu, puk = UB[d]
            for si in range(ns):
                c = seqs_d[d][si][step]
                for h2 in range(2):
                    u = c * 2 + h2
                    q = h2 * ns + si
                    fw.op('pe', lambda t, q=q, u=u, psu=psu: t.matmul(psu[:, q * 128:(q + 1) * 128], PP[:, u, :], Xb[:, q, :], start=True, stop=True),
                          reads=PPk + [kn('Xb', d)], writes=[puk])
        for d in range(2):
            psu, puk = UB[d]
            evac(K, 'dve' if d == 0 else 'act', Ubs[d][:, 0:2 * ns, :].rearrange("p s t -> p (s t)"), psu[:, 0:2 * ns * 128], [puk], [kn('Ub', d)])
        fw.replay(filler, nfill)
        YB = {}; MB_ = {}
        for d in range(2):
            TL2, AQ, STb, Ub, TBK = TL2s[d], AQs[d], STbs[d], Ubs[d], TBKs[d]
            AQk = [(kn('AQ', d), u) for u in range(8)]
            YB[d] = [bank(K), bank(K)]
            for si in range(ns):
                c = seqs_d[d][si][step]
                for h2 in range(2):
                    pr = slice(h2 * 64, h2 * 64 + 64)
                    u = c * 2 + h2
                    q = h2 * ns + si
                    psy, pyk = YB[d][h2]
                    fw.op('pe', lambda t, psy=psy, si=si, c=c, pr=pr: t.matmul(psy[:, si * 128:(si + 1) * 128], STb[pr, si, :], TL2[pr, c, 1, :], start=True, stop=False),
                          reads=[kn('TL2', d), kn('STb', d)], writes=[pyk])
                    fw.op('pe', lambda t, psy=psy, si=si, q=q, u=u: t.matmul(psy[:, si * 128:(si + 1) * 128], Ub[:, q, :], AQ[:, u, 0, :], start=False, stop=False),
                          reads=AQk + [kn('Ub', d)], writes=[pyk])
                    fw.op('pe', lambda t, psy=psy, si=si, c=c, u=u, h2=h2: t.matmul(psy[:, si * 128:(si + 1) * 128], Vaug[:, c, h2, :], AQ[:, u, 2, :], start=False, stop=True),
                          reads=AQk + ['Vaug'], writes=[pyk])
            for si in range(ns):
                c = seqs_d[d][si][step]
                fw.op('dve', lambda e, si=si, c=c, d=d: e.tensor_scalar(out=STWs[d][:, si, :], in0=ST32s[d][:, si, :], scalar1=WCs[d][:, c:c + 1], scalar2=None, op0=ALU.mult),
                      reads=[kn('ST32', d), kn('WC', d)], writes=[kn('STW', d)])
            MB_[d] = bank(K)
            psm, pmk = MB_[d]
            for si in range(ns):
                c = seqs_d[d][si][step]
                for h2 in range(2):
                    pr = slice(h2 * 64, h2 * 64 + 64)
                    q = h2 * ns + si
                    fw.op('pe', lambda t, si=si, c=c, q=q, pr=pr, h2=h2, psm=psm: t.matmul(psm[pr, si * 128:(si + 1) * 128], TBK[:, c, 0, h2 * 64:(h2 + 1) * 64], Ub[:, q, :], start=True, stop=False),
                          reads=[kn('TBK', d), kn('Ub', d)], writes=[pmk])
                    fw.op('pe', lambda t, si=si, c=c, pr=pr, h2=h2, psm=psm: t.matmul(psm[pr, si * 128:(si + 1) * 128], TBK[:, c, 1, h2 * 64:(h2 + 1) * 64], Vaug[:, c, h2, :], start=False, stop=True),
                          reads=[kn('TBK', d), 'Vaug'], writes=[pmk])
        for d in range(2):
            for si in range(ns):
                c = seqs_d[d][si][step]
                tok = half * 512 + c * 128
                for h2 in range(2):
                    pr = slice(h2 * 64, h2 * 64 + 64)
                    po = slice((1 - h2) * 64, (1 - h2) * 64 + 64)
                    psy, pyk = YB[d][h2]
                    li = base[si].index(c)
                    first = (d == 0) == (li < len(base[si]) / 2)
                    if first:
                        fw.op('dve', lambda e, psy=psy, si=si, pr=pr, tok=tok: e.tensor_copy(out=yT[pr, p, tok:tok + 128], in_=psy[pr, si * 128:(si + 1) * 128]),
                              reads=[pyk], writes=[('yT%d' % p, tok)])
                    else:
                        fw.op('dve', lambda e, psy=psy, si=si, pr=pr, tok=tok: e.tensor_tensor(out=yT[pr, p, tok:tok + 128], in0=psy[pr, si * 128:(si + 1) * 128],
                                                                                              in1=yT[pr, p, tok:tok + 128], op=ALU.add),
                              reads=[pyk, ('yT%d' % p, tok)], writes=[('yT%d' % p, tok)])
                    if half == 1:
                        fw.op('act', lambda e, psy=psy, si=si, po=po, c=c, d=d: e.activation(out=YP[po, p, d, c, :], in_=psy[po, si * 128:(si + 1) * 128], func=AF.Copy),
                              reads=[pyk], writes=['YP'])
            psm, pmk = MB_[d]
            for si in range(ns):
                c = seqs_d[d][si][step]
                fw.op('dve', lambda e, si=si, c=c, d=d, psm=psm: e.scalar_tensor_tensor(out=ST32s[d][:, si, :], in0=psm[:, si * 128:(si + 1) * 128], scalar=WCs[d][:, c:c + 1],
                                                                                     in1=STWs[d][:, si, :], op0=ALU.mult, op1=ALU.add),
                      reads=[pmk, kn('WC', d), kn('STW', d)], writes=[kn('ST32', d)])
                fw.op('act', lambda e, si=si, d=d: e.activation(out=STbs[d][:, si, :], in_=ST32s[d][:, si, :], func=AF.Copy), reads=[kn('ST32', d)], writes=[kn('STb', d)])
        fw.replay(filler, 2 * nfill)
    fw.replay(filler, len(filler))
    for d in range(2):
        ST32 = ST32s[d]
        if half == 0:
            for si in range(ns):
                ps, pk = bank(K)
                fw.op('pe', lambda t, ps=ps, si=si: t.transpose(ps[:, 0:128], ST32[:, si, :], K.ident_f[:]), reads=[kn('ST32', d), 'ident_f'], writes=[pk])
                evac(K, 'dve', stS[:], ps[:, 0:128], [pk], ['stS'])
                for h2 in range(2):
                    pr = slice(h2 * 64, h2 * 64 + 64)
                    fw.dma('sp', O.ns[si, d, 2 * p + h2, :, :], stS[pr, h2 * 64:(h2 + 1) * 64], reads=['stS'])
        else:
            fw.op('dve', lambda e: e.tensor_copy(out=CC[:, p, d, :], in_=ST32[:, 0, :]), reads=[kn('ST32', d)], writes=['CC'])


def rwkv_finish(K, L):
    nc, fw, I, sb = K.nc, K.fw, K.I, K.sb
    PV = K.PV
    yT, YP, SXG, CC, S0T, G2b, bd1_f, xi_f = (L[k] for k in ('yT', 'YP', 'SXG', 'CC', 'S0T', 'G2b', 'bd1_f', 'xi_f'))
    mixT = K.mixT
    cc_out = L['cc_out']
    with Scope(K):
        CG = sb("CG", [128, 4, 8, 128])
        s0t = sb("s0t", [64, 8, 128])
        for p in range(4):
            for d in range(2):
                pd = p * 2 + d
                fw.dma('sp', s0t[:, pd, :].rearrange("v (h k) -> v h k", h=2), I.s0[d, 2 * p:2 * p + 2].rearrange("h v k -> v h k"), writes=[('s0t', pd)])
        for p in range(4):
            for d in range(2):
                pd = p * 2 + d
                ps, pk = bank(K)
                fw.op('pe', lambda t, ps=ps, pd=pd: t.transpose(ps[:, 0:64], s0t[:, pd, :], K.ident_f[0:64, 0:64]), reads=[('s0t', pd), 'ident_f'], writes=[pk])
                evac(K, 'dve' if pd % 2 == 0 else 'act', S0T[:, p, d, :], ps[:, 0:64], [pk], ['S0T'])
        SWt = sb("SWt", [128, 8, 128])
        XS = sb("XS", [128, 4, 8, 64])
        MIN = sb("MIN", [128, 8, 64]); MINb = sb("MINb", [128, 8, 64], BF16); MINsw = sb("MINsw", [128, 8, 64], BF16)
        xi_b = sb("xi_b", [128, 128], BF16)
        Wall = [[sb("W%d_%d" % (i, q), [128, 512]) for i in range(3)] for q in range(4)]
        fw.dma('pool', xi_b[:], I.xi[:, :], writes=['xi_b'])
        fw.dma('pool', CG[:].rearrange("p r a c -> p r (a c)"), cc_out.ap().rearrange("(r p) f -> p r f", p=128), reads=['cc_out'], writes=['CG'])
        fw.op('dve', lambda e: e.tensor_copy(out=XS[:, 0, 0:8:2, :], in_=S0T[:, :, 0, :]), reads=['S0T'], writes=['XS'])
        fw.op('dve', lambda e: e.tensor_copy(out=XS[:, 3, 1:8:2, :], in_=S0T[:, :, 1, :]), reads=['S0T'], writes=['XS'])
        for i in range(3):
            rank = (i, 3 - i)
            cur = (i, 3 - i)
            nxt = (i + 1, 2 - i)
            sw = [bank(K), bank(K)]
            for pd in range(8):
                d = pd % 2
                ps, pk = sw[pd // 4]
                for hh in range(2):
                    fw.op('pe', lambda t, ps=ps, hh=hh, pd=pd, d=d: t.matmul(ps[hh * 64:(hh + 1) * 64, (pd % 4) * 128:(pd % 4 + 1) * 128],
                                                                           CG[:, rank[d], pd, (1 - hh) * 64:(2 - hh) * 64], K.ident_f[:], start=True, stop=True),
                          reads=['CG', 'ident_f'], writes=[pk])
            for b_ in range(2):
                ps, pk = sw[b_]
                evac(K, 'act' if b_ == 0 else 'dve', SWt[:, b_ * 4:(b_ + 1) * 4, :].rearrange("p a b -> p (a b)"), ps[:, 0:512], [pk], ['SWt'])
            cb = [bank(K), bank(K)]
            for pd in range(8):
                d = pd % 2
                for h2 in range(2):
                    pr = slice(h2 * 64, h2 * 64 + 64)
                    ps2, pk2 = cb[h2]
                    fw.op('pe', lambda t, ps2=ps2, pr=pr, h2=h2, pd=pd, d=d: t.matmul(ps2[pr, pd * 64:(pd + 1) * 64], SWt[pr, pd, h2 * 64:(h2 + 1) * 64], XS[pr, cur[d], pd, :],
                                                                                   start=True, stop=True), reads=['SWt', 'XS'], writes=[pk2])
            for h2 in range(2):
                pr = slice(h2 * 64, h2 * 64 + 64)
                ps2, pk2 = cb[h2]
                for d in range(2):
                    fw.op('dve', lambda e, ps2=ps2, pr=pr, h2=h2, d=d: e.tensor_tensor(
                        out=XS[pr, nxt[d], d:8:2, :], in0=ps2[pr, 0:512].rearrange("p (a b) -> p a b", a=8)[:, d:8:2, :],
                        in1=CG[pr, rank[d], d:8:2, h2 * 64:(h2 + 1) * 64], op=ALU.add), reads=[pk2, 'CG'], writes=['XS'])
        fw.op('dve', lambda e: e.tensor_scalar(out=MIN[:].rearrange("p a b -> p (a b)"), in0=XS[:, 0, :, :].rearrange("p a b -> p (a b)"), scalar1=K.cst[:, 0:1], scalar2=None, op0=ALU.mult),
              reads=['XS', 'cst'], writes=['MIN'])
        for j in range(1, 4):
            fw.op('dve', lambda e, j=j: e.scalar_tensor_tensor(out=MIN[:].rearrange("p a b -> p (a b)"), in0=XS[:, j, :, :].rearrange("p a b -> p (a b)"), scalar=K.cst[:, j:j + 1],
                                                                in1=MIN[:].rearrange("p a b -> p (a b)"), op0=ALU.mult, op1=ALU.add),
                  reads=['XS', 'cst', 'MIN'], writes=['MIN'])
        dump(K, "MIN", MIN[:], [128, 8, 64], ['MIN'])
        fw.op('act', lambda e: e.activation(out=MINb[:], in_=MIN[:], func=AF.Copy), reads=['MIN'], writes=['MINb'])
        ps, pk = bank(K)
        fw.op('pe', lambda t: t.matmul(ps[:, 0:512], xi_b[:], MINb[:].rearrange("p a b -> p (a b)"), start=True, stop=True), reads=['xi_b', 'MINb'], writes=[pk])
        evac(K, 'dve', MINsw[:].rearrange("p a b -> p (a b)"), ps[:, 0:512], [pk], ['MINsw'])
        for p in range(4):
            for d in range(2):
                pd = p * 2 + d
                for h2 in range(2):
                    pr = slice(h2 * 64, h2 * 64 + 64)
                    po = slice((1 - h2) * 64, (1 - h2) * 64 + 64)
                    psc, pck = bank(K)
                    fw.op('pe', lambda t, psc=psc, pr=pr, po=po: t.matmul(psc[pr, 0:512], MINsw[po, pd, :], YP[po, p, d, :, :].rearrange("p c t -> p (c t)"), start=True, stop=True),
                          reads=['MINsw', 'YP'], writes=[pck])
                    fw.op('dve', lambda e, psc=psc, pr=pr: e.tensor_tensor(out=yT[pr, p, 512:1024], in0=psc[pr, 0:512], in1=yT[pr, p, 512:1024], op=ALU.add),
                          reads=[pck] + [('yT%d' % p, 512 + c_ * 128) for c_ in range(4)], writes=[('yT%d' % p, 512 + c_ * 128) for c_ in range(4)])
        dump(K, "yT", yT[:], [128, 4, 1024], [('yT%d' % p, t_) for p in range(4) for t_ in range(0, 1024, 128)])
        for batch in range(2):
            units = [(batch * 2 + (q // 2), q % 2, q) for q in range(4)]
            ykeys = lambda p, hb: [('yT%d' % p, hb * 512 + c_ * 128) for c_ in range(4)]
            hsl = lambda hb: slice(hb * 512, hb * 512 + 512)
            b1 = {}; b2 = {}; b3 = {}
            for (p, hb, q) in units:
                b1[q] = bank(K)
                fw.op('pe', lambda t, p=p, hb=hb, q=q: t.matmul(b1[q][0][:, 0:512], bd1_f[:], yT[:, p, hsl(hb)], start=True, stop=True),
                      reads=['bd1_f'] + ykeys(p, hb), writes=[b1[q][1]])
            for (p, hb, q) in units:
                W = Wall[q]
                fw.op('dve', lambda e, p=p, hb=hb, q=q, W=W: e.scalar_tensor_tensor(out=W[0][:], in0=b1[q][0][:, 0:512], scalar=-1.0 / 64, in1=yT[:, p, hsl(hb)],
                                                                                   op0=ALU.mult, op1=ALU.add), reads=[b1[q][1]] + ykeys(p, hb), writes=['W0_%d' % q])
            for (p, hb, q) in units:
                W = Wall[q]
                fw.op('act', lambda e, W=W: e.activation(out=W[1][:], in_=W[0][:], func=AF.Square), reads=['W0_%d' % q], writes=['W1_%d' % q])
            for (p, hb, q) in units:
                W = Wall[q]
                b2[q] = bank(K)
                fw.op('pe', lambda t, q=q, W=W: t.matmul(b2[q][0][:, 0:512], bd1_f[:], W[1][:], start=True, stop=True), reads=['bd1_f', 'W1_%d' % q], writes=[b2[q][1]])
            for (p, hb, q) in units:
                b3[q] = bank(K)
                fw.op('pe', lambda t, p=p, hb=hb, q=q: t.matmul(b3[q][0][:, 0:512], G2b[:, p * 128:(p + 1) * 128], SXG[:, hsl(hb)], start=True, stop=True),
                      reads=['G2b', 'SXG'], writes=[b3[q][1]])
            for (p, hb, q) in units:
                W = Wall[q]
                fw.op('act', lambda e, q=q, W=W: e.activation(out=W[1][:], in_=b2[q][0][:, 0:512], func=AF.Ln, bias=K.epsc[:, 1:2], scale=1.0 / 64),
                      reads=[b2[q][1], 'epsc'], writes=['W1_%d' % q])
            for (p, hb, q) in units:
                W = Wall[q]
                fw.op('act', lambda e, W=W: e.activation(out=W[1][:], in_=W[1][:], func=AF.Exp, scale=-0.5), reads=['W1_%d' % q], writes=['W1_%d' % q])
            for (p, hb, q) in units:
                W = Wall[q]
                fw.op('dve', lambda e, W=W: e.tensor_tensor(out=W[0][:], in0=W[0][:], in1=W[1][:], op=ALU.mult), reads=['W0_%d' % q, 'W1_%d' % q], writes=['W0_%d' % q])
            for (p, hb, q) in units:
                W = Wall[q]
                fw.op('act', lambda e, p=p, W=W: e.activation(out=W[2][:], in_=W[0][:], func=AF.Identity, bias=PV[:, PV_LNB + p:PV_LNB + p + 1], scale=PV[:, PV_LNG + p:PV_LNG + p + 1]),
                      reads=['W0_%d' % q, 'PV'], writes=['W2_%d' % q])
            for (p, hb, q) in units:
                W = Wall[q]
                fw.op('dve', lambda e, p=p, hb=hb, W=W: e.tensor_tensor(out=W[2][:], in0=W[2][:], in1=mixT[:, 4 + p, hsl(hb)], op=ALU.add),
                      reads=['W2_%d' % q, ('mixT', 4 + p)], writes=['W2_%d' % q])
            for (p, hb, q) in units:
                W = Wall[q]
                fw.op('dve', lambda e, p=p, hb=hb, q=q, W=W: e.tensor_tensor(out=mixT[:, 4 + p, hsl(hb)], in0=b3[q][0][:, 0:512], in1=W[2][:], op=ALU.mult),
                      reads=[b3[q][1], 'W2_%d' % q], writes=[('mixT', 4 + p)])
        if 'mixR' in K.dbg:
            md = sb("mixdbgR", [128, 4, 1024])
            fw.op('dve', lambda e: e.tensor_copy(out=md[:], in_=mixT[:, 4:8, :]), reads=[('mixT', 4 + p) for p in range(4)], writes=['mixdbgR'])
            dump(K, "mixR", md[:], [128, 4, 1024], ['mixdbgR'])


def emit_back(K, stage):
    nc, fw, I, O, sb = K.nc, K.fw, K.I, K.O, K.sb
    xT, mixT, PV, MOD, AB = K.xT, K.mixT, K.PV, K.MOD, K.AB
    xkeys = K.xkeys
    xk = lambda mo, hb: ('xT', hb * 512)
    XK = [('xT', c) for c in range(0, 1024, 128)]
    XKh = [[('xT', c) for c in range(0, 512, 128)], [('xT', c) for c in range(512, 1024, 128)]]
    hc_in = nc.dram_tensor("hc_in", [128, 16], F32)
    hc_out = nc.dram_tensor("hc_out", [512, 16], F32)
    XH = sb("XH", [128, 8, 2]); HG = sb("HG", [128, 4, 8, 2]); XHh = sb("XHh", [128, 8, 2])

    def emit_halo_select():
        for (dst, src, sel0) in ((0, 1, 4), (1, 0, 8)):
            fw.op('dve', lambda e: e.tensor_scalar(out=XHh[:, :, dst], in0=HG[:, 0, :, src], scalar1=K.cst[:, sel0:sel0 + 1], scalar2=None, op0=ALU.mult),
                  reads=['HG', 'cst'], writes=['XHh'])
            for r in range(1, 4):
                fw.op('dve', lambda e, r=r: e.scalar_tensor_tensor(out=XHh[:, :, dst], in0=HG[:, r, :, src], scalar=K.cst[:, sel0 + r:sel0 + r + 1],
                                                                    in1=XHh[:, :, dst], op0=ALU.mult, op1=ALU.add), reads=['HG', 'cst', 'XHh'], writes=['XHh'])

    with Scope(K):
        h2T = sb("h2T", [128, 8, 1026], BF16)
        sq = sb("sq2", [128, 8, 512], BF16); rstd = sb("rstd2", [128, 512]); rstdS = sb("rstd2S", [128, 512]); tmpn = sb("tmpn2", [128, 2, 512])
        K.wslab = [sb("wslabF%d" % i, [128, 8, 512], BF16) for i in range(2)]
        slabs = [load_w(K, [(I.w_out[:, g * 512:(g + 1) * 512], 512)]) for g in range(2)]
        A2 = lambda cond: (lambda c: AB[:, 1, c, cond:cond + 1])
        B2 = lambda cond: (lambda c: MOD[:, 24 + c, cond:cond + 1])
        for hb in range(2):
            for g in range(2):
                slab, sk = slabs[g]
                for m in range(4):
                    mo = g * 4 + m
                    ps, pk = bank(K)
                    for k in range(8):
                        fw.op('pe', lambda t, ps=ps, k=k, m=m, hb=hb, slab=slab: t.matmul(ps[:, 0:512], slab[:, k, m * 128:(m + 1) * 128], mixT[:, k, hb * 512:(hb + 1) * 512],
                                                                                       start=(k == 0), stop=(k == 7)), reads=[sk] + [('mixT', q) for q in range(8)], writes=[pk])
                    fw.op('dve', lambda e, ps=ps, mo=mo, hb=hb: e.scalar_tensor_tensor(
                        out=xT[:, mo, hb * 512:(hb + 1) * 512], in0=ps[:, 0:512], scalar=MOD[:, 16 + mo, hb:hb + 1],
                        in1=xT[:, mo, hb * 512:(hb + 1) * 512], op0=ALU.mult, op1=ALU.add), reads=[pk, 'MOD'] + XKh[hb], writes=XKh[hb])
            if hb == 0:
                rms_stats(K, sq, rstd, xT, XKh[0], 0, 512, 'rstd2')
        dump(K, "xmid", xT[:], [128, 8, 1024], XK)
        fw.op('dve', lambda e: e.tensor_copy(out=XH[:, :, 0], in_=xT[:, :, 512]), reads=XKh[1], writes=['XH'])
        fw.op('dve', lambda e: e.tensor_copy(out=XH[:, :, 1], in_=xT[:, :, 1023]), reads=XKh[1], writes=['XH'])
        fw.dma('pool', hc_in.ap(), XH[:].rearrange("p a b -> p (a b)"), reads=['XH'], writes=['hc_in'])
        fw.async_op('pool', lambda g: g.collective_compute("AllGather", ALU.bypass, replica_groups=[[0, 1, 2, 3], [4, 5, 6, 7]],
                                                           ins=[hc_in.ap().opt()], outs=[hc_out.ap().opt()]), reads=['hc_in'], writes=['hc_out'])
        fw.dma('pool', HG[:].rearrange("p r a b -> p r (a b)"), hc_out.ap().rearrange("(r p) f -> p r f", p=128), reads=['hc_out'], writes=['HG'])
        rms_apply(K, rstd, tmpn, xT, XKh[0], 0, 512, h2T, 0, ['h2T0'], A2(0), B2(0), 'rstd2')
        rms_stats(K, sq, rstdS, xT, XKh[1], 512, 512, 'rstd2S')
        rms_apply(K, rstdS, tmpn, xT, XKh[1], 512, 512, h2T, 512, ['h2T512'], A2(1), B2(1), 'rstd2S')
        emit_halo_select()
        rmsnorm_block(K, sq, rstd, tmpn, XHh, ['XHh'], 0, 2, h2T, 1024, ['h2T1024'], A2(1), B2(1), 'rstd2')
        H2K = ['h2T0', 'h2T512', 'h2T1024']
        HM = sb("HM", [128, 22, 1024], BF16)
        W2f = sb("W2f", [128, 22, 1024], BF16)

        def load_w2(q):
            fw.dma('pool', W2f[:, q * 6:min(22, (q + 1) * 6), :], I.wf2[q * 768:min(2816, (q + 1) * 768), :].rearrange("(k p) c -> p k c", p=128), writes=['W2f'])
        Zf = [sb("Zf%d" % i, [128, 1030], BF16) for i in range(2)]
        uf = sb("uf", [128, 1024]); sf = sb("sf", [128, 1024])
        for zi in range(2):
            fw.op('pool', lambda e, zi=zi: e.memset(Zf[zi][:, 0:516].rearrange("p (s t) -> p s t", s=2)[:, :, 0:258:257], 0.0), writes=['Zf%d' % zi])
        for g in range(11):
            slab, sk = load_w(K, [(I.w1[:, g * 256:(g + 1) * 256], 256), (I.w3[:, g * 256:(g + 1) * 256], 256)])
            if g in (2, 4, 6, 8):
                load_w2(g // 2 - 1)
            for m in range(2):
                f = g * 2 + m
                Zt = Zf[f % 2]; zk = 'Zf%d' % (f % 2)
                for (col0, kind) in ((0, 'P'), (512, 'S')):
                    ps, pk = bank(K)
                    for k in range(8):
                        fw.op('pe', lambda t, ps=ps, k=k, m=m, col0=col0: t.matmul(ps[:, 0:512], slab[:, k, m * 128:(m + 1) * 128], h2T[:, k, col0:col0 + 512],
                                                                                start=(k == 0), stop=(k == 7)), reads=[sk] + H2K, writes=[pk])
                    if kind == 'P':
                        evac(K, 'act', Zt[:, 0:516].rearrange("p (s t) -> p s t", s=2)[:, :, 1:257], ps[:, 0:512].rearrange("p (s t) -> p s t", s=2), [pk], [zk])
                    else:
                        evac(K, 'act', Zt[:, 517:1029], ps[:, 0:512], [pk], [zk])
                ps, pk = bank(K)
                for k in range(8):
                    fw.op('pe', lambda t, ps=ps, k=k, m=m: t.matmul(ps[:, 0:2], slab[:, k, m * 128:(m + 1) * 128], h2T[:, k, 1024:1026], start=(k == 0), stop=(k == 7)),
                          reads=[sk] + H2K, writes=[pk])
                fw.op('dve', lambda e, ps=ps: e.tensor_scalar(out=Zt[:, 516:517], in0=ps[:, 0:1], scalar1=K.cst[:, 12:13], scalar2=None, op0=ALU.mult),
                      reads=[pk, 'cst'], writes=[zk])
                fw.op('dve', lambda e, ps=ps: e.tensor_scalar(out=Zt[:, 1029:1030], in0=ps[:, 1:2], scalar1=K.cst[:, 13:14], scalar2=None, op0=ALU.mult),
                      reads=[pk, 'cst'], writes=[zk])
                w = [PV[:, PV_WC + j * 22 + f:PV_WC + j * 22 + f + 1] for j in range(3)]
                zP = lambda off: Zt[:, 0:516].rearrange("p (s t) -> p s t", s=2)[:, :, off:off + 256]
                zS = lambda off: Zt[:, 516 + off:516 + off + 512]
                uP = uf[:, 0:512].rearrange("p (s t) -> p s t", s=2)
                uS = uf[:, 512:1024]
                for (zv, uv) in ((zP, uP), (zS, uS)):
                    fw.op('act', lambda e, zv=zv, uv=uv: e.activation(out=uv, in_=zv(0), func=AF.Identity, scale=w[0]), reads=[zk, 'PV'], writes=['uf'])
                    fw.op('dve', lambda e, zv=zv, uv=uv: e.scalar_tensor_tensor(out=uv, in0=zv(1), scalar=w[1], in1=uv, op0=ALU.mult, op1=ALU.add),
                          reads=[zk, 'PV', 'uf'], writes=['uf'])
                    fw.op('dve', lambda e, zv=zv, uv=uv: e.scalar_tensor_tensor(out=uv, in0=zv(2), scalar=w[2], in1=uv, op0=ALU.mult, op1=ALU.add),
                          reads=[zk, 'PV', 'uf'], writes=['uf'])
                fw.op('act', lambda e: e.activation(out=sf[:], in_=uf[:], func=AF.Silu), reads=['uf'], writes=['sf'])
                for hb in range(2):
                    ps, pk = bank(K)
                    for k in range(8):
                        fw.op('pe', lambda t, ps=ps, k=k, m=m, hb=hb: t.matmul(ps[:, 0:512], slab[:, k, 256 + m * 128:256 + (m + 1) * 128], h2T[:, k, hb * 512:(hb + 1) * 512],
                                                                            start=(k == 0), stop=(k == 7)), reads=[sk] + H2K, writes=[pk])
                    fw.op('dve', lambda e, ps=ps, hb=hb, f=f: e.tensor_tensor(out=HM[:, f, hb * 512:(hb + 1) * 512], in0=ps[:, 0:512], in1=sf[:, hb * 512:(hb + 1) * 512], op=ALU.mult),
                          reads=[pk, 'sf'], writes=[('HM', f)])
        HMK = [('HM', f) for f in range(22)]
        for g in range(2):
            for m in range(4):
                mo = g * 4 + m
                for hb in range(2):
                    ps, pk = bank(K)
                    for f in range(22):
                        fw.op('pe', lambda t, ps=ps, f=f, mo=mo, hb=hb: t.matmul(ps[:, 0:512], W2f[:, f, mo * 128:(mo + 1) * 128], HM[:, f, hb * 512:(hb + 1) * 512],
                                                                            start=(f == 0), stop=(f == 21)), reads=['W2f'] + HMK, writes=[pk])
                    fw.op('dve', lambda e, ps=ps, mo=mo, hb=hb: e.scalar_tensor_tensor(
                        out=xT[:, mo, hb * 512:(hb + 1) * 512], in0=ps[:, 0:512], scalar=MOD[:, 40 + mo, hb:hb + 1],
                        in1=xT[:, mo, hb * 512:(hb + 1) * 512], op0=ALU.mult, op1=ALU.add), reads=[pk, 'MOD'] + XK, writes=XK)
    with Scope(K):
        sq = sb("sq3", [128, 8, 512], BF16); rstd = sb("rstd3", [128, 512]); tmpn = sb("tmpn3", [128, 2, 512])
        YF = sb("YF", [128, 8, 512])
        OT = [sb("OT%d" % i, [128, 1024]) for i in range(2)]
        oc = 0
        for hb, dst in ((0, O.yp), (1, O.ys)):
            rmsnorm_block(K, sq, rstd, tmpn, xT, XK, hb * 512, 512, YF, 0, ['YF'],
                          lambda c: PV[:, PV_NORMF + c:PV_NORMF + c + 1], None)
            for tb in range(4):
                ot = OT[oc % 2]; otk = 'OT%d' % (oc % 2); oc += 1
                for half in range(2):
                    ps, pk = bank(K)
                    for c4 in range(4):
                        c = half * 4 + c4
                        fw.op('pe', lambda t, ps=ps, c=c, c4=c4, tb=tb: t.transpose(ps[:, c4 * 128:(c4 + 1) * 128], YF[:, c, tb * 128:(tb + 1) * 128], K.ident_f[:]),
                              reads=['YF', 'ident_f'], writes=[pk])
                    evac(K, 'act' if half == 0 else 'dve', ot[:, half * 512:(half + 1) * 512], ps[:, 0:512], [pk], [otk])
                fw.dma('sp', dst[tb * 128:(tb + 1) * 128, :], ot[:], reads=[otk])


def _prep_inputs(inp):
    f = lambda a: np.ascontiguousarray(np.asarray(a, dtype=np.float32))
    x_prompt = f(inp['x_prompt']); x_sample = f(inp['x_sample'])
    shared = {}
    w_ada = f(inp['w_ada'][0]); b_ada = f(inp['b_ada'][0]); shared['w_in'] = f(inp['w_in'][0])
    shared['w2'] = f(inp['w2'][0]); shared['a2'] = f(inp['a2'][0]); shared['g2'] = f(inp['g2'][0])
    shared['w_out'] = f(inp['w_out'][0]); shared['w1'] = f(inp['w_ffn1'][0]); shared['w3'] = f(inp['w_ffn3'][0])
    shared['wf2'] = f(inp['w_ffn2'][0])
    shared['ident'] = np.eye(128, dtype=np.float32)
    bd = np.zeros((128, 128), np.float32); bd[:64, :64] = 1; bd[64:, 64:] = 1
    shared['bd1'] = bd
    s = np.arange(128)[:, None]; t = np.arange(128)[None, :]
    mq = np.zeros((128, 2, 4, 128), np.float32)
    for d, (strict, incl) in enumerate((((s < t), (s <= t)), ((s > t), (s >= t)))):
        mq[:, d, 0] = -1.0 * strict
        mq[:, d, 1] = 1.0 * incl
        mq[:, d, 2] = 1.0 * strict
        mq[:, d, 3] = 1.0 * incl
    shared['mq'] = mq.reshape(128, 1024)
    mb = np.zeros((128, 2, 128), np.float32)
    mb[:, 0] = -1.0 * (t < s)
    mb[:, 1] = -1.0 * (t > s)
    shared['mb'] = mb.reshape(128, 256)
    rst = np.ones((128, 512), np.float32); rst[:, ::128] = 0
    shared['rst'] = rst
    xi = np.zeros((128, 128), np.float32); xi[np.arange(128), (np.arange(128) + 64) % 128] = 1
    shared['xi'] = xi
    rpb = f(inp['rpb'][0])
    bt = np.full((128, 2, 8, 8, 64), NEG, np.float32)
    jj = np.arange(64)
    cs_ = np.clip(jj - 8, 0, 48)
    for par in range(2):
        for blk in range(8):
            for p in range(128):
                rel = (-8 if par == 0 else -7) + 2 * blk + p // 64
                ap = rel + 7
                kc = p % 64
                if ap < 0 or ap > 14:
                    continue
                ok = (kc >= cs_) & (kc < cs_ + 16)
                co = kc - jj + 15
                vals = rpb[:, ap, np.clip(co, 0, 30)]
                bt[p, par, blk] = np.where(ok[None, :], vals, NEG)
    shared['bt'] = bt.reshape(128, -1)
    pv_common = np.zeros((PV_ROWS, 128), np.float32)
    pv_common[PV_NORM1:PV_NORM1 + 8] = f(inp['norm1'][0]).reshape(8, 128)
    pv_common[PV_NORM2:PV_NORM2 + 8] = f(inp['norm2'][0]).reshape(8, 128)
    pv_common[PV_NORMF:PV_NORMF + 8] = f(inp['norm_f']).reshape(8, 128)
    pv_common[PV_WTS:PV_WTS + 45] = f(inp['w_ts'][0]).reshape(45, 128)
    pv_common[PV_W0:PV_W0 + 8] = f(inp['w0'][0]).reshape(8, 128)
    pv_common[PV_A0:PV_A0 + 8] = f(inp['a0'][0]).reshape(8, 128)
    pv_common[PV_KK:PV_KK + 4] = f(inp['k_k'][0]).reshape(4, 128)
    pv_common[PV_KA:PV_KA + 4] = f(inp['k_a'][0]).reshape(4, 128)
    pv_common[PV_RK:PV_RK + 4] = f(inp['r_k'][0]).reshape(4, 128)
    pv_common[PV_LNG:PV_LNG + 4] = f(inp['ln_x_g'][0]).reshape(4, 128)
    pv_common[PV_LNB:PV_LNB + 4] = f(inp['ln_x_b'][0]).reshape(4, 128)
    pv_common[PV_WC:PV_WC + 66] = f(inp['w_ffn_conv'][0]).reshape(66, 128)
    cache_k = f(inp['cache_k']); cache_v = f(inp['cache_v']); st = f(inp['state_rwkv'])
    cvec = f(inp['c']); c_ctx = f(inp['c_ctx'])
    in_maps = []
    for c in range(8):
        b, j = c // 4, c % 4
        m = dict(shared)
        m['xp'] = x_prompt[2 * c:2 * c + 2].reshape(512, D)
        m['xs'] = x_sample[b, 512 * j:512 * j + 512]
        xh = np.zeros((448, D), np.float32)
        lo = 512 * j - 256
        if lo >= 0:
            xh[0:256] = x_sample[b, lo:lo + 256]
        hi = 512 * j + 512
        if hi + 192 <= 2048:
            xh[256:448] = x_sample[b, hi:hi + 192]
        m['xh'] = xh
        m['ck'] = cache_k[b, 0].reshape(512, 512)
        m['cv'] = cache_v[b, 0].reshape(512, 512)
        m['s0'] = st[b, 0]
        pv = pv_common.copy()
        pv[PV_C:PV_C + 8] = c_ctx.reshape(8, 128)
        pv[PV_C + 8:PV_C + 16] = cvec[0].reshape(8, 128)
        pv[PV_C + 16:PV_C + 24] = cvec[1].reshape(8, 128)
        pv[PV_BADA:PV_BADA + 12] = b_ada[j * 1536:(j + 1) * 1536].reshape(12, 128)
        m['w_ada_sh'] = np.ascontiguousarray(w_ada[:, j * 1536:(j + 1) * 1536])
        m['pvec'] = pv
        rb = np.full((128, 8, 8), NEG, np.float32)
        for l in range(8):
            i = 8 * j + l
            si = min(max(i - 4, 0), 24)
            par = l % 2
            for blk in range(8):
                for half in range(2):
                    rel = (-8 if par == 0 else -7) + 2 * blk + half
                    kr = i + rel
                    if si <= kr < si + 8:
                        rb[half * 64:(half + 1) * 64, l, blk] = 0.0
        m['rbp'] = rb.reshape(128, 64)
        cst = np.zeros((128, 16), np.float32)
        cst[:, 0 + j] = 1.0
        cst[:, 14 + b] = 1.0
        if j > 0:
            cst[:, 4 + (j - 1)] = 1.0; cst[:, 12] = 1.0
        if j < 3:
            cst[:, 8 + (j + 1)] = 1.0; cst[:, 13] = 1.0
        m['cst'] = cst
        in_maps.append(m)
    return in_maps


_NC_CACHE = {}


def kernel(**inputs):
    in_maps = _prep_inputs(inputs)
    if 'nc' not in _NC_CACHE:
        _NC_CACHE['nc'] = build_nc()[0]
    nc = _NC_CACHE['nc']
    res = run_bass_kernel_spmd(nc, in_maps, core_ids=list(range(8)))
    R = res.results
    y_prompt = np.concatenate([R[c]['yp'].reshape(2, 256, D) for c in range(8)], 0)
    y_sample = np.stack([np.concatenate([R[b * 4 + j]['ys'] for j in range(4)], 0) for b in range(2)], 0)
    nk = np.concatenate([R[c]['nk'].reshape(2, 1, 256, 8, 64) for c in range(8)], 0)
    nv = np.concatenate([R[c]['nv'].reshape(2, 1, 256, 8, 64) for c in range(8)], 0)
    ns = np.concatenate([R[c]['ns'].reshape(2, 1, 2, 8, 64, 64) for c in range(8)], 0)
    return (y_prompt.astype(np.float32), y_sample.astype(np.float32), nk.astype(np.float32),
            nv.astype(np.float32), ns.astype(np.float32))
```

```python
import numpy as np
from contextlib import ExitStack
import concourse.bass as bass
import concourse.mybir as mybir
from concourse.bass_utils import run_bass_kernel_spmd

F32 = mybir.dt.float32
BF16 = mybir.dt.bfloat16
AF = mybir.ActivationFunctionType
ALU = mybir.AluOpType

D = 1024
NEG = -30000.0
EPS = 1e-6
GN_EPS = 64e-5

PV_NORM1, PV_NORM2, PV_NORMF, PV_BADA, PV_WTS, PV_W0, PV_A0 = 0, 8, 16, 24, 72, 117, 125
PV_KK, PV_KA, PV_RK, PV_LNG, PV_LNB, PV_WC, PV_C = 133, 137, 141, 145, 149, 153, 219
PV_ROWS = 256


class _Eng:
    def __init__(self, name, handle, sem):
        self.name = name
        self.h = handle
        self.sem = sem
        self.count = 0
        self.waited = {}
        self.dma_rr = 0


class FW:
    def __init__(self, nc, es, n_dma_sems=10):
        self.nc = nc
        mk = lambda n: es.enter_context(nc.semaphore(n))
        self.mk = mk
        self.E = {
            'pe': _Eng('pe', nc.tensor, mk('s_pe')),
            'act': _Eng('act', nc.scalar, mk('s_act')),
            'dve': _Eng('dve', nc.vector, mk('s_dve')),
            'pool': _Eng('pool', nc.gpsimd, mk('s_pool')),
            'sp': _Eng('sp', nc.sync, mk('s_sp')),
        }
        self.dma_sems = {}
        for q in ('sp', 'pool'):
            self.dma_sems[q] = [[mk('d_%s%d' % (q, i)), 0] for i in range(n_dma_sems)]
        self.last_write = {}
        self.readers = {}
        self.semobj = {}
        self.nops = 0
        self.bank_rr = 0

    def _need(self, needed, tok):
        if tok is None:
            return
        sem, val, owner = tok
        k = id(sem)
        self.semobj[k] = sem
        if needed.get(k, (0, None))[0] < val:
            needed[k] = (val, owner)

    def _deps(self, eng, reads, writes):
        needed = {}
        for r in reads:
            self._need(needed, self.last_write.get(r))
        for w in writes:
            self._need(needed, self.last_write.get(w))
            for t in self.readers.get(w, {}).values():
                self._need(needed, t)
        e = self.E[eng]
        for k, (val, owner) in needed.items():
            if owner == 'pe' and eng == 'pe':
                continue
            if e.waited.get(k, 0) < val:
                e.h.wait_ge(self.semobj[k], val)
                e.waited[k] = val

    def _track(self, tok, reads, writes):
        owner = tok[2]
        for w in writes:
            self.last_write[w] = tok
            self.readers[w] = {}
        for r in reads:
            if r in writes:
                continue
            self.readers.setdefault(r, {})[owner] = tok

    capture = None

    def replay(self, lst, n):
        for _ in range(min(n, len(lst))):
            eng, fn, reads, writes = lst.pop(0)
            self.op(eng, fn, reads, writes)

    def async_op(self, eng, fn, reads=(), writes=()):
        e = self.E[eng]
        self._deps(eng, reads, writes)
        sem = self.mk('cc_sem%d' % self.nops)
        ins = fn(e.h)
        ins.then_inc(sem, 1)
        self._track((sem, 1, 'cc%d' % self.nops), reads, writes)
        self.nops += 1
        return ins

    def op(self, eng, fn, reads=(), writes=()):
        if self.capture is not None:
            self.capture.append((eng, fn, list(reads), list(writes)))
            return None
        reads = [r.key if isinstance(r, LazyBank) else r for r in reads]
        writes = [r.key if isinstance(r, LazyBank) else r for r in writes]
        e = self.E[eng]
        bk = [r for r in reads if isinstance(r, str) and r.startswith('bank') and r not in writes]
        if bk:
            writes = list(writes) + bk
        self._deps(eng, reads, writes)
        ins = fn(e.h)
        e.count += 1
        ins.then_inc(e.sem, 1)
        self._track((e.sem, e.count, eng), reads, writes)
        self.nops += 1
        return ins

    def dma(self, q, out, in_, reads=(), writes=(), **kw):
        e = self.E[q]
        self._deps(q, reads, writes)
        slots = self.dma_sems[q]
        i = e.dma_rr % len(slots)
        e.dma_rr += 1
        sem, val = slots[i]
        k = id(sem)
        self.semobj[k] = sem
        if val > 0 and e.waited.get(k, 0) < val:
            e.h.wait_ge(sem, val)
            e.waited[k] = val
        ins = e.h.dma_start(out=out, in_=in_, **kw)
        ins.then_inc(sem, 16)
        slots[i][1] = val + 16
        self._track((sem, val + 16, 'dma_%s_%d' % (q, i)), reads, writes)
        self.nops += 1
        return ins

    def finish(self, eng='sp'):
        e = self.E[eng]
        for q, slots in self.dma_sems.items():
            for sem, val in slots:
                if val > 0 and e.waited.get(id(sem), 0) < val:
                    e.h.wait_ge(sem, val)
                    e.waited[id(sem)] = val
        for n, o in self.E.items():
            if n != eng and o.count > 0 and e.waited.get(id(o.sem), 0) < o.count:
                e.h.wait_ge(o.sem, o.count)
                e.waited[id(o.sem)] = o.count


class Ctx:
    pass


class LazyBank:
    def __init__(self, K):
        self.K = K
        self.v = None

    def get(self):
        if self.v is None:
            self.v = bank(self.K)
        return self.v

    @property
    def ps(self):
        return self.get()[0]

    @property
    def key(self):
        return self.get()[1]


def build_nc(stage=99, dbg=()):
    nc = bass.Bass("TRN2", target_bir_lowering=False)
    K = Ctx()
    K.nc = nc
    K.dbg = set(dbg)
    K.dbg_specs = {}

    def din(name, shape):
        return nc.dram_tensor(name, list(shape), F32, kind="ExternalInput").ap()

    def dout(name, shape):
        return nc.dram_tensor(name, list(shape), F32, kind="ExternalOutput").ap()

    I = Ctx()
    I.xp = din("xp", [512, D]); I.xs = din("xs", [512, D]); I.xh = din("xh", [448, D])
    I.ck = din("ck", [512, 512]); I.cv = din("cv", [512, 512]); I.s0 = din("s0", [2, 8, 64, 64])
    I.pvec = din("pvec", [PV_ROWS, 128])
    I.w_ada_sh = din("w_ada_sh", [D, 1536]); I.w_in = din("w_in", [D, 3456])
    I.w2 = din("w2", [2, 64, 512]); I.a2 = din("a2", [2, 64, 512]); I.g2 = din("g2", [128, 512])
    I.w_out = din("w_out", [D, D]); I.w1 = din("w1", [D, 2816]); I.w3 = din("w3", [D, 2816])
    I.wf2 = din("wf2", [2816, D])
    I.bt = din("bt", [128, 2 * 8 * 8 * 64]); I.rbp = din("rbp", [128, 64])
    I.cst = din("cst", [128, 16]); I.ident = din("ident", [128, 128])
    I.mq = din("mq", [128, 2 * 512]); I.mb = din("mb", [128, 2 * 128]); I.bd1 = din("bd1", [128, 128])
    I.rst = din("rst", [128, 512]); I.xi = din("xi", [128, 128])
    O = Ctx()
    O.yp = dout("yp", [512, D]); O.ys = dout("ys", [512, D]); O.nk = dout("nk", [512, 512])
    O.nv = dout("nv", [512, 512]); O.ns = dout("ns", [2, 2, 8, 64, 64])
    K.I = I; K.O = O

    with ExitStack() as es:
        K.es = es
        fw = FW(nc, es)
        K.fw = fw
        K.cur_es = es
        K.sb = lambda n, s, d=F32: K.cur_es.enter_context(nc.sbuf_tensor("sb_" + n, list(s), d))
        K.banks = [es.enter_context(nc.psum_tensor("bank%d" % i, [128, 512], F32)) for i in range(8)]
        emit_program(K, stage)
        fw.finish('sp')
    K.nops = fw.nops
    return nc, K


def bank(K):
    fw = K.fw
    i = fw.bank_rr % 8
    fw.bank_rr += 1
    return K.banks[i], 'bank%d' % i


def dump(K, name, ap, shape, reads):
    if name not in K.dbg:
        return
    t = K.nc.dram_tensor("dbg_" + name, list(shape), ap.dtype, kind="ExternalOutput").ap()
    K.fw.dma('sp', t, ap, reads=reads)
    K.dbg_specs[name] = shape


def emit_program(K, stage):
    nc, fw, I, O, sb = K.nc, K.fw, K.I, K.O, K.sb
    ident_f = sb("ident_f", [128, 128]); ident_b = sb("ident_b", [128, 128], BF16)
    ones_b = sb("ones_b", [128, 128], BF16)
    bd1_b = sb("bd1_b", [128, 128], BF16)
    cst = sb("cst", [128, 16])
    PV = sb("PV", [128, PV_ROWS])
    K.ident_f, K.ident_b, K.ones_b, K.bd1_b, K.cst, K.PV = ident_f, ident_b, ones_b, bd1_b, cst, PV
    fw.dma('sp', ident_f[:], I.ident[:, :], writes=['ident_f'])
    fw.dma('pool', ident_b[:], I.ident[:, :], writes=['ident_b'])
    fw.dma('pool', bd1_b[:], I.bd1[:, :], writes=['bd1_b'])
    fw.dma('sp', cst[:], I.cst[:, :], writes=['cst'])
    fw.op('pool', lambda e: e.memset(ones_b[:], 1.0), writes=['ones_b'])
    epsc = sb("epsc", [128, 2])
    fw.op('pool', lambda e: e.memset(epsc[:, 0:1], EPS), writes=['epsc'])
    fw.op('pool', lambda e: e.memset(epsc[:, 1:2], GN_EPS), writes=['epsc'])
    K.epsc = epsc

    pv_rows = sb("pv_rows", [128, 2, 128])
    fw.dma('sp', pv_rows[:], I.pvec.rearrange("(a p) f -> p a f", p=128), writes=['pv_rows'])
    for a in range(2):
        ps, pk = bank(K)
        fw.op('pe', lambda t, a=a, ps=ps: t.transpose(ps[:, 0:128], pv_rows[:, a, :], ident_f[:]),
              reads=['pv_rows', 'ident_f'], writes=[pk])
        fw.op('dve', lambda e, a=a, ps=ps: e.tensor_copy(out=PV[:, a * 128:(a + 1) * 128], in_=ps[:, 0:128]),
              reads=[pk], writes=['PV'])

    MOD = sb("MOD", [128, 48, 2])
    cs = sb("cs", [128, 24], BF16)
    K.MOD = MOD
    fw.op('act', lambda e: e.activation(out=cs[:], in_=PV[:, PV_C:PV_C + 24], func=AF.Silu), reads=['PV'], writes=['cs'])
    K.wslab_rr = 0
    K.AB = sb("AB", [128, 2, 8, 2])
    K.cs = cs
    if stage <= 0:
        return
    emit_front(K, stage)


def barrier(K):
    fw = K.fw
    for n, e in fw.E.items():
        for q, slots in fw.dma_sems.items():
            for sem, val in slots:
                if val > 0 and e.waited.get(id(sem), 0) < val:
                    e.h.wait_ge(sem, val)
                    e.waited[id(sem)] = val
        for n2, o in fw.E.items():
            if n2 != n and o.count > 0 and e.waited.get(id(o.sem), 0) < o.count:
                e.h.wait_ge(o.sem, o.count)
                e.waited[id(o.sem)] = o.count


class Scope:
    def __init__(self, K):
        self.K = K

    def __enter__(self):
        self.prev = self.K.cur_es
        self.es = ExitStack()
        self.es.__enter__()
        self.K.cur_es = self.es
        return self

    def __exit__(self, *a):
        barrier(self.K)
        self.K.cur_es = self.prev
        return self.es.__exit__(*a)


def evac(K, eng, out, in_, reads, writes, scale=None):
    if eng == 'act':
        if scale is None:
            return K.fw.op('act', lambda e: e.activation(out=out, in_=in_, func=AF.Copy), reads=reads, writes=writes)
        return K.fw.op('act', lambda e: e.activation(out=out, in_=in_, func=AF.Identity, scale=scale), reads=reads, writes=writes)
    if scale is None:
        return K.fw.op(eng, lambda e: e.tensor_copy(out=out, in_=in_), reads=reads, writes=writes)
    return K.fw.op(eng, lambda e: e.tensor_scalar(out=out, in0=in_, scalar1=scale, scalar2=None, op0=ALU.mult), reads=reads, writes=writes)


def load_w(K, parts):
    si = K.wslab_rr % 2
    K.wslab_rr += 1
    slab = K.wslab[si]
    key = 'wslab%d' % (si if K.wslab[0] is not K.wslab[1] else 0)
    c = 0
    for ap, n in parts:
        K.fw.dma('pool', slab[:, :, c:c + n], ap.rearrange("(k p) c -> p k c", p=128), writes=[key])
        c += n
    return slab, key


def emit_mod_load(K):
    nc, fw, I, sb = K.nc, K.fw, K.I, K.sb
    PV, cst, MOD, cs = K.PV, K.cst, K.MOD, K.cs
    md_in = nc.dram_tensor("md_in", [128, 48], F32)
    md_out = nc.dram_tensor("md_out", [512, 48], F32)
    wslab = [sb("wslab%d" % i, [128, 8, 768], BF16) for i in range(2)]
    K.wslab = wslab
    MP = sb("MP", [128, 12, 4]); MG = sb("MG", [128, 48, 4])
    fw.op('pool', lambda e: e.memset(MP[:].rearrange("p a b -> p (a b)"), 0.0), writes=['MP'])
    for g in range(2):
        fw.dma('pool', wslab[g][:], I.w_ada_sh[:, g * 768:(g + 1) * 768].rearrange("(k p) c -> p k c", p=128), writes=['wslab%d' % g])
    K.mod_tiles = (wslab, MP, MG, md_in, md_out)


def emit_mod_compute(K):
    nc, fw, I, sb = K.nc, K.fw, K.I, K.sb
    PV, cst, MOD, cs = K.PV, K.cst, K.MOD, K.cs
    wslab, MP, MG, md_in, md_out = K.mod_tiles
    modps, modk = bank(K)
    for g in range(2):
        slab = wslab[g]
        for m in range(6):
            mm = g * 6 + m
            for k in range(8):
                fw.op('pe', lambda t, m=m, k=k, mm=mm, slab=slab: t.matmul(modps[:, 3 * mm:3 * mm + 3], slab[:, k, m * 128:(m + 1) * 128], cs[:, k:24:8],
                                                                         start=(k == 0), stop=(k == 7)), reads=['wslab%d' % g, 'cs'], writes=[modk])
    fw.op('dve', lambda e: e.tensor_tensor(out=MP[:, :, 0:3], in0=modps[:, 0:36].rearrange("p (m c) -> p m c", c=3),
                                           in1=PV[:, PV_BADA:PV_BADA + 12].unsqueeze(2).broadcast_to([128, 12, 3]), op=ALU.add),
          reads=[modk, 'PV'], writes=['MP'])
    fw.dma('pool', md_in.ap(), MP[:].rearrange("p a b -> p (a b)"), reads=['MP'], writes=['md_in'])
    fw.async_op('pool', lambda g: g.collective_compute("AllGather", ALU.bypass, replica_groups=[[0, 1, 2, 3], [4, 5, 6, 7]],
                                                 ins=[md_in.ap().opt()], outs=[md_out.ap().opt()]), reads=['md_in'], writes=['md_out'])
    fw.dma('pool', MG[:].rearrange("p (r m) c -> p r (m c)", r=4), md_out.ap().rearrange("(r p) f -> p r f", p=128), reads=['md_out'], writes=['MG'])
    fw.op('dve', lambda e: e.tensor_copy(out=MOD[:, :, 0], in_=MG[:, :, 0]), reads=['MG'], writes=['MOD'])
    fw.op('dve', lambda e: e.tensor_scalar(out=MOD[:, :, 1], in0=MG[:, :, 1], scalar1=cst[:, 14:15], scalar2=None, op0=ALU.mult), reads=['MG', 'cst'], writes=['MOD'])
    fw.op('dve', lambda e: e.scalar_tensor_tensor(out=MOD[:, :, 1], in0=MG[:, :, 2], scalar=cst[:, 15:16], in1=MOD[:, :, 1], op0=ALU.mult, op1=ALU.add),
          reads=['MG', 'cst', 'MOD'], writes=['MOD'])
    AB = K.AB
    for which, (sc_m, nrow) in enumerate(((8, PV_NORM1), (32, PV_NORM2))):
        for cond in range(2):
            fw.op('dve', lambda e, which=which, sc_m=sc_m, nrow=nrow, cond=cond: e.scalar_tensor_tensor(
                out=AB[:, which, :, cond], in0=MOD[:, sc_m:sc_m + 8, cond], scalar=1.0,
                in1=PV[:, nrow:nrow + 8], op0=ALU.add, op1=ALU.mult),
                reads=['MOD', 'PV'], writes=['AB'])
    dump(K, "MOD", MOD[:], [128, 48, 2], ['MOD'])


def emit_front(K, stage):
    nc, fw, I, O, sb = K.nc, K.fw, K.I, K.O, K.sb
    xT = sb("xT", [128, 8, 1024])
    mixT = sb("mixT", [128, 8, 1024], BF16)
    K.xT, K.mixT = xT, mixT
    xkeys = [('xT', c) for c in range(0, 1024, 128)]
    K.xkeys = xkeys
    with Scope(K):
        hT = sb("hT", [128, 8, 1026], BF16)
        K.hT = hT
        hscope = Scope(K)
        hscope.__enter__()
        hTh = sb("hTh", [128, 8, 448], BF16)
        K.hTh = hTh
        K.vctx = sb("vctx", [128, 4, 512], BF16)
        K.MB = sb("MB", [128, 2, 8, 8, 64], BF16)
        K.RB = sb("RB", [128, 64])
        K.cktm = [sb("cktm%d" % i, [128, 512]) for i in range(4)]
        K.wslabA = [sb("wslabA%d" % i, [128, 8, 768], BF16) for i in range(2)]
        with Scope(K):
            xTh = sb("xTh", [128, 8, 448])
            xtm = [sb("xtm%d" % i, [128, 1024]) for i in range(3)]
            emit_mod_load(K)
            blocks = []
            for b in range(4):
                blocks.append((I.xp[b * 128:(b + 1) * 128, :], 128, xT, b * 128, 'xT'))
            for b in range(4):
                blocks.append((I.xs[b * 128:(b + 1) * 128, :], 128, xT, 512 + b * 128, 'xT'))
            for b in range(4):
                n = 128 if b < 3 else 64
                blocks.append((I.xh[b * 128:b * 128 + n, :], n, xTh, b * 128, 'xTh'))
            for bi, (src, n, dst, col, dk) in enumerate(blocks):
                t = xtm[bi % 3]; tk = 'xtm%d' % (bi % 3)
                fw.dma('sp', t[0:n, :], src, writes=[tk])
                for half in range(2):
                    ps, pk = bank(K)
                    for c4 in range(4):
                        c = half * 4 + c4
                        fw.op('pe', lambda tt, t=t, n=n, c=c, c4=c4, ps=ps: tt.transpose(
                            ps[:, c4 * 128:c4 * 128 + n], t[0:n, c * 128:(c + 1) * 128], K.ident_f[0:n, 0:n]),
                            reads=[tk, 'ident_f'], writes=[pk])
                    evac(K, 'act' if (bi + half) % 2 == 0 else 'dve', dst[:, half * 4:half * 4 + 4, col:col + n],
                         ps[:].rearrange("p (c t) -> p c t", c=4)[:, :, 0:n], [pk], [(dk, col)])
            dump(K, "xT", xT[:], [128, 8, 1024], xkeys)

            sq = sb("sq", [128, 8, 512], BF16)
            emit_mod_compute(K)
            fw.dma('pool', K.vctx[:], I.cv.rearrange("(b p) c -> p b c", p=128), writes=['vctx'])
            fw.dma('pool', K.MB[:].rearrange("p a b h j -> p (a b h j)"), I.bt[:, :], writes=['MB'])
            fw.dma('sp', K.RB[:], I.rbp[:, :], writes=['RB'])
            for kb in range(4):
                fw.dma('sp', K.cktm[kb][:, :], I.ck[kb * 128:(kb + 1) * 128, :], writes=['cktm%d' % kb])
            K.wslab = K.wslabA
            K.pre_slab = load_w(K, [(I.w_in[:, 0:256], 256), (I.w_in[:, 512:768], 256), (I.w_in[:, 1024:1280], 256)])
            rstds = [sb("rstdF%d" % i, [128, 512]) for i in range(3)]
            tmpn = sb("tmpn", [128, 2, 512])
            hkeys = [('xTh', c) for c in range(0, 512, 128)]
            nblocks = ((xT, xkeys, 0, 512, hT, 0, 'hT0', 0), (xT, xkeys, 512, 512, hT, 512, 'hT512', 1), (xTh, hkeys, 0, 448, hTh, 0, 'hT1024', 1))
            for bi_, (src, skeys, scol, n, dst, dcol, okey, cond) in enumerate(nblocks):
                rms_stats(K, sq, rstds[bi_], src, skeys, scol, n, 'rstd%d' % bi_)
            for bi_, (src, skeys, scol, n, dst, dcol, okey, cond) in enumerate(nblocks):
                rms_apply(K, rstds[bi_], tmpn, src, skeys, scol, n, dst, dcol, [okey],
                          lambda c, cond=cond: K.AB[:, 0, c, cond:cond + 1],
                          lambda c, cond=cond: K.MOD[:, c, cond:cond + 1], 'rstd%d' % bi_)
            fw.op('dve', lambda e: e.tensor_copy(out=hT[:, :, 1024:1026], in_=hTh[:, :, 255:257]), reads=['hT1024'], writes=['hTx'])
        with Scope(K):
            emit_attention(K, stage)
        hscope.__exit__(None, None, None)
        with Scope(K):
            emit_rwkv(K, stage)
    if stage <= 3:
        return
    emit_back(K, stage)


def rmsnorm_block(K, sq, rstd, tmpn, src, src_keys, scol, n, dst, dcol, out_keys, Asc, Bsc, rkey='rstd'):
    rms_stats(K, sq, rstd, src, src_keys, scol, n, rkey)
    rms_apply(K, rstd, tmpn, src, src_keys, scol, n, dst, dcol, out_keys, Asc, Bsc, rkey)


def rms_stats(K, sq, rstd, src, src_keys, scol, n, rkey='rstd'):
    fw = K.fw
    for c in range(8):
        fw.op('act', lambda e, c=c: e.activation(out=sq[:, c, 0:n], in_=src[:, c, scol:scol + n], func=AF.Square),
              reads=src_keys, writes=[('sq', c)])
    ps, pk = bank(K)
    for c in range(8):
        fw.op('pe', lambda t, c=c, ps=ps: t.matmul(ps[:, 0:n], K.ones_b[:], sq[:, c, 0:n], start=(c == 0), stop=(c == 7)),
              reads=[('sq', c), 'ones_b'], writes=[pk])
    fw.op('act', lambda e, ps=ps: e.activation(out=rstd[:, 0:n], in_=ps[:, 0:n], func=AF.Ln,
                                               bias=K.epsc[:, 0:1], scale=1.0 / D),
          reads=[pk, 'epsc'], writes=[rkey])
    fw.op('act', lambda e: e.activation(out=rstd[:, 0:n], in_=rstd[:, 0:n], func=AF.Exp, scale=-0.5), reads=[rkey], writes=[rkey])


def rms_apply(K, rstd, tmpn, src, src_keys, scol, n, dst, dcol, out_keys, Asc, Bsc, rkey='rstd'):
    fw = K.fw
    for c in range(8):
        fw.op('dve', lambda e, c=c: e.tensor_tensor(out=tmpn[:, c % 2, 0:n], in0=src[:, c, scol:scol + n], in1=rstd[:, 0:n], op=ALU.mult),
              reads=src_keys + [rkey], writes=[('tmpn', c % 2)])
        if Bsc is not None:
            fw.op('act', lambda e, c=c: e.activation(out=dst[:, c, dcol:dcol + n], in_=tmpn[:, c % 2, 0:n], func=AF.Identity,
                                                     bias=Bsc(c), scale=Asc(c)),
                  reads=[('tmpn', c % 2), 'MOD', 'AB', 'PV'], writes=out_keys)
        else:
            fw.op('act', lambda e, c=c: e.activation(out=dst[:, c, dcol:dcol + n], in_=tmpn[:, c % 2, 0:n], func=AF.Identity,
                                                     scale=Asc(c)),
                  reads=[('tmpn', c % 2), 'MOD', 'AB', 'PV'], writes=out_keys)


def emit_attention(K, stage=99):
    nc, fw, I, O, sb = K.nc, K.fw, K.I, K.O, K.sb
    hT, mixT = K.hT, K.mixT
    hsel = lambda k, col0, n: hT[:, k, col0:col0 + n] if col0 < 1024 else K.hTh[:, k, col0 - 1024:col0 - 1024 + n]
    kctx = sb("kctx", [128, 4, 512], BF16)
    cktm = K.cktm
    vctx, MB, RB = K.vctx, K.MB, K.RB
    K.wslab = K.wslabA
    for kb in range(4):
        t = cktm[kb]; tk = 'cktm%d' % kb
        ps, pk = bank(K)
        for p in range(4):
            fw.op('pe', lambda tt, t=t, p=p, ps=ps: tt.transpose(ps[:, p * 128:(p + 1) * 128], t[:, p * 128:(p + 1) * 128], K.ident_f[:]),
                  reads=[tk, 'ident_f'], writes=[pk])
        evac(K, 'act' if kb % 2 == 0 else 'dve', kctx[:, :, kb * 128:(kb + 1) * 128],
             ps[:].rearrange("p (c t) -> p c t", c=4), [pk], ['kctx'])
    HK = ['hT0', 'hT512', 'hT1024']
    QP = sb("QP", [128, 2, 2, 2, 256], BF16)
    QS = sb("QS", [128, 2, 8, 2, 64], BF16)
    kP = sb("kP", [128, 2, 512], BF16)
    kS = sb("kS", [128, 2, 1536], BF16)
    vP = sb("vP", [128, 4, 256], BF16)
    vS = sb("vS", [128, 12, 256], BF16)
    E = [sb("E%d" % i, [128, 12, 256], BF16) for i in range(2)]
    EP = sb("EP", [128, 2, 2, 512], BF16)
    rec = sb("rec", [128, 512])
    recS = [sb("recS%d" % i, [128, 256]) for i in range(2)]
    stkv = [sb("stkv%d" % i, [128, 512]) for i in range(2)]
    fw.op('act', lambda e: e.activation(out=MB[:].rearrange("p a b h j -> p (a b h j)"),
                                        in_=MB[:].rearrange("p a b h j -> p (a b h j)"), func=AF.Exp),
          reads=['MB'], writes=['MB'])
    fw.op('pool', lambda e: e.memset(QP[:].rearrange("p a b c d -> p (a b c d)"), 0.0), writes=['QP'])
    fw.op('pool', lambda e: e.memset(QS[:].rearrange("p a b c d -> p (a b c d)"), 0.0), writes=['QS'])
    fw.op('pool', lambda e: e.memset(kS[:].rearrange("p a b -> p (a b)"), 0.0), writes=['kS'])
    fw.op('pool', lambda e: e.memset(vS[:].rearrange("p a b -> p (a b)"), 0.0), writes=['vS'])
    ecount = 0
    if stage < 1.15:
        return
    for hh in range(2):
        if hh == 0:
            slab, sk = K.pre_slab
        else:
            slab, sk = load_w(K, [(I.w_in[:, hh * 256:hh * 256 + 256], 256),
                                  (I.w_in[:, 512 + hh * 256:512 + hh * 256 + 256], 256),
                                  (I.w_in[:, 1024 + hh * 256:1024 + hh * 256 + 256], 256)])
        for pi in range(2):
            for (col0, isS) in ((0, False), (512, True)):
                ps, pk = bank(K)
                for k in range(8):
                    fw.op('pe', lambda t, k=k, ps=ps, pi=pi, col0=col0: t.matmul(
                        ps[:, 0:512], slab[:, k, pi * 128:(pi + 1) * 128], hT[:, k, col0:col0 + 512],
                        start=(k == 0), stop=(k == 7)), reads=[sk] + HK, writes=[pk])
                for h2 in range(2):
                    pr = slice(h2 * 64, h2 * 64 + 64)
                    if not isS:
                        evac(K, 'act' if h2 == 0 else 'dve', QP[pr, pi, :, h2, :],
                             ps[pr, :].rearrange("p (s t) -> p s t", s=2), [pk], ['QP'])
                    else:
                        evac(K, 'act' if h2 == 0 else 'dve', QS[pr, pi, :, h2, :],
                             ps[pr, :].rearrange("p (l t) -> p l t", l=8), [pk], ['QS'])
        for pi in range(2):
            for (col0, n, kind) in ((0, 512, 'P'), (512, 512, 'S'), (1024, 448, 'H')):
                ps, pk = bank(K)
                for k in range(8):
                    fw.op('pe', lambda t, k=k, ps=ps, pi=pi, col0=col0, n=n: t.matmul(
                        ps[:, 0:n], slab[:, k, 256 + pi * 128:256 + (pi + 1) * 128], hsel(k, col0, n),
                        start=(k == 0), stop=(k == 7)), reads=[sk] + HK, writes=[pk])
                if kind == 'P':
                    evac(K, 'act', kP[:, pi, :], ps[:, 0:512], [pk], ['kP'])
                elif kind == 'S':
                    evac(K, 'dve', kS[:, pi, 512:1024], ps[:, 0:512], [pk], ['kS'])
                else:
                    evac(K, 'act', kS[:, pi, 256:512], ps[:, 0:256], [pk], ['kS'])
                    evac(K, 'dve', kS[:, pi, 1024:1216], ps[:, 256:448], [pk], ['kS'])
        for tb in range(4):
            ps, pk = bank(K)
            for k in range(8):
                fw.op('pe', lambda t, k=k, ps=ps, tb=tb: t.matmul(
                    ps[:, 0:512], hT[:, k, tb * 128:(tb + 1) * 128], slab[:, k, 256:768],
                    start=(k == 0), stop=(k == 7)), reads=[sk] + HK, writes=[pk])
            st = stkv[tb % 2]; stk = 'stkv%d' % (tb % 2)
            evac(K, 'act', st[:], ps[:, 0:512], [pk], [stk])
            evac(K, 'dve', vP[:, tb, :], ps[:, 256:512], [pk], ['vP'])
            fw.dma('sp', O.nk[tb * 128:(tb + 1) * 128, hh * 256:(hh + 1) * 256], st[:, 0:256], reads=[stk])
            fw.dma('sp', O.nv[tb * 128:(tb + 1) * 128, hh * 256:(hh + 1) * 256], st[:, 256:512], reads=[stk])
        vsrc = [(4 + b, 512 + b * 128, 128) for b in range(4)] + [(2, 1024, 128), (3, 1152, 128), (8, 1280, 128), (9, 1408, 64)]
        for gi in range(0, 8, 2):
            ps, pk = bank(K)
            for gj in range(2):
                blk, col0, n = vsrc[gi + gj]
                for k in range(8):
                    fw.op('pe', lambda t, k=k, ps=ps, gj=gj, col0=col0, n=n: t.matmul(
                        ps[0:n, gj * 256:(gj + 1) * 256], hsel(k, col0, n), slab[:, k, 512:768],
                        start=(k == 0), stop=(k == 7)), reads=[sk] + HK, writes=[pk])
            for gj in range(2):
                blk, col0, n = vsrc[gi + gj]
                evac(K, 'act' if gj == 0 else 'dve', vS[0:n, blk, :], ps[0:n, gj * 256:(gj + 1) * 256], [pk], ['vS'])
        if stage < 1.25:
            continue
        for seq in range(2):
            for pi in range(2):
                p = 2 * hh + pi
                for kb in range(2):
                    ps, pk = bank(K)
                    fw.op('pe', lambda t, ps=ps, pi=pi, seq=seq, kb=kb: t.matmul(
                        ps[:, 0:512], kP[:, pi, seq * 256 + kb * 128:seq * 256 + (kb + 1) * 128],
                        QP[:, pi, seq, :, :].rearrange("p a q -> p (a q)"), start=True, stop=True),
                        reads=['kP', 'QP'], writes=[pk])
                    fw.op('act', lambda e, ps=ps, pi=pi, kb=kb: e.activation(out=EP[:, pi, kb, :], in_=ps[:, 0:512], func=AF.Exp, scale=0.125),
                          reads=[pk], writes=[('EP', pi, kb)])
                psd, pdk = bank(K)
                for kb in range(2):
                    fw.op('pe', lambda t, psd=psd, pi=pi, kb=kb: t.matmul(psd[:, 0:512], K.ones_b[:], EP[:, pi, kb, :], start=(kb == 0), stop=(kb == 1)),
                          reads=[('EP', pi, kb), 'ones_b'], writes=[pdk])
                psn, pnk = bank(K)
                for kb in range(2):
                    fw.op('pe', lambda t, psn=psn, pi=pi, kb=kb, seq=seq: t.matmul(
                        psn[:, 0:512], vP[:, seq * 2 + kb, pi * 128:(pi + 1) * 128], EP[:, pi, kb, :], start=(kb == 0), stop=(kb == 1)),
                        reads=[('EP', pi, kb), 'vP'], writes=[pnk])
                fw.op('act', lambda e, psd=psd: e.activation(out=rec[:, 0:512], in_=psd[:, 0:512], func=AF.Ln), reads=[pdk], writes=['rec'])
                fw.op('act', lambda e: e.activation(out=rec[:, 0:512], in_=rec[:, 0:512], func=AF.Exp, scale=-1.0), reads=['rec'], writes=['rec'])
                for h2 in range(2):
                    pr = slice(h2 * 64, h2 * 64 + 64)
                    fw.op('dve', lambda e, psn=psn, pr=pr, h2=h2, p=p, seq=seq: e.tensor_tensor(
                        out=mixT[pr, p, seq * 256:(seq + 1) * 256], in0=psn[pr, h2 * 256:(h2 + 1) * 256],
                        in1=rec[pr, h2 * 256:(h2 + 1) * 256], op=ALU.mult),
                        reads=[pnk, 'rec'], writes=[('mixT', p)])
        if stage < 1.35:
            continue
        for l in range(8):
            Et = E[ecount % 2]; ek = 'E%d' % (ecount % 2); ecount += 1
            par = l % 2
            b0 = (l + 1) // 2
            for g in range(6):
                ps, pk = bank(K)
                for gj in range(2):
                    blk = 2 * g + gj
                    for pi in range(2):
                        if blk < 8:
                            lhs = kS[:, pi, (b0 + blk) * 128:(b0 + blk + 1) * 128]
                            rk = 'kS'
                        else:
                            lhs = kctx[:, 2 * hh + pi, (blk - 8) * 128:(blk - 7) * 128]
                            rk = 'kctx'
                        fw.op('pe', lambda t, ps=ps, lhs=lhs, gj=gj, pi=pi, l=l: t.matmul(
                            ps[:, gj * 256 + pi * 128:gj * 256 + (pi + 1) * 128], lhs,
                            QS[:, pi, l, :, :].rearrange("p a q -> p (a q)"), start=True, stop=True),
                            reads=[rk, 'QS'], writes=[pk])
                for gj in range(2):
                    blk = 2 * g + gj
                    if blk < 8:
                        fw.op('act', lambda e, ps=ps, gj=gj, blk=blk, l=l, Et=Et: e.activation(
                            out=Et[:, blk, :], in_=ps[:, gj * 256:(gj + 1) * 256], func=AF.Exp, scale=0.125,
                            bias=RB[:, l * 8 + blk:l * 8 + blk + 1]), reads=[pk, 'RB'], writes=[(ek, blk)])
                    else:
                        fw.op('act', lambda e, ps=ps, gj=gj, blk=blk, Et=Et: e.activation(
                            out=Et[:, blk, :], in_=ps[:, gj * 256:(gj + 1) * 256], func=AF.Exp, scale=0.125),
                            reads=[pk], writes=[(ek, blk)])
            ekeys = [(ek, b) for b in range(12)]
            fw.op('dve', lambda e, Et=Et, par=par, hh=hh: e.tensor_tensor(
                out=Et[:, 0:8, :], in0=Et[:, 0:8, :],
                in1=MB[:, par, :, hh * 4:hh * 4 + 4, :].rearrange("p b h j -> p b (h j)"), op=ALU.mult),
                reads=ekeys[0:8] + ['MB'], writes=ekeys[0:8])
            rq = recS[(ecount - 1) % 2]; rqk = 'recS%d' % ((ecount - 1) % 2)
            psd, pdk = bank(K)
            for blk in range(12):
                fw.op('pe', lambda t, psd=psd, blk=blk, Et=Et: t.matmul(psd[:, 0:256], K.ones_b[:], Et[:, blk, :], start=(blk == 0), stop=(blk == 11)),
                      reads=[(ek, blk), 'ones_b'], writes=[pdk])
            psn, pnk = bank(K)
            for pi in range(2):
                for blk in range(12):
                    if blk < 8:
                        lhs = vS[:, b0 + blk, pi * 128:(pi + 1) * 128]; rk = 'vS'
                    else:
                        lhs = vctx[:, blk - 8, hh * 256 + pi * 128:hh * 256 + (pi + 1) * 128]; rk = 'vctx'
                    fw.op('pe', lambda t, psn=psn, lhs=lhs, blk=blk, pi=pi, Et=Et: t.matmul(
                        psn[:, pi * 128:(pi + 1) * 128], lhs, Et[:, blk, pi * 128:(pi + 1) * 128],
                        start=(blk == 0), stop=(blk == 11)), reads=[(ek, blk), rk], writes=[pnk])
            fw.op('act', lambda e, psd=psd: e.activation(out=rq[:, 0:256], in_=psd[:, 0:256], func=AF.Ln), reads=[pdk], writes=[rqk])
            fw.op('act', lambda e: e.activation(out=rq[:, 0:256], in_=rq[:, 0:256], func=AF.Exp, scale=-1.0), reads=[rqk], writes=[rqk])
            for h2 in range(2):
                pr = slice(h2 * 64, h2 * 64 + 64)
                fw.op('dve', lambda e, psn=psn, pr=pr, h2=h2, hh=hh, l=l: e.tensor_tensor(
                    out=mixT[pr, 2 * hh:2 * hh + 2, 512 + l * 64:512 + (l + 1) * 64],
                    in0=psn[pr, 0:256].rearrange("p (a b) -> p a b", a=2)[:, :, h2 * 64:(h2 + 1) * 64],
                    in1=rq[pr, 0:256].rearrange("p (a b) -> p a b", a=2)[:, :, h2 * 64:(h2 + 1) * 64], op=ALU.mult),
                    reads=[pnk, rqk], writes=[('mixT', 2 * hh), ('mixT', 2 * hh + 1)])
    if 'mixA' in K.dbg:
        md = sb("mixdbg", [128, 4, 1024])
        fw.op('dve', lambda e: e.tensor_copy(out=md[:], in_=mixT[:, 0:4, :]), reads=[('mixT', p) for p in range(4)], writes=['mixdbg'])
        dump(K, "mixA", md[:], [128, 4, 1024], ['mixdbg'])


C0 = 0.6065306597126334


def emit_rwkv(K, stage):
    nc, fw, I, O, sb = K.nc, K.fw, K.I, K.O, K.sb
    hT, mixT, PV = K.hT, K.mixT, K.PV
    HK = ['hT0', 'hT512', 'hTx']
    yT = sb("yT", [128, 4, 1024])
    YP = sb("YP", [128, 4, 2, 4, 128], BF16)
    TXW = sb("TXW", [128, 1024], BF16); UXA = sb("UXA", [128, 1024], BF16); SXG = sb("SXG", [128, 1024], BF16)
    CC = sb("CC", [128, 4, 2, 128])
    S0T = sb("S0T", [128, 4, 2, 64])
    W2b = sb("W2b", [128, 512], BF16); A2b = sb("A2b", [128, 512], BF16); G2b = sb("G2b", [128, 512], BF16)
    mqb = sb("mqb", [128, 2, 512], BF16); mbb = sb("mbb", [128, 2, 128], BF16)
    rst = sb("rst", [128, 512], BF16); xi_f = sb("xi_f", [128, 128]); bd1_f = sb("bd1_f", [128, 128])
    OMK = sb("OMK", [128, 4])
    eps12 = sb("eps12", [128, 1])
    K.wslab = [sb("wslabR0", [128, 8, 384], BF16)] * 2
    fw.dma('pool', W2b[:], I.w2.rearrange("d l c -> (d l) c"), writes=['W2b'])
    fw.dma('pool', A2b[:], I.a2.rearrange("d l c -> (d l) c"), writes=['A2b'])
    fw.dma('pool', G2b[:], I.g2[:, :], writes=['G2b'])
    fw.dma('pool', mqb[:].rearrange("p a b -> p (a b)"), I.mq[:, :], writes=['mqb'])
    fw.dma('pool', mbb[:].rearrange("p a b -> p (a b)"), I.mb[:, :], writes=['mbb'])
    fw.dma('pool', rst[:], I.rst[:, :], writes=['rst'])
    fw.dma('sp', xi_f[:], I.xi[:, :], writes=['xi_f'])
    fw.dma('sp', bd1_f[:], I.bd1[:, :], writes=['bd1_f'])
    fw.op('pool', lambda e: e.memset(eps12[:], 1e-12), writes=['eps12'])
    fw.op('dve', lambda e: e.tensor_scalar(out=OMK[:], in0=PV[:, PV_KA:PV_KA + 4], scalar1=-1.0, scalar2=1.0, op0=ALU.mult, op1=ALU.add),
          reads=['PV'], writes=['OMK'])
    with Scope(K):
        Z = [sb("Z%d" % i, [128, 1030], BF16) for i in range(3)]
        TT6 = sb("TT6", [128, 6, 512])
        T = [TT6[:, i, :] for i in range(6)]
        utmp = TT6[:, 0:2, :].rearrange("p a b -> p (a b)")
        Uset = [[sb("U%s%d" % (n, 0), [128, 1024], BF16) for n in ("r", "kr", "v")]] * 2
        RKK = sb("RKK", [128, 1024], BF16)
        Vaug = sb("Vaug", [128, 4, 2, 128], BF16)
        PSET = [dict(TL2=[sb("TL2_%d_%d" % (d, q), [128, 4, 2, 128], BF16) for d in range(2)],
                     TB=[sb("TB_%d_%d" % (d, q), [128, 512], BF16) for d in range(2)],
                     TK=[sb("TK_%d_%d" % (d, q), [128, 512], BF16) for d in range(2)],
                     TBK=[sb("TBK_%d_%d" % (d, q), [128, 4, 2, 128], BF16) for d in range(2)],
                     WC=[sb("WC_%d_%d" % (d, q), [128, 4]) for d in range(2)],
                     ) for q in range(2)]
        KKt1 = sb("KKt", [128, 512]); SQK1 = sb("SQK", [128, 512], BF16)
        for q in range(2):
            PSET[q]['KKt'] = KKt1; PSET[q]['SQK'] = SQK1
        AQ = [sb("AQ_%d" % d, [128, 8, 3, 128], BF16) for d in range(2)]
        NP = [sb("NP_%d" % d, [128, 8, 2, 128], BF16) for d in range(2)]; BB = [sb("BB_%d" % d, [128, 8, 128], BF16) for d in range(2)]
        NN = [NP[d][:, :, 0, :] for d in range(2)]
        PP = [NP[d][:, :, 1, :] for d in range(2)]
        ST32 = [sb("ST32_%d" % d, [128, 2, 128]) for d in range(2)]; STb = [sb("STb_%d" % d, [128, 2, 128], BF16) for d in range(2)]
        STW = [sb("STW_%d" % d, [128, 2, 128]) for d in range(2)]
        Xb = [sb("Xb_%d" % d, [128, 4, 128], BF16) for d in range(2)]; Ub = [sb("Ub_%d" % d, [128, 4, 128], BF16) for d in range(2)]
        stS = sb("stS", [128, 128])
        for zi in range(3):
            fw.op('pool', lambda e, zi=zi: e.memset(Z[zi][:, 0:516].rearrange("p (s t) -> p s t", s=2)[:, :, 0:258:257], 0.0), writes=['Z%d' % zi])

        def proj_mm(slab, sk, scol, zi):
            Zt = Z[zi]; zk = 'Z%d' % zi
            for (col0, kind) in ((0, 'P'), (512, 'S')):
                ps, pk = bank(K)
                for k in range(8):
                    fw.op('pe', lambda t, k=k, ps=ps, col0=col0: t.matmul(
                        ps[:, 0:512], slab[:, k, scol:scol + 128], hT[:, k, col0:col0 + 512], start=(k == 0), stop=(k == 7)),
                        reads=[sk] + HK, writes=[pk])
                if kind == 'P':
                    evac(K, 'act', Zt[:, 0:516].rearrange("p (s t) -> p s t", s=2)[:, :, 1:257],
                         ps[:, 0:512].rearrange("p (s t) -> p s t", s=2), [pk], [zk])
                else:
                    evac(K, 'act', Zt[:, 517:1029], ps[:, 0:512], [pk], [zk])
            ps, pk = bank(K)
            for k in range(8):
                fw.op('pe', lambda t, k=k, ps=ps: t.matmul(ps[:, 0:2], slab[:, k, scol:scol + 128], hT[:, k, 1024:1026], start=(k == 0), stop=(k == 7)),
                      reads=[sk] + HK, writes=[pk])
            fw.op('dve', lambda e, ps=ps: e.tensor_scalar(out=Zt[:, 516:517], in0=ps[:, 0:1], scalar1=K.cst[:, 12:13], scalar2=None, op0=ALU.mult),
                  reads=[pk, 'cst'], writes=[zk])
            fw.op('dve', lambda e, ps=ps: e.tensor_scalar(out=Zt[:, 1029:1030], in0=ps[:, 1:2], scalar1=K.cst[:, 13:14], scalar2=None, op0=ALU.mult),
                  reads=[pk, 'cst'], writes=[zk])

        def conv(zi, ci, dest, dkey, func=None):
            Zt = Z[zi]; zk = 'Z%d' % zi
            w = [PV[:, PV_WTS + j * 15 + ci:PV_WTS + j * 15 + ci + 1] for j in range(3)]
            zP = lambda off: Zt[:, 0:516].rearrange("p (s t) -> p s t", s=2)[:, :, off:off + 256]
            zS = lambda off: Zt[:, 516 + off:516 + off + 512]
            uP = utmp[:, 0:512].rearrange("p (s t) -> p s t", s=2)
            uS = utmp[:, 512:1024]
            final = dest if func is None else utmp
            dP = final[:, 0:512].rearrange("p (s t) -> p s t", s=2)
            dS = final[:, 512:1024]
            fkeys = [dkey] if func is None else ['T0', 'T1']
            for (zv, uv, dv) in ((zP, uP, dP), (zS, uS, dS)):
                fw.op('act', lambda e, zv=zv, uv=uv: e.activation(out=uv, in_=zv(0), func=AF.Identity, scale=w[0]), reads=[zk, 'PV'], writes=['T0', 'T1'])
                fw.op('dve', lambda e, zv=zv, uv=uv: e.scalar_tensor_tensor(out=uv, in0=zv(1), scalar=w[1], in1=uv, op0=ALU.mult, op1=ALU.add),
                      reads=[zk, 'PV', 'T0', 'T1'], writes=['T0', 'T1'])
                fw.op('dve', lambda e, zv=zv, uv=uv, dv=dv: e.scalar_tensor_tensor(out=dv, in0=zv(2), scalar=w[2], in1=uv, op0=ALU.mult, op1=ALU.add),
                      reads=[zk, 'PV', 'T0', 'T1'], writes=fkeys)
            if func is not None:
                fw.op('act', lambda e: e.activation(out=dest[:], in_=utmp[:], func=func), reads=['T0', 'T1'], writes=[dkey])

        slab, sk = load_w(K, [(I.w_in[:, 3072:3456], 384)])
        for zi in range(3):
            proj_mm(slab, sk, zi * 128, zi)
        conv(0, 12, TXW, 'TXW', AF.Tanh)
        conv(1, 13, UXA, 'UXA')
        conv(2, 14, SXG, 'SXG', AF.Sigmoid)

        def load_pair(p):
            return load_w(K, [(I.w_in[:, 1536 + p * 128:1536 + (p + 1) * 128], 128),
                              (I.w_in[:, 2048 + p * 128:2048 + (p + 1) * 128], 128),
                              (I.w_in[:, 2560 + p * 128:2560 + (p + 1) * 128], 128)])

        def proj_pair(slab, sk):
            for zi in range(3):
                proj_mm(slab, sk, zi * 128, zi)

        def conv_pair(p):
            us = Uset[p % 2]
            for zi, nm in enumerate(("r", "kr", "v")):
                conv(zi, zi * 4 + p, us[zi], 'U%s0' % nm)

        slab, sk = load_pair(0)
        proj_pair(slab, sk)
        conv_pair(0)
        Ur, Ukr, Uv = Uset[0]
        ukeys = {'Ur': 'Ur0', 'Ukr': 'Ukr0', 'Uv': 'Uv0'}
        base = dict(T=T, Ur=Ur, Ukr=Ukr, Uv=Uv, RKK=RKK, TXW=TXW, UXA=UXA, W2b=W2b, A2b=A2b, rst=rst, OMK=OMK, PSET=PSET, AQ=AQ, NN=NN, BB=BB, PP=PP, NP=NP,
                    ST32=ST32, STb=STb, STW=STW, Xb=Xb, Ub=Ub, Vaug=Vaug, YP=YP, yT=yT, CC=CC, mqb=mqb, mbb=mbb, xi_f=xi_f, stS=stS, eps12=eps12, ukeys=ukeys)

        def mkL(it):
            L = dict(base)
            hf = 1 - it % 2
            L.update(p=it // 2, half=hf, par=it % 2, hs=slice(hf * 512, hf * 512 + 512))
            return L

        def bonus(p):
            for hb in range(2):
                ps, pk = bank(K)
                fw.op('pe', lambda t, ps=ps, hb=hb: t.matmul(ps[:, 0:512], K.bd1_b[:], RKK[:, hb * 512:(hb + 1) * 512], start=True, stop=True),
                      reads=['RKK', 'bd1_b'], writes=[pk])
                fw.op('dve', lambda e, ps=ps, hb=hb: e.tensor_tensor(out=mixT[:, 4 + p, hb * 512:(hb + 1) * 512], in0=ps[:, 0:512],
                                                                      in1=Uv[:, hb * 512:(hb + 1) * 512], op=ALU.mult),
                      reads=[pk, 'Uv0'], writes=[('mixT', 4 + p)])

        def vaug(half):
            fw.op('pool', lambda e: e.memset(Vaug[:].rearrange("p a b c -> p (a b c)"), 0.0), writes=['Vaug'])
            ps, pk = bank(K)
            psb = ps[:].bitcast(BF16)
            for c in range(4):
                fw.op('pe', lambda t, c=c, psb=psb: t.transpose(psb[:, c * 128:(c + 1) * 128], Uv[:, half * 512 + c * 128:half * 512 + (c + 1) * 128], K.ident_b[:]),
                      reads=['Uv0', 'ident_b'], writes=[pk])
            for h2 in range(2):
                evac(K, 'act' if h2 == 0 else 'dve', Vaug[:, :, h2, h2 * 64:(h2 + 1) * 64],
                     psb[:, 0:512].rearrange("p (c f) -> p c f", c=4)[:, :, h2 * 64:(h2 + 1) * 64], [pk], ['Vaug'])

        ops0 = rwkv_prep(K, mkL(0))
        fw.replay(ops0, len(ops0))
        nslab = nsk = None
        cc_in = nc.dram_tensor("cc_in", [128, 1024], F32)
        cc_out = nc.dram_tensor("cc_out", [512, 1024], F32)
        for it in range(8):
            p, second = it // 2, it % 2
            L = mkL(it)
            if second == 0 and p < 3:
                nslab, nsk = load_pair(p + 1)
                L['hook_prep'] = lambda nslab=nslab, nsk=nsk: proj_pair(nslab, nsk)

            def hook_mid(L=L, it=it, p=p, second=second):
                if second == 1:
                    bonus(p)
                    if p < 3:
                        conv_pair(p + 1)
                return rwkv_prep(K, mkL(it + 1)) if it < 7 else []
            L['hook_mid'] = hook_mid
            L['vaug'] = vaug
            rwkv_core(K, L)
            if it == 6:
                fw.dma('pool', cc_in.ap(), CC[:].rearrange("p a b c -> p (a b c)"), reads=['CC'], writes=['cc_in'])
                fw.async_op('pool', lambda g: g.collective_compute("AllGather", ALU.bypass, replica_groups=[[0, 1, 2, 3], [4, 5, 6, 7]],
                                                             ins=[cc_in.ap().opt()], outs=[cc_out.ap().opt()]), reads=['cc_in'], writes=['cc_out'])
    if stage < 3:
        if 'yT' in K.dbg:
            dump(K, "yT", yT[:], [128, 4, 1024], [('yT%d' % p, t_) for p in range(4) for t_ in range(0, 1024, 128)])
            dump(K, "CC", CC[:], [128, 4, 2, 128], ['CC'])
        return
    rwkv_finish(K, locals())


def rwkv_prep(K, L):
    fw = K.fw
    PV = K.PV
    p, half, hs = L['p'], L['half'], L['hs']
    uk = L['ukeys']
    T, Ur, Ukr, Uv, RKK = L['T'], L['Ur'], L['Ukr'], L['Uv'], L['RKK']
    TXW, UXA, W2b, A2b, rst, OMK = L['TXW'], L['UXA'], L['W2b'], L['A2b'], L['rst'], L['OMK']
    par = L['par']
    PS = L['PSET'][par]
    TL2s, TBs, TKs, TBKs, WCs = PS['TL2'], PS['TB'], PS['TK'], PS['TBK'], PS['WC']
    AQs, NNs, BBs, PPs = L['AQ'], L['NN'], L['BB'], L['PP']
    ST32s, STbs, STWs, Xbs, Ubs = L['ST32'], L['STb'], L['STW'], L['Xb'], L['Ub']
    Vaug, YP, yT, CC, mqb, mbb, xi_f, stS = (L[k] for k in ('Vaug', 'YP', 'yT', 'CC', 'mqb', 'mbb', 'xi_f', 'stS'))
    O = K.O
    SG, LC, LX, AT, TE, T2 = T[0], T[1], T[2], T[3], T[4], T[5]
    kn = lambda base, d: ('%s_%d_%d' % (base, d, par)) if base in ('TL2', 'TB', 'TK', 'TBK', 'WC') else ('%s_%d' % (base, d))
    KKt, SQK = PS['KKt'], PS['SQK']
    kkk = 'KKt'
    eps12 = L['eps12']
    fw.capture = []

    def prep_kk():
        fw.op('dve', lambda e: e.tensor_scalar(out=KKt[:], in0=Ukr[:, hs], scalar1=PV[:, PV_KK + p:PV_KK + p + 1], scalar2=None, op0=ALU.mult),
              reads=[uk['Ukr'], 'PV'], writes=[kkk])
        fw.op('act', lambda e: e.activation(out=SQK[:], in_=KKt[:], func=AF.Square), reads=[kkk], writes=['SQK'])
        lb = LazyBank(K)
        fw.op('pe', lambda t: t.matmul(lb.ps[:, 0:512], K.bd1_b[:], SQK[:], start=True, stop=True), reads=['SQK', 'bd1_b'], writes=[lb])
        fw.op('act', lambda e: e.activation(out=T[5][:], in_=lb.ps[:, 0:512], func=AF.Ln, bias=eps12[:, 0:1], scale=1.0),
              reads=[lb, 'eps12'], writes=['T5'])
        fw.op('act', lambda e: e.activation(out=T[5][:], in_=T[5][:], func=AF.Exp, scale=-0.5), reads=['T5'], writes=['T5'])
        fw.op('dve', lambda e: e.tensor_tensor(out=KKt[:], in0=KKt[:], in1=T[5][:], op=ALU.mult), reads=[kkk, 'T5'], writes=[kkk])

    def prep_dir(d):
        dr = slice(d * 64, d * 64 + 64)
        TL2, TB, TK, WC = TL2s[d], TBs[d], TKs[d], WCs[d]
        lb = LazyBank(K)
        fw.op('pe', lambda t: t.matmul(lb.ps[:, 0:512], W2b[dr, p * 128:(p + 1) * 128], TXW[dr, hs], start=True, stop=True), reads=['W2b', 'TXW'], writes=[lb])
        fw.op('act', lambda e: e.activation(out=SG[:], in_=lb.ps[:, 0:512], func=AF.Sigmoid, bias=PV[:, PV_W0 + d * 4 + p:PV_W0 + d * 4 + p + 1], scale=1.0),
              reads=[lb, 'PV'], writes=['T0'])
        lb2 = LazyBank(K)
        fw.op('pe', lambda t: t.matmul(lb2.ps[:, 0:512], A2b[dr, p * 128:(p + 1) * 128], UXA[dr, hs], start=True, stop=True), reads=['A2b', 'UXA'], writes=[lb2])
        fw.op('act', lambda e: e.activation(out=AT[:], in_=lb2.ps[:, 0:512], func=AF.Sigmoid, bias=PV[:, PV_A0 + d * 4 + p:PV_A0 + d * 4 + p + 1], scale=1.0),
              reads=[lb2, 'PV'], writes=['T3'])
        fw.op('dve', lambda e: e.tensor_tensor_scan(out=LC[:], data0=rst[:], data1=SG[:], initial=0.0, op0=ALU.mult, op1=ALU.add),
              reads=['rst', 'T0'], writes=['T1'])
        if d == 0:
            fw.op('dve', lambda e: e.tensor_tensor(out=LX[:], in0=LC[:], in1=SG[:], op=ALU.subtract), reads=['T1', 'T0'], writes=['T2'])
            LI, lik = LC, 'T1'
        else:
            for c in range(4):
                cs = slice(c * 128, (c + 1) * 128)
                fw.op('dve', lambda e, cs=cs, c=c: e.tensor_scalar(out=LX[:, cs], in0=LC[:, cs], scalar1=-1.0, scalar2=LC[:, c * 128 + 127:c * 128 + 128],
                                                                   op0=ALU.mult, op1=ALU.add), reads=['T1'], writes=['T2'])
            fw.op('dve', lambda e: e.tensor_tensor(out=SG[:], in0=LX[:], in1=SG[:], op=ALU.add), reads=['T2', 'T0'], writes=['T0'])
            LI, lik = SG, 'T0'
        fw.op('act', lambda e: e.activation(out=TE[:], in_=LI[:], func=AF.Exp, scale=-C0), reads=[lik], writes=['T4'])
        wcol = 127 if d == 0 else 0
        fw.op('dve', lambda e: e.tensor_copy(out=WC[:], in_=TE[:, wcol:512:128]), reads=['T4'], writes=[kn('WC', d)])
        fw.op('dve', lambda e: e.tensor_tensor(out=TL2[:, :, 1, :], in0=Ur[:, hs].rearrange("p (c t) -> p c t", c=4),
                                               in1=TE[:].rearrange("p (c t) -> p c t", c=4), op=ALU.mult), reads=[uk['Ur'], 'T4'], writes=[kn('TL2', d)])
        fw.op('act', lambda e: e.activation(out=TE[:], in_=LX[:], func=AF.Exp, scale=-C0), reads=['T2'], writes=['T4'])
        fw.op('dve', lambda e: e.tensor_tensor(out=TL2[:, :, 0, :], in0=KKt[:].rearrange("p (c t) -> p c t", c=4),
                                               in1=TE[:].rearrange("p (c t) -> p c t", c=4), op=ALU.mult), reads=[kkk, 'T4'], writes=[kn('TL2', d)])
        fw.op('act', lambda e: e.activation(out=TE[:], in_=LI[:], func=AF.Exp, scale=C0), reads=[lik], writes=['T4'])
        fw.op('dve', lambda e: e.tensor_tensor(out=T2[:], in0=KKt[:], in1=AT[:], op=ALU.mult), reads=[kkk, 'T3'], writes=['T5'])
        fw.op('dve', lambda e: e.tensor_tensor(out=TB[:], in0=T2[:], in1=TE[:], op=ALU.mult), reads=['T5', 'T4'], writes=[kn('TB', d)])
        fw.op('dve', lambda e: e.tensor_scalar(out=AT[:], in0=AT[:], scalar1=PV[:, PV_KA + p:PV_KA + p + 1], scalar2=OMK[:, p:p + 1], op0=ALU.mult, op1=ALU.add),
              reads=['T3', 'PV', 'OMK'], writes=['T3'])
        fw.op('dve', lambda e: e.tensor_tensor(out=T2[:], in0=Ukr[:, hs], in1=AT[:], op=ALU.mult), reads=[uk['Ukr'], 'T3'], writes=['T5'])
        fw.op('dve', lambda e: e.tensor_tensor(out=TK[:], in0=T2[:], in1=TE[:], op=ALU.mult), reads=['T5', 'T4'], writes=[kn('TK', d)])
        if d == 0:
            fw.op('dve', lambda e: e.scalar_tensor_tensor(out=RKK[:, hs], in0=Ur[:, hs], scalar=PV[:, PV_RK + p:PV_RK + p + 1], in1=T2[:], op0=ALU.mult, op1=ALU.mult),
                  reads=[uk['Ur'], 'PV', 'T5'], writes=['RKK'])
        else:
            fw.op('dve', lambda e: e.scalar_tensor_tensor(out=T2[:], in0=Ur[:, hs], scalar=PV[:, PV_RK + p:PV_RK + p + 1], in1=T2[:], op0=ALU.mult, op1=ALU.mult),
                  reads=[uk['Ur'], 'PV', 'T5'], writes=['T5'])
            fw.op('dve', lambda e: e.tensor_tensor(out=RKK[:, hs], in0=RKK[:, hs], in1=T2[:], op=ALU.add), reads=['RKK', 'T5'], writes=['RKK'])

    prep_kk()
    for d in range(2):
        prep_dir(d)
    ops = fw.capture
    fw.capture = None
    return ops


def rwkv_prepB(K, L):
    fw = K.fw
    PV = K.PV
    p, half, hs = L['p'], L['half'], L['hs']
    uk = L['ukeys']
    T, Ur, Ukr, Uv, RKK = L['T'], L['Ur'], L['Ukr'], L['Uv'], L['RKK']
    TXW, UXA, W2b, A2b, rst, OMK = L['TXW'], L['UXA'], L['W2b'], L['A2b'], L['rst'], L['OMK']
    par = L['par']
    PS = L['PSET'][par]
    TL2s, TBs, TKs, TBKs, WCs = PS['TL2'], PS['TB'], PS['TK'], PS['TBK'], PS['WC']
    AQs, NNs, BBs, PPs = L['AQ'], L['NN'], L['BB'], L['PP']
    ST32s, STbs, STWs, Xbs, Ubs = L['ST32'], L['STb'], L['STW'], L['Xb'], L['Ub']
    Vaug, YP, yT, CC, mqb, mbb, xi_f, stS = (L[k] for k in ('Vaug', 'YP', 'yT', 'CC', 'mqb', 'mbb', 'xi_f', 'stS'))
    O = K.O
    SG, LC, LX, AT, TE, T2 = T[0], T[1], T[2], T[3], T[4], T[5]
    kn = lambda base, d: ('%s_%d_%d' % (base, d, par)) if base in ('TL2', 'TB', 'TK', 'TBK', 'WC') else ('%s_%d' % (base, d))
    for d in range(2):
        TB, TK = TBs[d], TKs[d]
        ps, pk = bank(K)
        psb = ps[:].bitcast(BF16)
        for c in range(4):
            for a, (src, skey) in enumerate(((TB, kn('TB', d)), (TK, kn('TK', d)))):
                fw.op('pe', lambda t, c=c, a=a, src=src: t.transpose(psb[:, (c * 2 + a) * 128:(c * 2 + a + 1) * 128], src[:, c * 128:(c + 1) * 128], K.ident_b[:]),
                      reads=[skey, 'ident_b'], writes=[pk])
        evac(K, 'act', TBKs[d][:].rearrange("p c a t -> p (c a t)"), psb[:, 0:1024], [pk], [kn('TBK', d)])


def rwkv_core(K, L):
    fw = K.fw
    PV = K.PV
    p, half, hs = L['p'], L['half'], L['hs']
    uk = L['ukeys']
    T, Ur, Ukr, Uv, RKK = L['T'], L['Ur'], L['Ukr'], L['Uv'], L['RKK']
    TXW, UXA, W2b, A2b, rst, OMK = L['TXW'], L['UXA'], L['W2b'], L['A2b'], L['rst'], L['OMK']
    par = L['par']
    PS = L['PSET'][par]
    TL2s, TBs, TKs, TBKs, WCs = PS['TL2'], PS['TB'], PS['TK'], PS['TBK'], PS['WC']
    AQs, NNs, BBs, PPs = L['AQ'], L['NN'], L['BB'], L['PP']
    ST32s, STbs, STWs, Xbs, Ubs = L['ST32'], L['STb'], L['STW'], L['Xb'], L['Ub']
    Vaug, YP, yT, CC, mqb, mbb, xi_f, stS = (L[k] for k in ('Vaug', 'YP', 'yT', 'CC', 'mqb', 'mbb', 'xi_f', 'stS'))
    O = K.O
    SG, LC, LX, AT, TE, T2 = T[0], T[1], T[2], T[3], T[4], T[5]
    kn = lambda base, d: ('%s_%d_%d' % (base, d, par)) if base in ('TL2', 'TB', 'TK', 'TBK', 'WC') else ('%s_%d' % (base, d))
    rwkv_prepB(K, L)
    L['vaug'](half)
    if L.get('hook_prep'):
        L['hook_prep']()
    for d in range(2):
        TL2, TB, TK, AQ, NN, BB = TL2s[d], TBs[d], TKs[d], AQs[d], NNs[d], BBs[d]
        tl = TL2[:].rearrange("p c a t -> p c (a t)")
        for c in range(4):
            for h2 in range(2):
                u = c * 2 + h2
                pr = slice(h2 * 64, h2 * 64 + 64)
                ps, pk = bank(K)
                fw.op('pe', lambda t, ps=ps, c=c, pr=pr: t.matmul(ps[:, 0:256], TB[pr, c * 128:(c + 1) * 128], tl[pr, c, :], start=True, stop=True),
                      reads=[kn('TB', d), kn('TL2', d)], writes=[pk])
                fw.op('pe', lambda t, ps=ps, c=c, pr=pr: t.matmul(ps[:, 256:512], TK[pr, c * 128:(c + 1) * 128], tl[pr, c, :], start=True, stop=True),
                      reads=[kn('TK', d), kn('TL2', d)], writes=[pk])
                fw.op('dve', lambda e, ps=ps, u=u: e.tensor_tensor(out=NN[:, u, :], in0=ps[:, 0:128], in1=mqb[:, d, 0:128], op=ALU.mult),
                      reads=[pk, 'mqb'], writes=[(kn('NN', d), u // 4)])
                fw.op('act', lambda e, ps=ps, u=u: e.activation(out=AQ[:, u, :, :].rearrange("p a t -> p (a t)"), in_=ps[:, 128:512], func=AF.Copy),
                      reads=[pk], writes=[(kn('AQ', d), u)])
                fw.op('pool', lambda e, u=u: e.tensor_tensor(out=AQ[:, u, :, :].rearrange("p a t -> p (a t)"), in0=AQ[:, u, :, :].rearrange("p a t -> p (a t)"),
                                                             in1=mqb[:, d, 128:512], op=ALU.mult),
                      reads=[(kn('AQ', d), u), 'mqb'], writes=[(kn('AQ', d), u)])
        for h2 in range(2):
            pr = slice(h2 * 64, h2 * 64 + 64)
            ps, pk = bank(K)
            for c in range(4):
                fw.op('pe', lambda t, ps=ps, c=c, pr=pr: t.matmul(ps[:, c * 128:(c + 1) * 128], TL2[pr, c, 0, :], TB[pr, c * 128:(c + 1) * 128], start=True, stop=True),
                      reads=[kn('TB', d), kn('TL2', d)], writes=[pk])
            fw.op('dve', lambda e, ps=ps, h2=h2: e.tensor_tensor(out=BB[:, h2:8:2, :], in0=ps[:, 0:512].rearrange("p (c t) -> p c t", c=4),
                                                                in1=mbb[:, d:d + 1, :].broadcast_to([128, 4, 128]), op=ALU.mult),
                  reads=[pk, 'mbb'], writes=[(kn('BB', d), 0), (kn('BB', d), 1)])
    if L.get('hook_quad'):
        L['hook_quad']()
    for d in range(2):
        for g in range(2):
            gs = slice(g * 4, g * 4 + 4)
            fw.op('dve', lambda e, gs=gs, d=d: e.tensor_tensor(out=PPs[d][:, gs, :], in0=NNs[d][:, gs, :], in1=K.ident_b[:].unsqueeze(1).broadcast_to([128, 4, 128]), op=ALU.add),
                  reads=[(kn('NN', d), g), 'ident_b'], writes=[(kn('PP', d), g)])
    NPs = L['NP']
    for lvl in range(1, 8):
        for d in range(2):
            NN, BB, PP, NPt = NNs[d], BBs[d], PPs[d], NPs[d]
            for g in range(2):
                nk_, bk_, pk_ = (kn('NN', d), g), (kn('BB', d), g), (kn('PP', d), g)
                gs = slice(g * 4, g * 4 + 4)
                if lvl == 1:
                    psn, pnk = bank(K)
                    for j in range(4):
                        u = g * 4 + j
                        fw.op('pe', lambda t, psn=psn, j=j, u=u: t.matmul(psn[:, j * 128:(j + 1) * 128], BB[:, u, :], NN[:, u, :], start=True, stop=True),
                              reads=[bk_, nk_], writes=[pnk])
                    psb_, pbk = bank(K)
                    for j in range(4):
                        u = g * 4 + j
                        fw.op('pe', lambda t, psb_=psb_, j=j, u=u: t.matmul(psb_[:, j * 128:(j + 1) * 128], NN[:, u, :], BB[:, u, :], start=True, stop=True),
                              reads=[bk_, nk_], writes=[pbk])
                    evac(K, 'act', NN[:, gs, :], psn[:, 0:512].rearrange("p (u t) -> p u t", u=4), [pnk], [nk_])
                    evac(K, 'act', BB[:, gs, :].rearrange("p u t -> p (u t)"), psb_[:, 0:512], [pbk], [bk_])
                elif lvl <= 5:
                    pm = [bank(K), bank(K)]
                    for j in range(4):
                        u = g * 4 + j
                        psm, pmk = pm[j // 2]
                        jj = j % 2
                        fw.op('pe', lambda t, psm=psm, jj=jj, u=u: t.matmul(psm[:, jj * 256:jj * 256 + 256], BB[:, u, :], NPt[:, u, :, :].rearrange("p a t -> p (a t)"),
                                                                            start=True, stop=True), reads=[bk_, nk_, pk_], writes=[pmk])
                    psb_, pbk = bank(K)
                    for j in range(4):
                        u = g * 4 + j
                        fw.op('pe', lambda t, psb_=psb_, j=j, u=u: t.matmul(psb_[:, j * 128:(j + 1) * 128], NN[:, u, :], BB[:, u, :], start=True, stop=True),
                              reads=[bk_, nk_], writes=[pbk])
                    for hb_ in range(2):
                        psm, pmk = pm[hb_]
                        u0 = g * 4 + hb_ * 2
                        v3 = psm[:, 0:512].rearrange("p (j c) -> p j c", j=2)
                        evac(K, 'act', NN[:, u0:u0 + 2, :], v3[:, :, 0:128], [pmk], [nk_])
                        fw.op('dve', lambda e, v3=v3, u0=u0: e.tensor_tensor(out=PP[:, u0:u0 + 2, :], in0=v3[:, :, 128:256], in1=PP[:, u0:u0 + 2, :], op=ALU.add),
                              reads=[pmk, pk_], writes=[pk_])
                    evac(K, 'dve' if (g + d + lvl) % 2 == 0 else 'act', BB[:, gs, :].rearrange("p u t -> p (u t)"), psb_[:, 0:512], [pbk], [bk_])
                else:
                    psp, ppk = bank(K)
                    for j in range(4):
                        u = g * 4 + j
                        fw.op('pe', lambda t, psp=psp, j=j, u=u: t.matmul(psp[:, j * 128:(j + 1) * 128], BB[:, u, :], PP[:, u, :], start=True, stop=True),
                              reads=[bk_, pk_], writes=[ppk])
                    if lvl == 6:
                        psb_, pbk = bank(K)
                        for j in range(4):
                            u = g * 4 + j
                            fw.op('pe', lambda t, psb_=psb_, j=j, u=u: t.matmul(psb_[:, j * 128:(j + 1) * 128], NN[:, u, :], BB[:, u, :], start=True, stop=True),
                                  reads=[bk_, nk_], writes=[pbk])
                    fw.op('dve', lambda e, psp=psp: e.tensor_tensor(out=PP[:, gs, :], in0=psp[:, 0:512].rearrange("p (u t) -> p u t", u=4),
                                                                   in1=PP[:, gs, :], op=ALU.add), reads=[ppk, pk_], writes=[pk_])
                    if lvl == 6:
                        evac(K, 'act', BB[:, gs, :].rearrange("p u t -> p (u t)"), psb_[:, 0:512], [pbk], [bk_])
    filler = L['hook_mid']()
    if getattr(K, 'no_interleave', False):
        fw.replay(filler, len(filler))
    if half == 0:
        base = [[0, 1], [2, 3]]
    else:
        base = [[0, 1, 2, 3]]
    seqs_d = [base, [list(reversed(x)) for x in base]]
    ns = len(base)
    nsteps = len(base[0])
    nfill = -(-len(filler) // (4 * nsteps))
    for d in range(2):
        for si in range(ns):
            fw.op('dve', lambda e, si=si, d=d: e.tensor_copy(out=ST32s[d][:, si, :], in_=xi_f[:]), reads=['xi_f'], writes=[kn('ST32', d)])
            fw.op('act', lambda e, si=si, d=d: e.activation(out=STbs[d][:, si, :], in_=xi_f[:], func=AF.Copy), reads=['xi_f'], writes=[kn('STb', d)])
    for step in range(nsteps):
        XB = {}
        for d in range(2):
            TL2, AQ, STb = TL2s[d], AQs[d], STbs[d]
            AQk = [(kn('AQ', d), u) for u in range(8)]
            XB[d] = [bank(K), bank(K)]
            for si in range(ns):
                c = seqs_d[d][si][step]
                for h2 in range(2):
                    pr = slice(h2 * 64, h2 * 64 + 64)
                    u = c * 2 + h2
                    psx, pxk = XB[d][h2]
                    fw.op('pe', lambda t, psx=psx, si=si, c=c, pr=pr: t.matmul(psx[:, si * 128:(si + 1) * 128], TL2[pr, c, 0, :], STb[pr, si, :], start=True, stop=False),
                          reads=[kn('TL2', d), kn('STb', d)], writes=[pxk])
                    fw.op('pe', lambda t, psx=psx, si=si, c=c, u=u, h2=h2: t.matmul(psx[:, si * 128:(si + 1) * 128], AQ[:, u, 1, :], Vaug[:, c, h2, :], start=False, stop=True),
                          reads=AQk + ['Vaug'], writes=[pxk])
        for d in range(2):
            for h2 in range(2):
                psx, pxk = XB[d][h2]
                fw.op('act' if d == 0 else 'dve',
                      (lambda e, psx=psx, h2=h2, d=d: e.activation(out=Xbs[d][:, h2 * ns:(h2 + 1) * ns, :].rearrange("p s t -> p (s t)"), in_=psx[:, 0:ns * 128],
                                                                   func=AF.Identity, scale=-1.0)) if d == 0 else
                      (lambda e, psx=psx, h2=h2, d=d: e.tensor_scalar(out=Xbs[d][:, h2 * ns:(h2 + 1) * ns, :].rearrange("p s t -> p (s t)"), in0=psx[:, 0:ns * 128],
                                                                      scalar1=-1.0, scalar2=None, op0=ALU.mult)),
                      reads=[pxk], writes=[kn('Xb', d)])
        fw.replay(filler, nfill)
        UB = {}
        for d in range(2):
            PP, Xb = PPs[d], Xbs[d]
            PPk = [(kn('PP', d), 0), (kn('PP', d), 1)]
            UB[d] = bank(K)
            psu, puk = UB[d]
            for si in range(ns):
                c = seqs_d[d][si][step]
                for h2 in range(2):
                    u = c * 2 + h2
                    q = h2 * ns + si
                    fw.op('pe', lambda t, q=q, u=u, psu=psu: t.matmul(psu[:, q * 128:(q + 1) * 128], PP[:, u, :], Xb[:, q, :], start=True, stop=True),
                          reads=PPk + [kn('Xb', d)], writes=[puk])
        for d in range(2):
            psu, puk = UB[d]
            evac(K, 'dve' if d == 0 else 'act', Ubs[d][:, 0:2 * ns, :].rearrange("p s t -> p (s t)"), psu[:, 0:2 * ns * 128], [puk], [kn('Ub', d)])
        fw.replay(filler, nfill)
        YB = {}; MB_ = {}
        for d in range(2):
            TL2, AQ, STb, Ub, TBK = TL2s[d], AQs[d], STbs[d], Ubs[d], TBKs[d]
            AQk = [(kn('AQ', d), u) for u in range(8)]
            YB[d] = [bank(K), bank(K)]
            for si in range(ns):
                c = seqs_d[d][si][step]
                for h2 in range(2):
                    pr = slice(h2 * 64, h2 * 64 + 64)
                    u = c * 2 + h2
                    q = h2 * ns + si
                    psy, pyk = YB[d][h2]
                    fw.op('pe', lambda t, psy=psy, si=si, c=c, pr=pr: t.matmul(psy[:, si * 128:(si + 1) * 128], STb[pr, si, :], TL2[pr, c, 1, :], start=True, stop=False),
                          reads=[kn('TL2', d), kn('STb', d)], writes=[pyk])
                    fw.op('pe', lambda t, psy=psy, si=si, q=q, u=u: t.matmul(psy[:, si * 128:(si + 1) * 128], Ub[:, q, :], AQ[:, u, 0, :], start=False, stop=False),
                          reads=AQk + [kn('Ub', d)], writes=[pyk])
                    fw.op('pe', lambda t, psy=psy, si=si, c=c, u=u, h2=h2: t.matmul(psy[:, si * 128:(si + 1) * 128], Vaug[:, c, h2, :], AQ[:, u, 2, :], start=False, stop=True),
                          reads=AQk + ['Vaug'], writes=[pyk])
            for si in range(ns):
                c = seqs_d[d][si][step]
                fw.op('dve', lambda e, si=si, c=c, d=d: e.tensor_scalar(out=STWs[d][:, si, :], in0=ST32s[d][:, si, :], scalar1=WCs[d][:, c:c + 1], scalar2=None, op0=ALU.mult),
                      reads=[kn('ST32', d), kn('WC', d)], writes=[kn('STW', d)])
            MB_[d] = bank(K)
            psm, pmk = MB_[d]
            for si in range(ns):
                c = seqs_d[d][si][step]
                for h2 in range(2):
                    pr = slice(h2 * 64, h2 * 64 + 64)
                    q = h2 * ns + si
                    fw.op('pe', lambda t, si=si, c=c, q=q, pr=pr, h2=h2, psm=psm: t.matmul(psm[pr, si * 128:(si + 1) * 128], TBK[:, c, 0, h2 * 64:(h2 + 1) * 64], Ub[:, q, :], start=True, stop=False),
                          reads=[kn('TBK', d), kn('Ub', d)], writes=[pmk])
                    fw.op('pe', lambda t, si=si, c=c, pr=pr, h2=h2, psm=psm: t.matmul(psm[pr, si * 128:(si + 1) * 128], TBK[:, c, 1, h2 * 64:(h2 + 1) * 64], Vaug[:, c, h2, :], start=False, stop=True),
                          reads=[kn('TBK', d), 'Vaug'], writes=[pmk])
        for d in range(2):
            for si in range(ns):
                c = seqs_d[d][si][step]
                tok = half * 512 + c * 128
                for h2 in range(2):
                    pr = slice(h2 * 64, h2 * 64 + 64)
                    po = slice((1 - h2) * 64, (1 - h2) * 64 + 64)
                    psy, pyk = YB[d][h2]
                    li = base[si].index(c)
                    first = (d == 0) == (li < len(base[si]) / 2)
                    if first:
                        fw.op('dve', lambda e, psy=psy, si=si, pr=pr, tok=tok: e.tensor_copy(out=yT[pr, p, tok:tok + 128], in_=psy[pr, si * 128:(si + 1) * 128]),
                              reads=[pyk], writes=[('yT%d' % p, tok)])
                    else:
                        fw.op('dve', lambda e, psy=psy, si=si, pr=pr, tok=tok: e.tensor_tensor(out=yT[pr, p, tok:tok + 128], in0=psy[pr, si * 128:(si + 1) * 128],
                                                                                              in1=yT[pr, p, tok:tok + 128], op=ALU.add),
                              reads=[pyk, ('yT%d' % p, tok)], writes=[('yT%d' % p, tok)])
                    if half == 1:
                        fw.op('act', lambda e, psy=psy, si=si, po=po, c=c, d=d: e.activation(out=YP[po, p, d, c, :], in_=psy[po, si * 128:(si + 1) * 128], func=AF.Copy),
                              reads=[pyk], writes=['YP'])
            psm, pmk = MB_[d]
            for si in range(ns):
                c = seqs_d[d][si][step]
                fw.op('dve', lambda e, si=si, c=c, d=d, psm=psm: e.scalar_tensor_tensor(out=ST32s[d][:, si, :], in0=psm[:, si * 128:(si + 1) * 128], scalar=WCs[d][:, c:c + 1],
                                                                                     in1=STWs[d][:, si, :], op0=ALU.mult, op1=ALU.add),
                      reads=[pmk, kn('WC', d), kn('STW', d)], writes=[kn('ST32', d)])
                fw.op('act', lambda e, si=si, d=d: e.activation(out=STbs[d][:, si, :], in_=ST32s[d][:, si, :], func=AF.Copy), reads=[kn('ST32', d)], writes=[kn('STb', d)])
        fw.replay(filler, 2 * nfill)
    fw.replay(filler, len(filler))
    for d in range(2):
        ST32 = ST32s[d]
        if half == 0:
            for si in range(ns):
                ps, pk = bank(K)
                fw.op('pe', lambda t, ps=ps, si=si: t.transpose(ps[:, 0:128], ST32[:, si, :], K.ident_f[:]), reads=[kn('ST32', d), 'ident_f'], writes=[pk])
                evac(K, 'dve', stS[:], ps[:, 0:128], [pk], ['stS'])
                for h2 in range(2):
                    pr = slice(h2 * 64, h2 * 64 + 64)
                    fw.dma('sp', O.ns[si, d, 2 * p + h2, :, :], stS[pr, h2 * 64:(h2 + 1) * 64], reads=['stS'])
        else:
            fw.op('dve', lambda e: e.tensor_copy(out=CC[:, p, d, :], in_=ST32[:, 0, :]), reads=[kn('ST32', d)], writes=['CC'])


def rwkv_finish(K, L):
    nc, fw, I, sb = K.nc, K.fw, K.I, K.sb
    PV = K.PV
    yT, YP, SXG, CC, S0T, G2b, bd1_f, xi_f = (L[k] for k in ('yT', 'YP', 'SXG', 'CC', 'S0T', 'G2b', 'bd1_f', 'xi_f'))
    mixT = K.mixT
    cc_out = L['cc_out']
    with Scope(K):
        CG = sb("CG", [128, 4, 8, 128])
        s0t = sb("s0t", [64, 8, 128])
        for p in range(4):
            for d in range(2):
                pd = p * 2 + d
                fw.dma('sp', s0t[:, pd, :].rearrange("v (h k) -> v h k", h=2), I.s0[d, 2 * p:2 * p + 2].rearrange("h v k -> v h k"), writes=[('s0t', pd)])
        for p in range(4):
            for d in range(2):
                pd = p * 2 + d
                ps, pk = bank(K)
                fw.op('pe', lambda t, ps=ps, pd=pd: t.transpose(ps[:, 0:64], s0t[:, pd, :], K.ident_f[0:64, 0:64]), reads=[('s0t', pd), 'ident_f'], writes=[pk])
                evac(K, 'dve' if pd % 2 == 0 else 'act', S0T[:, p, d, :], ps[:, 0:64], [pk], ['S0T'])
        SWt = sb("SWt", [128, 8, 128])
        XS = sb("XS", [128, 4, 8, 64])
        MIN = sb("MIN", [128, 8, 64]); MINb = sb("MINb", [128, 8, 64], BF16); MINsw = sb("MINsw", [128, 8, 64], BF16)
        xi_b = sb("xi_b", [128, 128], BF16)
        Wall = [[sb("W%d_%d" % (i, q), [128, 512]) for i in range(3)] for q in range(4)]
        fw.dma('pool', xi_b[:], I.xi[:, :], writes=['xi_b'])
        fw.dma('pool', CG[:].rearrange("p r a c -> p r (a c)"), cc_out.ap().rearrange("(r p) f -> p r f", p=128), reads=['cc_out'], writes=['CG'])
        fw.op('dve', lambda e: e.tensor_copy(out=XS[:, 0, 0:8:2, :], in_=S0T[:, :, 0, :]), reads=['S0T'], writes=['XS'])
        fw.op('dve', lambda e: e.tensor_copy(out=XS[:, 3, 1:8:2, :], in_=S0T[:, :, 1, :]), reads=['S0T'], writes=['XS'])
        for i in range(3):
            rank = (i, 3 - i)
            cur = (i, 3 - i)
            nxt = (i + 1, 2 - i)
            sw = [bank(K), bank(K)]
            for pd in range(8):
                d = pd % 2
                ps, pk = sw[pd // 4]
                for hh in range(2):
                    fw.op('pe', lambda t, ps=ps, hh=hh, pd=pd, d=d: t.matmul(ps[hh * 64:(hh + 1) * 64, (pd % 4) * 128:(pd % 4 + 1) * 128],
                                                                           CG[:, rank[d], pd, (1 - hh) * 64:(2 - hh) * 64], K.ident_f[:], start=True, stop=True),
                          reads=['CG', 'ident_f'], writes=[pk])
            for b_ in range(2):
                ps, pk = sw[b_]
                evac(K, 'act' if b_ == 0 else 'dve', SWt[:, b_ * 4:(b_ + 1) * 4, :].rearrange("p a b -> p (a b)"), ps[:, 0:512], [pk], ['SWt'])
            cb = [bank(K), bank(K)]
            for pd in range(8):
                d = pd % 2
                for h2 in range(2):
                    pr = slice(h2 * 64, h2 * 64 + 64)
                    ps2, pk2 = cb[h2]
                    fw.op('pe', lambda t, ps2=ps2, pr=pr, h2=h2, pd=pd, d=d: t.matmul(ps2[pr, pd * 64:(pd + 1) * 64], SWt[pr, pd, h2 * 64:(h2 + 1) * 64], XS[pr, cur[d], pd, :],
                                                                                   start=True, stop=True), reads=['SWt', 'XS'], writes=[pk2])
            for h2 in range(2):
                pr = slice(h2 * 64, h2 * 64 + 64)
                ps2, pk2 = cb[h2]
                for d in range(2):
                    fw.op('dve', lambda e, ps2=ps2, pr=pr, h2=h2, d=d: e.tensor_tensor(
                        out=XS[pr, nxt[d], d:8:2, :], in0=ps2[pr, 0:512].rearrange("p (a b) -> p a b", a=8)[:, d:8:2, :],
                        in1=CG[pr, rank[d], d:8:2, h2 * 64:(h2 + 1) * 64], op=ALU.add), reads=[pk2, 'CG'], writes=['XS'])
        fw.op('dve', lambda e: e.tensor_scalar(out=MIN[:].rearrange("p a b -> p (a b)"), in0=XS[:, 0, :, :].rearrange("p a b -> p (a b)"), scalar1=K.cst[:, 0:1], scalar2=None, op0=ALU.mult),
              reads=['XS', 'cst'], writes=['MIN'])
        for j in range(1, 4):
            fw.op('dve', lambda e, j=j: e.scalar_tensor_tensor(out=MIN[:].rearrange("p a b -> p (a b)"), in0=XS[:, j, :, :].rearrange("p a b -> p (a b)"), scalar=K.cst[:, j:j + 1],
                                                                in1=MIN[:].rearrange("p a b -> p (a b)"), op0=ALU.mult, op1=ALU.add),
                  reads=['XS', 'cst', 'MIN'], writes=['MIN'])
        dump(K, "MIN", MIN[:], [128, 8, 64], ['MIN'])
        fw.op('act', lambda e: e.activation(out=MINb[:], in_=MIN[:], func=AF.Copy), reads=['MIN'], writes=['MINb'])
        ps, pk = bank(K)
        fw.op('pe', lambda t: t.matmul(ps[:, 0:512], xi_b[:], MINb[:].rearrange("p a b -> p (a b)"), start=True, stop=True), reads=['xi_b', 'MINb'], writes=[pk])
        evac(K, 'dve', MINsw[:].rearrange("p a b -> p (a b)"), ps[:, 0:512], [pk], ['MINsw'])
        for p in range(4):
            for d in range(2):
                pd = p * 2 + d
                for h2 in range(2):
                    pr = slice(h2 * 64, h2 * 64 + 64)
                    po = slice((1 - h2) * 64, (1 - h2) * 64 + 64)
                    psc, pck = bank(K)
                    fw.op('pe', lambda t, psc=psc, pr=pr, po=po: t.matmul(psc[pr, 0:512], MINsw[po, pd, :], YP[po, p, d, :, :].rearrange("p c t -> p (c t)"), start=True, stop=True),
                          reads=['MINsw', 'YP'], writes=[pck])
                    fw.op('dve', lambda e, psc=psc, pr=pr: e.tensor_tensor(out=yT[pr, p, 512:1024], in0=psc[pr, 0:512], in1=yT[pr, p, 512:1024], op=ALU.add),
                          reads=[pck] + [('yT%d' % p, 512 + c_ * 128) for c_ in range(4)], writes=[('yT%d' % p, 512 + c_ * 128) for c_ in range(4)])
        dump(K, "yT", yT[:], [128, 4, 1024], [('yT%d' % p, t_) for p in range(4) for t_ in range(0, 1024, 128)])
        for batch in range(2):
            units = [(batch * 2 + (q // 2), q % 2, q) for q in range(4)]
            ykeys = lambda p, hb: [('yT%d' % p, hb * 512 + c_ * 128) for c_ in range(4)]
            hsl = lambda hb: slice(hb * 512, hb * 512 + 512)
            b1 = {}; b2 = {}; b3 = {}
            for (p, hb, q) in units:
                b1[q] = bank(K)
                fw.op('pe', lambda t, p=p, hb=hb, q=q: t.matmul(b1[q][0][:, 0:512], bd1_f[:], yT[:, p, hsl(hb)], start=True, stop=True),
                      reads=['bd1_f'] + ykeys(p, hb), writes=[b1[q][1]])
            for (p, hb, q) in units:
                W = Wall[q]
                fw.op('dve', lambda e, p=p, hb=hb, q=q, W=W: e.scalar_tensor_tensor(out=W[0][:], in0=b1[q][0][:, 0:512], scalar=-1.0 / 64, in1=yT[:, p, hsl(hb)],
                                                                                   op0=ALU.mult, op1=ALU.add), reads=[b1[q][1]] + ykeys(p, hb), writes=['W0_%d' % q])
            for (p, hb, q) in units:
                W = Wall[q]
                fw.op('act', lambda e, W=W: e.activation(out=W[1][:], in_=W[0][:], func=AF.Square), reads=['W0_%d' % q], writes=['W1_%d' % q])
            for (p, hb, q) in units:
                W = Wall[q]
                b2[q] = bank(K)
                fw.op('pe', lambda t, q=q, W=W: t.matmul(b2[q][0][:, 0:512], bd1_f[:], W[1][:], start=True, stop=True), reads=['bd1_f', 'W1_%d' % q], writes=[b2[q][1]])
            for (p, hb, q) in units:
                b3[q] = bank(K)
                fw.op('pe', lambda t, p=p, hb=hb, q=q: t.matmul(b3[q][0][:, 0:512], G2b[:, p * 128:(p + 1) * 128], SXG[:, hsl(hb)], start=True, stop=True),
                      reads=['G2b', 'SXG'], writes=[b3[q][1]])
            for (p, hb, q) in units:
                W = Wall[q]
                fw.op('act', lambda e, q=q, W=W: e.activation(out=W[1][:], in_=b2[q][0][:, 0:512], func=AF.Ln, bias=K.epsc[:, 1:2], scale=1.0 / 64),
                      reads=[b2[q][1], 'epsc'], writes=['W1_%d' % q])
            for (p, hb, q) in units:
                W = Wall[q]
                fw.op('act', lambda e, W=W: e.activation(out=W[1][:], in_=W[1][:], func=AF.Exp, scale=-0.5), reads=['W1_%d' % q], writes=['W1_%d' % q])
            for (p, hb, q) in units:
                W = Wall[q]
                fw.op('dve', lambda e, W=W: e.tensor_tensor(out=W[0][:], in0=W[0][:], in1=W[1][:], op=ALU.mult), reads=['W0_%d' % q, 'W1_%d' % q], writes=['W0_%d' % q])
            for (p, hb, q) in units:
                W = Wall[q]
                fw.op('act', lambda e, p=p, W=W: e.activation(out=W[2][:], in_=W[0][:], func=AF.Identity, bias=PV[:, PV_LNB + p:PV_LNB + p + 1], scale=PV[:, PV_LNG + p:PV_LNG + p + 1]),
                      reads=['W0_%d' % q, 'PV'], writes=['W2_%d' % q])
            for (p, hb, q) in units:
                W = Wall[q]
                fw.op('dve', lambda e, p=p, hb=hb, W=W: e.tensor_tensor(out=W[2][:], in0=W[2][:], in1=mixT[:, 4 + p, hsl(hb)], op=ALU.add),
                      reads=['W2_%d' % q, ('mixT', 4 + p)], writes=['W2_%d' % q])
            for (p, hb, q) in units:
                W = Wall[q]
                fw.op('dve', lambda e, p=p, hb=hb, q=q, W=W: e.tensor_tensor(out=mixT[:, 4 + p, hsl(hb)], in0=b3[q][0][:, 0:512], in1=W[2][:], op=ALU.mult),
                      reads=[b3[q][1], 'W2_%d' % q], writes=[('mixT', 4 + p)])
        if 'mixR' in K.dbg:
            md = sb("mixdbgR", [128, 4, 1024])
            fw.op('dve', lambda e: e.tensor_copy(out=md[:], in_=mixT[:, 4:8, :]), reads=[('mixT', 4 + p) for p in range(4)], writes=['mixdbgR'])
            dump(K, "mixR", md[:], [128, 4, 1024], ['mixdbgR'])


def emit_back(K, stage):
    nc, fw, I, O, sb = K.nc, K.fw, K.I, K.O, K.sb
    xT, mixT, PV, MOD, AB = K.xT, K.mixT, K.PV, K.MOD, K.AB
    xkeys = K.xkeys
    xk = lambda mo, hb: ('xT', hb * 512)
    XK = [('xT', c) for c in range(0, 1024, 128)]
    XKh = [[('xT', c) for c in range(0, 512, 128)], [('xT', c) for c in range(512, 1024, 128)]]
    hc_in = nc.dram_tensor("hc_in", [128, 16], F32)
    hc_out = nc.dram_tensor("hc_out", [512, 16], F32)
    XH = sb("XH", [128, 8, 2]); HG = sb("HG", [128, 4, 8, 2]); XHh = sb("XHh", [128, 8, 2])

    def emit_halo_select():
        for (dst, src, sel0) in ((0, 1, 4), (1, 0, 8)):
            fw.op('dve', lambda e: e.tensor_scalar(out=XHh[:, :, dst], in0=HG[:, 0, :, src], scalar1=K.cst[:, sel0:sel0 + 1], scalar2=None, op0=ALU.mult),
                  reads=['HG', 'cst'], writes=['XHh'])
            for r in range(1, 4):
                fw.op('dve', lambda e, r=r: e.scalar_tensor_tensor(out=XHh[:, :, dst], in0=HG[:, r, :, src], scalar=K.cst[:, sel0 + r:sel0 + r + 1],
                                                                    in1=XHh[:, :, dst], op0=ALU.mult, op1=ALU.add), reads=['HG', 'cst', 'XHh'], writes=['XHh'])

    with Scope(K):
        h2T = sb("h2T", [128, 8, 1026], BF16)
        sq = sb("sq2", [128, 8, 512], BF16); rstd = sb("rstd2", [128, 512]); rstdS = sb("rstd2S", [128, 512]); tmpn = sb("tmpn2", [128, 2, 512])
        K.wslab = [sb("wslabF%d" % i, [128, 8, 512], BF16) for i in range(2)]
        slabs = [load_w(K, [(I.w_out[:, g * 512:(g + 1) * 512], 512)]) for g in range(2)]
        A2 = lambda cond: (lambda c: AB[:, 1, c, cond:cond + 1])
        B2 = lambda cond: (lambda c: MOD[:, 24 + c, cond:cond + 1])
        for hb in range(2):
            for g in range(2):
                slab, sk = slabs[g]
                for m in range(4):
                    mo = g * 4 + m
                    ps, pk = bank(K)
                    for k in range(8):
                        fw.op('pe', lambda t, ps=ps, k=k, m=m, hb=hb, slab=slab: t.matmul(ps[:, 0:512], slab[:, k, m * 128:(m + 1) * 128], mixT[:, k, hb * 512:(hb + 1) * 512],
                                                                                       start=(k == 0), stop=(k == 7)), reads=[sk] + [('mixT', q) for q in range(8)], writes=[pk])
                    fw.op('dve', lambda e, ps=ps, mo=mo, hb=hb: e.scalar_tensor_tensor(
                        out=xT[:, mo, hb * 512:(hb + 1) * 512], in0=ps[:, 0:512], scalar=MOD[:, 16 + mo, hb:hb + 1],
                        in1=xT[:, mo, hb * 512:(hb + 1) * 512], op0=ALU.mult, op1=ALU.add), reads=[pk, 'MOD'] + XKh[hb], writes=XKh[hb])
            if hb == 0:
                rms_stats(K, sq, rstd, xT, XKh[0], 0, 512, 'rstd2')
        dump(K, "xmid", xT[:], [128, 8, 1024], XK)
        fw.op('dve', lambda e: e.tensor_copy(out=XH[:, :, 0], in_=xT[:, :, 512]), reads=XKh[1], writes=['XH'])
        fw.op('dve', lambda e: e.tensor_copy(out=XH[:, :, 1], in_=xT[:, :, 1023]), reads=XKh[1], writes=['XH'])
        fw.dma('pool', hc_in.ap(), XH[:].rearrange("p a b -> p (a b)"), reads=['XH'], writes=['hc_in'])
        fw.async_op('pool', lambda g: g.collective_compute("AllGather", ALU.bypass, replica_groups=[[0, 1, 2, 3], [4, 5, 6, 7]],
                                                           ins=[hc_in.ap().opt()], outs=[hc_out.ap().opt()]), reads=['hc_in'], writes=['hc_out'])
        fw.dma('pool', HG[:].rearrange("p r a b -> p r (a b)"), hc_out.ap().rearrange("(r p) f -> p r f", p=128), reads=['hc_out'], writes=['HG'])
        rms_apply(K, rstd, tmpn, xT, XKh[0], 0, 512, h2T, 0, ['h2T0'], A2(0), B2(0), 'rstd2')
        rms_stats(K, sq, rstdS, xT, XKh[1], 512, 512, 'rstd2S')
        rms_apply(K, rstdS, tmpn, xT, XKh[1], 512, 512, h2T, 512, ['h2T512'], A2(1), B2(1), 'rstd2S')
        emit_halo_select()
        rmsnorm_block(K, sq, rstd, tmpn, XHh, ['XHh'], 0, 2, h2T, 1024, ['h2T1024'], A2(1), B2(1), 'rstd2')
        H2K = ['h2T0', 'h2T512', 'h2T1024']
        HM = sb("HM", [128, 22, 1024], BF16)
        W2f = sb("W2f", [128, 22, 1024], BF16)

        def load_w2(q):
            fw.dma('pool', W2f[:, q * 6:min(22, (q + 1) * 6), :], I.wf2[q * 768:min(2816, (q + 1) * 768), :].rearrange("(k p) c -> p k c", p=128), writes=['W2f'])
        Zf = [sb("Zf%d" % i, [128, 1030], BF16) for i in range(2)]
        uf = sb("uf", [128, 1024]); sf = sb("sf", [128, 1024])
        for zi in range(2):
            fw.op('pool', lambda e, zi=zi: e.memset(Zf[zi][:, 0:516].rearrange("p (s t) -> p s t", s=2)[:, :, 0:258:257], 0.0), writes=['Zf%d' % zi])
        for g in range(11):
            slab, sk = load_w(K, [(I.w1[:, g * 256:(g + 1) * 256], 256), (I.w3[:, g * 256:(g + 1) * 256], 256)])
            if g in (2, 4, 6, 8):
                load_w2(g // 2 - 1)
            for m in range(2):
                f = g * 2 + m
                Zt = Zf[f % 2]; zk = 'Zf%d' % (f % 2)
                for (col0, kind) in ((0, 'P'), (512, 'S')):
                    ps, pk = bank(K)
                    for k in range(8):
                        fw.op('pe', lambda t, ps=ps, k=k, m=m, col0=col0: t.matmul(ps[:, 0:512], slab[:, k, m * 128:(m + 1) * 128], h2T[:, k, col0:col0 + 512],
                                                                                start=(k == 0), stop=(k == 7)), reads=[sk] + H2K, writes=[pk])
                    if kind == 'P':
                        evac(K, 'act', Zt[:, 0:516].rearrange("p (s t) -> p s t", s=2)[:, :, 1:257], ps[:, 0:512].rearrange("p (s t) -> p s t", s=2), [pk], [zk])
                    else:
                        evac(K, 'act', Zt[:, 517:1029], ps[:, 0:512], [pk], [zk])
                ps, pk = bank(K)
                for k in range(8):
                    fw.op('pe', lambda t, ps=ps, k=k, m=m: t.matmul(ps[:, 0:2], slab[:, k, m * 128:(m + 1) * 128], h2T[:, k, 1024:1026], start=(k == 0), stop=(k == 7)),
                          reads=[sk] + H2K, writes=[pk])
                fw.op('dve', lambda e, ps=ps: e.tensor_scalar(out=Zt[:, 516:517], in0=ps[:, 0:1], scalar1=K.cst[:, 12:13], scalar2=None, op0=ALU.mult),
                      reads=[pk, 'cst'], writes=[zk])
                fw.op('dve', lambda e, ps=ps: e.tensor_scalar(out=Zt[:, 1029:1030], in0=ps[:, 1:2], scalar1=K.cst[:, 13:14], scalar2=None, op0=ALU.mult),
                      reads=[pk, 'cst'], writes=[zk])
                w = [PV[:, PV_WC + j * 22 + f:PV_WC + j * 22 + f + 1] for j in range(3)]
                zP = lambda off: Zt[:, 0:516].rearrange("p (s t) -> p s t", s=2)[:, :, off:off + 256]
                zS = lambda off: Zt[:, 516 + off:516 + off + 512]
                uP = uf[:, 0:512].rearrange("p (s t) -> p s t", s=2)
                uS = uf[:, 512:1024]
                for (zv, uv) in ((zP, uP), (zS, uS)):
                    fw.op('act', lambda e, zv=zv, uv=uv: e.activation(out=uv, in_=zv(0), func=AF.Identity, scale=w[0]), reads=[zk, 'PV'], writes=['uf'])
                    fw.op('dve', lambda e, zv=zv, uv=uv: e.scalar_tensor_tensor(out=uv, in0=zv(1), scalar=w[1], in1=uv, op0=ALU.mult, op1=ALU.add),
                          reads=[zk, 'PV', 'uf'], writes=['uf'])
                    fw.op('dve', lambda e, zv=zv, uv=uv: e.scalar_tensor_tensor(out=uv, in0=zv(2), scalar=w[2], in1=uv, op0=ALU.mult, op1=ALU.add),
                          reads=[zk, 'PV', 'uf'], writes=['uf'])
                fw.op('act', lambda e: e.activation(out=sf[:], in_=uf[:], func=AF.Silu), reads=['uf'], writes=['sf'])
                for hb in range(2):
                    ps, pk = bank(K)
                    for k in range(8):
                        fw.op('pe', lambda t, ps=ps, k=k, m=m, hb=hb: t.matmul(ps[:, 0:512], slab[:, k, 256 + m * 128:256 + (m + 1) * 128], h2T[:, k, hb * 512:(hb + 1) * 512],
                                                                            start=(k == 0), stop=(k == 7)), reads=[sk] + H2K, writes=[pk])
                    fw.op('dve', lambda e, ps=ps, hb=hb, f=f: e.tensor_tensor(out=HM[:, f, hb * 512:(hb + 1) * 512], in0=ps[:, 0:512], in1=sf[:, hb * 512:(hb + 1) * 512], op=ALU.mult),
                          reads=[pk, 'sf'], writes=[('HM', f)])
        HMK = [('HM', f) for f in range(22)]
        for g in range(2):
            for m in range(4):
                mo = g * 4 + m
                for hb in range(2):
                    ps, pk = bank(K)
                    for f in range(22):
                        fw.op('pe', lambda t, ps=ps, f=f, mo=mo, hb=hb: t.matmul(ps[:, 0:512], W2f[:, f, mo * 128:(mo + 1) * 128], HM[:, f, hb * 512:(hb + 1) * 512],
                                                                            start=(f == 0), stop=(f == 21)), reads=['W2f'] + HMK, writes=[pk])
                    fw.op('dve', lambda e, ps=ps, mo=mo, hb=hb: e.scalar_tensor_tensor(
                        out=xT[:, mo, hb * 512:(hb + 1) * 512], in0=ps[:, 0:512], scalar=MOD[:, 40 + mo, hb:hb + 1],
                        in1=xT[:, mo, hb * 512:(hb + 1) * 512], op0=ALU.mult, op1=ALU.add), reads=[pk, 'MOD'] + XK, writes=XK)
    with Scope(K):
        sq = sb("sq3", [128, 8, 512], BF16); rstd = sb("rstd3", [128, 512]); tmpn = sb("tmpn3", [128, 2, 512])
        YFs = [sb("YF%d" % i, [128, 8, 512]) for i in range(2)]
        rstds3 = [rstd, sb("rstd3b", [128, 512])]
        OT = [sb("OT%d" % i, [128, 1024]) for i in range(2)]
        oc = 0
        for hb in range(2):
            rms_stats(K, sq, rstds3[hb], xT, XK, hb * 512, 512, 'rstd3_%d' % hb)
        for hb in range(2):
            rms_apply(K, rstds3[hb], tmpn, xT, XK, hb * 512, 512, YFs[hb], 0, ['YF%d' % hb],
                      lambda c: PV[:, PV_NORMF + c:PV_NORMF + c + 1], None, 'rstd3_%d' % hb)
        for hb, dst in ((0, O.yp), (1, O.ys)):
            YF = YFs[hb]
            for tb in range(4):
                ot = OT[oc % 2]; otk = 'OT%d' % (oc % 2); oc += 1
                for half in range(2):
                    ps, pk = bank(K)
                    for c4 in range(4):
                        c = half * 4 + c4
                        fw.op('pe', lambda t, ps=ps, c=c, c4=c4, tb=tb: t.transpose(ps[:, c4 * 128:(c4 + 1) * 128], YF[:, c, tb * 128:(tb + 1) * 128], K.ident_f[:]),
                              reads=['YF%d' % hb, 'ident_f'], writes=[pk])
                    evac(K, 'act' if half == 0 else 'dve', ot[:, half * 512:(half + 1) * 512], ps[:, 0:512], [pk], [otk])
                fw.dma('sp', dst[tb * 128:(tb + 1) * 128, :], ot[:], reads=[otk])


def _prep_inputs(inp):
    f = lambda a: np.ascontiguousarray(np.asarray(a, dtype=np.float32))
    x_prompt = f(inp['x_prompt']); x_sample = f(inp['x_sample'])
    shared = {}
    w_ada = f(inp['w_ada'][0]); b_ada = f(inp['b_ada'][0]); shared['w_in'] = f(inp['w_in'][0])
    shared['w2'] = f(inp['w2'][0]); shared['a2'] = f(inp['a2'][0]); shared['g2'] = f(inp['g2'][0])
    shared['w_out'] = f(inp['w_out'][0]); shared['w1'] = f(inp['w_ffn1'][0]); shared['w3'] = f(inp['w_ffn3'][0])
    shared['wf2'] = f(inp['w_ffn2'][0])
    shared['ident'] = np.eye(128, dtype=np.float32)
    bd = np.zeros((128, 128), np.float32); bd[:64, :64] = 1; bd[64:, 64:] = 1
    shared['bd1'] = bd
    s = np.arange(128)[:, None]; t = np.arange(128)[None, :]
    mq = np.zeros((128, 2, 4, 128), np.float32)
    for d, (strict, incl) in enumerate((((s < t), (s <= t)), ((s > t), (s >= t)))):
        mq[:, d, 0] = -1.0 * strict
        mq[:, d, 1] = 1.0 * incl
        mq[:, d, 2] = 1.0 * strict
        mq[:, d, 3] = 1.0 * incl
    shared['mq'] = mq.reshape(128, 1024)
    mb = np.zeros((128, 2, 128), np.float32)
    mb[:, 0] = -1.0 * (t < s)
    mb[:, 1] = -1.0 * (t > s)
    shared['mb'] = mb.reshape(128, 256)
    rst = np.ones((128, 512), np.float32); rst[:, ::128] = 0
    shared['rst'] = rst
    xi = np.zeros((128, 128), np.float32); xi[np.arange(128), (np.arange(128) + 64) % 128] = 1
    shared['xi'] = xi
    rpb = f(inp['rpb'][0])
    bt = np.full((128, 2, 8, 8, 64), NEG, np.float32)
    jj = np.arange(64)
    cs_ = np.clip(jj - 8, 0, 48)
    for par in range(2):
        for blk in range(8):
            for p in range(128):
                rel = (-8 if par == 0 else -7) + 2 * blk + p // 64
                ap = rel + 7
                kc = p % 64
                if ap < 0 or ap > 14:
                    continue
                ok = (kc >= cs_) & (kc < cs_ + 16)
                co = kc - jj + 15
                vals = rpb[:, ap, np.clip(co, 0, 30)]
                bt[p, par, blk] = np.where(ok[None, :], vals, NEG)
    shared['bt'] = bt.reshape(128, -1)
    pv_common = np.zeros((PV_ROWS, 128), np.float32)
    pv_common[PV_NORM1:PV_NORM1 + 8] = f(inp['norm1'][0]).reshape(8, 128)
    pv_common[PV_NORM2:PV_NORM2 + 8] = f(inp['norm2'][0]).reshape(8, 128)
    pv_common[PV_NORMF:PV_NORMF + 8] = f(inp['norm_f']).reshape(8, 128)
    pv_common[PV_WTS:PV_WTS + 45] = f(inp['w_ts'][0]).reshape(45, 128)
    pv_common[PV_W0:PV_W0 + 8] = f(inp['w0'][0]).reshape(8, 128)
    pv_common[PV_A0:PV_A0 + 8] = f(inp['a0'][0]).reshape(8, 128)
    pv_common[PV_KK:PV_KK + 4] = f(inp['k_k'][0]).reshape(4, 128)
    pv_common[PV_KA:PV_KA + 4] = f(inp['k_a'][0]).reshape(4, 128)
    pv_common[PV_RK:PV_RK + 4] = f(inp['r_k'][0]).reshape(4, 128)
    pv_common[PV_LNG:PV_LNG + 4] = f(inp['ln_x_g'][0]).reshape(4, 128)
    pv_common[PV_LNB:PV_LNB + 4] = f(inp['ln_x_b'][0]).reshape(4, 128)
    pv_common[PV_WC:PV_WC + 66] = f(inp['w_ffn_conv'][0]).reshape(66, 128)
    cache_k = f(inp['cache_k']); cache_v = f(inp['cache_v']); st = f(inp['state_rwkv'])
    cvec = f(inp['c']); c_ctx = f(inp['c_ctx'])
    in_maps = []
    for c in range(8):
        b, j = c // 4, c % 4
        m = dict(shared)
        m['xp'] = x_prompt[2 * c:2 * c + 2].reshape(512, D)
        m['xs'] = x_sample[b, 512 * j:512 * j + 512]
        xh = np.zeros((448, D), np.float32)
        lo = 512 * j - 256
        if lo >= 0:
            xh[0:256] = x_sample[b, lo:lo + 256]
        hi = 512 * j + 512
        if hi + 192 <= 2048:
            xh[256:448] = x_sample[b, hi:hi + 192]
        m['xh'] = xh
        m['ck'] = cache_k[b, 0].reshape(512, 512)
        m['cv'] = cache_v[b, 0].reshape(512, 512)
        m['s0'] = st[b, 0]
        pv = pv_common.copy()
        pv[PV_C:PV_C + 8] = c_ctx.reshape(8, 128)
        pv[PV_C + 8:PV_C + 16] = cvec[0].reshape(8, 128)
        pv[PV_C + 16:PV_C + 24] = cvec[1].reshape(8, 128)
        pv[PV_BADA:PV_BADA + 12] = b_ada[j * 1536:(j + 1) * 1536].reshape(12, 128)
        m['w_ada_sh'] = np.ascontiguousarray(w_ada[:, j * 1536:(j + 1) * 1536])
        m['pvec'] = pv
        rb = np.full((128, 8, 8), NEG, np.float32)
        for l in range(8):
            i = 8 * j + l
            si = min(max(i - 4, 0), 24)
            par = l % 2
            for blk in range(8):
                for half in range(2):
                    rel = (-8 if par == 0 else -7) + 2 * blk + half
                    kr = i + rel
                    if si <= kr < si + 8:
                        rb[half * 64:(half + 1) * 64, l, blk] = 0.0
        m['rbp'] = rb.reshape(128, 64)
        cst = np.zeros((128, 16), np.float32)
        cst[:, 0 + j] = 1.0
        cst[:, 14 + b] = 1.0
        if j > 0:
            cst[:, 4 + (j - 1)] = 1.0; cst[:, 12] = 1.0
        if j < 3:
            cst[:, 8 + (j + 1)] = 1.0; cst[:, 13] = 1.0
        m['cst'] = cst
        in_maps.append(m)
    return in_maps


_NC_CACHE = {}


def kernel(**inputs):
    in_maps = _prep_inputs(inputs)
    if 'nc' not in _NC_CACHE:
        _NC_CACHE['nc'] = build_nc()[0]
    nc = _NC_CACHE['nc']
    res = run_bass_kernel_spmd(nc, in_maps, core_ids=list(range(8)))
    R = res.results
    y_prompt = np.concatenate([R[c]['yp'].reshape(2, 256, D) for c in range(8)], 0)
    y_sample = np.stack([np.concatenate([R[b * 4 + j]['ys'] for j in range(4)], 0) for b in range(2)], 0)
    nk = np.concatenate([R[c]['nk'].reshape(2, 1, 256, 8, 64) for c in range(8)], 0)
    nv = np.concatenate([R[c]['nv'].reshape(2, 1, 256, 8, 64) for c in range(8)], 0)
    ns = np.concatenate([R[c]['ns'].reshape(2, 1, 2, 8, 64, 64) for c in range(8)], 0)
    return (y_prompt.astype(np.float32), y_sample.astype(np.float32), nk.astype(np.float32),
            nv.astype(np.float32), ns.astype(np.float32))
```

```python
import numpy as np
from contextlib import ExitStack
import concourse.bass as bass
import concourse.mybir as mybir
from concourse.bass_utils import run_bass_kernel_spmd

F32 = mybir.dt.float32
BF16 = mybir.dt.bfloat16
AF = mybir.ActivationFunctionType
ALU = mybir.AluOpType

D = 1024
NEG = -30000.0
EPS = 1e-6
GN_EPS = 64e-5

PV_NORM1, PV_NORM2, PV_NORMF, PV_BADA, PV_WTS, PV_W0, PV_A0 = 0, 8, 16, 24, 72, 117, 125
PV_KK, PV_KA, PV_RK, PV_LNG, PV_LNB, PV_WC, PV_C = 133, 137, 141, 145, 149, 153, 219
PV_ROWS = 256


class _Eng:
    def __init__(self, name, handle, sem):
        self.name = name
        self.h = handle
        self.sem = sem
        self.count = 0
        self.waited = {}
        self.dma_rr = 0


class FW:
    def __init__(self, nc, es, n_dma_sems=10):
        self.nc = nc
        mk = lambda n: es.enter_context(nc.semaphore(n))
        self.mk = mk
        self.E = {
            'pe': _Eng('pe', nc.tensor, mk('s_pe')),
            'act': _Eng('act', nc.scalar, mk('s_act')),
            'dve': _Eng('dve', nc.vector, mk('s_dve')),
            'pool': _Eng('pool', nc.gpsimd, mk('s_pool')),
            'sp': _Eng('sp', nc.sync, mk('s_sp')),
        }
        self.dma_sems = {}
        for q in ('sp', 'pool'):
            self.dma_sems[q] = [[mk('d_%s%d' % (q, i)), 0] for i in range(n_dma_sems)]
        self.last_write = {}
        self.readers = {}
        self.semobj = {}
        self.nops = 0
        self.bank_rr = 0

    def _need(self, needed, tok):
        if tok is None:
            return
        sem, val, owner = tok
        k = id(sem)
        self.semobj[k] = sem
        if needed.get(k, (0, None))[0] < val:
            needed[k] = (val, owner)

    def _deps(self, eng, reads, writes):
        needed = {}
        for r in reads:
            self._need(needed, self.last_write.get(r))
        for w in writes:
            self._need(needed, self.last_write.get(w))
            for t in self.readers.get(w, {}).values():
                self._need(needed, t)
        e = self.E[eng]
        for k, (val, owner) in needed.items():
            if owner == 'pe' and eng == 'pe':
                continue
            if e.waited.get(k, 0) < val:
                e.h.wait_ge(self.semobj[k], val)
                e.waited[k] = val

    def _track(self, tok, reads, writes):
        owner = tok[2]
        for w in writes:
            self.last_write[w] = tok
            self.readers[w] = {}
        for r in reads:
            if r in writes:
                continue
            self.readers.setdefault(r, {})[owner] = tok

    capture = None

    def replay(self, lst, n):
        for _ in range(min(n, len(lst))):
            eng, fn, reads, writes = lst.pop(0)
            self.op(eng, fn, reads, writes)

    def async_op(self, eng, fn, reads=(), writes=()):
        e = self.E[eng]
        self._deps(eng, reads, writes)
        sem = self.mk('cc_sem%d' % self.nops)
        ins = fn(e.h)
        ins.then_inc(sem, 1)
        self._track((sem, 1, 'cc%d' % self.nops), reads, writes)
        self.nops += 1
        return ins

    def op(self, eng, fn, reads=(), writes=()):
        if self.capture is not None:
            self.capture.append((eng, fn, list(reads), list(writes)))
            return None
        reads = [r.key if isinstance(r, LazyBank) else r for r in reads]
        writes = [r.key if isinstance(r, LazyBank) else r for r in writes]
        e = self.E[eng]
        bk = [r for r in reads if isinstance(r, str) and r.startswith('bank') and r not in writes]
        if bk:
            writes = list(writes) + bk
        self._deps(eng, reads, writes)
        ins = fn(e.h)
        e.count += 1
        ins.then_inc(e.sem, 1)
        self._track((e.sem, e.count, eng), reads, writes)
        self.nops += 1
        return ins

    def dma(self, q, out, in_, reads=(), writes=(), **kw):
        e = self.E[q]
        self._deps(q, reads, writes)
        slots = self.dma_sems[q]
        i = e.dma_rr % len(slots)
        e.dma_rr += 1
        sem, val = slots[i]
        k = id(sem)
        self.semobj[k] = sem
        if val > 0 and e.waited.get(k, 0) < val:
            e.h.wait_ge(sem, val)
            e.waited[k] = val
        ins = e.h.dma_start(out=out, in_=in_, **kw)
        ins.then_inc(sem, 16)
        slots[i][1] = val + 16
        self._track((sem, val + 16, 'dma_%s_%d' % (q, i)), reads, writes)
        self.nops += 1
        return ins

    def finish(self, eng='sp'):
        e = self.E[eng]
        for q, slots in self.dma_sems.items():
            for sem, val in slots:
                if val > 0 and e.waited.get(id(sem), 0) < val:
                    e.h.wait_ge(sem, val)
                    e.waited[id(sem)] = val
        for n, o in self.E.items():
            if n != eng and o.count > 0 and e.waited.get(id(o.sem), 0) < o.count:
                e.h.wait_ge(o.sem, o.count)
                e.waited[id(o.sem)] = o.count


class Ctx:
    pass


class LazyBank:
    def __init__(self, K):
        self.K = K
        self.v = None

    def get(self):
        if self.v is None:
            self.v = bank(self.K)
        return self.v

    @property
    def ps(self):
        return self.get()[0]

    @property
    def key(self):
        return self.get()[1]


def build_nc(stage=99, dbg=()):
    nc = bass.Bass("TRN2", target_bir_lowering=False)
    K = Ctx()
    K.nc = nc
    K.dbg = set(dbg)
    K.dbg_specs = {}

    def din(name, shape):
        return nc.dram_tensor(name, list(shape), F32, kind="ExternalInput").ap()

    def dout(name, shape):
        return nc.dram_tensor(name, list(shape), F32, kind="ExternalOutput").ap()

    I = Ctx()
    I.xp = din("xp", [512, D]); I.xs = din("xs", [512, D]); I.xh = din("xh", [448, D])
    I.ck = din("ck", [512, 512]); I.cv = din("cv", [512, 512]); I.s0 = din("s0", [2, 8, 64, 64])
    I.pvec = din("pvec", [PV_ROWS, 128])
    I.w_ada_sh = din("w_ada_sh", [D, 1536]); I.w_in = din("w_in", [D, 3456])
    I.w2 = din("w2", [2, 64, 512]); I.a2 = din("a2", [2, 64, 512]); I.g2 = din("g2", [128, 512])
    I.w_out = din("w_out", [D, D]); I.w1 = din("w1", [D, 2816]); I.w3 = din("w3", [D, 2816])
    I.wf2 = din("wf2", [2816, D])
    I.bt = din("bt", [128, 2 * 8 * 8 * 64]); I.rbp = din("rbp", [128, 64])
    I.cst = din("cst", [128, 16]); I.ident = din("ident", [128, 128])
    I.mq = din("mq", [128, 2 * 512]); I.mb = din("mb", [128, 2 * 128]); I.bd1 = din("bd1", [128, 128])
    I.rst = din("rst", [128, 512]); I.xi = din("xi", [128, 128])
    O = Ctx()
    O.yp = dout("yp", [512, D]); O.ys = dout("ys", [512, D]); O.nk = dout("nk", [512, 512])
    O.nv = dout("nv", [512, 512]); O.ns = dout("ns", [2, 2, 8, 64, 64])
    K.I = I; K.O = O

    with ExitStack() as es:
        K.es = es
        fw = FW(nc, es)
        K.fw = fw
        K.cur_es = es
        K.sb = lambda n, s, d=F32: K.cur_es.enter_context(nc.sbuf_tensor("sb_" + n, list(s), d))
        K.banks = [es.enter_context(nc.psum_tensor("bank%d" % i, [128, 512], F32)) for i in range(8)]
        emit_program(K, stage)
        fw.finish('sp')
    K.nops = fw.nops
    return nc, K


def bank(K):
    fw = K.fw
    i = fw.bank_rr % 8
    fw.bank_rr += 1
    return K.banks[i], 'bank%d' % i


def dump(K, name, ap, shape, reads):
    if name not in K.dbg:
        return
    t = K.nc.dram_tensor("dbg_" + name, list(shape), ap.dtype, kind="ExternalOutput").ap()
    K.fw.dma('sp', t, ap, reads=reads)
    K.dbg_specs[name] = shape


def emit_program(K, stage):
    nc, fw, I, O, sb = K.nc, K.fw, K.I, K.O, K.sb
    ident_f = sb("ident_f", [128, 128]); ident_b = sb("ident_b", [128, 128], BF16)
    ones_b = sb("ones_b", [128, 128], BF16)
    bd1_b = sb("bd1_b", [128, 128], BF16)
    cst = sb("cst", [128, 16])
    PV = sb("PV", [128, PV_ROWS])
    K.ident_f, K.ident_b, K.ones_b, K.bd1_b, K.cst, K.PV = ident_f, ident_b, ones_b, bd1_b, cst, PV
    fw.dma('sp', ident_f[:], I.ident[:, :], writes=['ident_f'])
    fw.dma('pool', ident_b[:], I.ident[:, :], writes=['ident_b'])
    fw.dma('pool', bd1_b[:], I.bd1[:, :], writes=['bd1_b'])
    fw.dma('sp', cst[:], I.cst[:, :], writes=['cst'])
    fw.op('pool', lambda e: e.memset(ones_b[:], 1.0), writes=['ones_b'])
    epsc = sb("epsc", [128, 2])
    fw.op('pool', lambda e: e.memset(epsc[:, 0:1], EPS), writes=['epsc'])
    fw.op('pool', lambda e: e.memset(epsc[:, 1:2], GN_EPS), writes=['epsc'])
    K.epsc = epsc

    pv_rows = sb("pv_rows", [128, 2, 128])
    fw.dma('sp', pv_rows[:], I.pvec.rearrange("(a p) f -> p a f", p=128), writes=['pv_rows'])
    for a in range(2):
        ps, pk = bank(K)
        fw.op('pe', lambda t, a=a, ps=ps: t.transpose(ps[:, 0:128], pv_rows[:, a, :], ident_f[:]),
              reads=['pv_rows', 'ident_f'], writes=[pk])
        fw.op('dve', lambda e, a=a, ps=ps: e.tensor_copy(out=PV[:, a * 128:(a + 1) * 128], in_=ps[:, 0:128]),
              reads=[pk], writes=['PV'])

    MOD = sb("MOD", [128, 48, 2])
    cs = sb("cs", [128, 24], BF16)
    K.MOD = MOD
    fw.op('act', lambda e: e.activation(out=cs[:], in_=PV[:, PV_C:PV_C + 24], func=AF.Silu), reads=['PV'], writes=['cs'])
    K.wslab_rr = 0
    K.AB = sb("AB", [128, 2, 8, 2])
    K.cs = cs
    if stage <= 0:
        return
    emit_front(K, stage)


def barrier(K):
    fw = K.fw
    for n, e in fw.E.items():
        for q, slots in fw.dma_sems.items():
            for sem, val in slots:
                if val > 0 and e.waited.get(id(sem), 0) < val:
                    e.h.wait_ge(sem, val)
                    e.waited[id(sem)] = val
        for n2, o in fw.E.items():
            if n2 != n and o.count > 0 and e.waited.get(id(o.sem), 0) < o.count:
                e.h.wait_ge(o.sem, o.count)
                e.waited[id(o.sem)] = o.count


class Scope:
    def __init__(self, K):
        self.K = K

    def __enter__(self):
        self.prev = self.K.cur_es
        self.es = ExitStack()
        self.es.__enter__()
        self.K.cur_es = self.es
        return self

    def __exit__(self, *a):
        barrier(self.K)
        self.K.cur_es = self.prev
        return self.es.__exit__(*a)


def evac(K, eng, out, in_, reads, writes, scale=None):
    if eng == 'act':
        if scale is None:
            return K.fw.op('act', lambda e: e.activation(out=out, in_=in_, func=AF.Copy), reads=reads, writes=writes)
        return K.fw.op('act', lambda e: e.activation(out=out, in_=in_, func=AF.Identity, scale=scale), reads=reads, writes=writes)
    if scale is None:
        return K.fw.op(eng, lambda e: e.tensor_copy(out=out, in_=in_), reads=reads, writes=writes)
    return K.fw.op(eng, lambda e: e.tensor_scalar(out=out, in0=in_, scalar1=scale, scalar2=None, op0=ALU.mult), reads=reads, writes=writes)


def load_w(K, parts):
    si = K.wslab_rr % 2
    K.wslab_rr += 1
    slab = K.wslab[si]
    key = 'wslab%d' % (si if K.wslab[0] is not K.wslab[1] else 0)
    c = 0
    for ap, n in parts:
        K.fw.dma('pool', slab[:, :, c:c + n], ap.rearrange("(k p) c -> p k c", p=128), writes=[key])
        c += n
    return slab, key


def emit_mod_load(K):
    nc, fw, I, sb = K.nc, K.fw, K.I, K.sb
    PV, cst, MOD, cs = K.PV, K.cst, K.MOD, K.cs
    md_in = nc.dram_tensor("md_in", [128, 48], F32)
    md_out = nc.dram_tensor("md_out", [512, 48], F32)
    wslab = [sb("wslab%d" % i, [128, 8, 768], BF16) for i in range(2)]
    K.wslab = wslab
    MP = sb("MP", [128, 12, 4]); MG = sb("MG", [128, 48, 4])
    fw.op('pool', lambda e: e.memset(MP[:].rearrange("p a b -> p (a b)"), 0.0), writes=['MP'])
    for g in range(2):
        fw.dma('pool', wslab[g][:], I.w_ada_sh[:, g * 768:(g + 1) * 768].rearrange("(k p) c -> p k c", p=128), writes=['wslab%d' % g])
    K.mod_tiles = (wslab, MP, MG, md_in, md_out)


def emit_mod_compute(K):
    nc, fw, I, sb = K.nc, K.fw, K.I, K.sb
    PV, cst, MOD, cs = K.PV, K.cst, K.MOD, K.cs
    wslab, MP, MG, md_in, md_out = K.mod_tiles
    modps, modk = bank(K)
    for g in range(2):
        slab = wslab[g]
        for m in range(6):
            mm = g * 6 + m
            for k in range(8):
                fw.op('pe', lambda t, m=m, k=k, mm=mm, slab=slab: t.matmul(modps[:, 3 * mm:3 * mm + 3], slab[:, k, m * 128:(m + 1) * 128], cs[:, k:24:8],
                                                                         start=(k == 0), stop=(k == 7)), reads=['wslab%d' % g, 'cs'], writes=[modk])
    fw.op('dve', lambda e: e.tensor_tensor(out=MP[:, :, 0:3], in0=modps[:, 0:36].rearrange("p (m c) -> p m c", c=3),
                                           in1=PV[:, PV_BADA:PV_BADA + 12].unsqueeze(2).broadcast_to([128, 12, 3]), op=ALU.add),
          reads=[modk, 'PV'], writes=['MP'])
    fw.dma('pool', md_in.ap(), MP[:].rearrange("p a b -> p (a b)"), reads=['MP'], writes=['md_in'])
    fw.async_op('pool', lambda g: g.collective_compute("AllGather", ALU.bypass, replica_groups=[[0, 1, 2, 3], [4, 5, 6, 7]],
                                                 ins=[md_in.ap().opt()], outs=[md_out.ap().opt()]), reads=['md_in'], writes=['md_out'])
    fw.dma('pool', MG[:].rearrange("p (r m) c -> p r (m c)", r=4), md_out.ap().rearrange("(r p) f -> p r f", p=128), reads=['md_out'], writes=['MG'])
    fw.op('dve', lambda e: e.tensor_copy(out=MOD[:, :, 0], in_=MG[:, :, 0]), reads=['MG'], writes=['MOD'])
    fw.op('dve', lambda e: e.tensor_scalar(out=MOD[:, :, 1], in0=MG[:, :, 1], scalar1=cst[:, 14:15], scalar2=None, op0=ALU.mult), reads=['MG', 'cst'], writes=['MOD'])
    fw.op('dve', lambda e: e.scalar_tensor_tensor(out=MOD[:, :, 1], in0=MG[:, :, 2], scalar=cst[:, 15:16], in1=MOD[:, :, 1], op0=ALU.mult, op1=ALU.add),
          reads=['MG', 'cst', 'MOD'], writes=['MOD'])
    AB = K.AB
    for which, (sc_m, nrow) in enumerate(((8, PV_NORM1), (32, PV_NORM2))):
        for cond in range(2):
            fw.op('dve', lambda e, which=which, sc_m=sc_m, nrow=nrow, cond=cond: e.scalar_tensor_tensor(
                out=AB[:, which, :, cond], in0=MOD[:, sc_m:sc_m + 8, cond], scalar=1.0,
                in1=PV[:, nrow:nrow + 8], op0=ALU.add, op1=ALU.mult),
                reads=['MOD', 'PV'], writes=['AB'])
    dump(K, "MOD", MOD[:], [128, 48, 2], ['MOD'])


def emit_front(K, stage):
    nc, fw, I, O, sb = K.nc, K.fw, K.I, K.O, K.sb
    xT = sb("xT", [128, 8, 1024])
    mixT = sb("mixT", [128, 8, 1024], BF16)
    K.xT, K.mixT = xT, mixT
    xkeys = [('xT', c) for c in range(0, 1024, 128)]
    K.xkeys = xkeys
    with Scope(K):
        hT = sb("hT", [128, 8, 1026], BF16)
        K.hT = hT
        K.rc = dict(W2b=sb("W2b", [128, 512], BF16), A2b=sb("A2b", [128, 512], BF16), G2b=sb("G2b", [128, 512], BF16),
                    mqb=sb("mqb", [128, 2, 512], BF16), mbb=sb("mbb", [128, 2, 128], BF16), rst=sb("rst", [128, 512], BF16),
                    xi_f=sb("xi_f", [128, 128]), bd1_f=sb("bd1_f", [128, 128]))
        hscope = Scope(K)
        hscope.__enter__()
        hTh = sb("hTh", [128, 8, 448], BF16)
        K.hTh = hTh
        K.vctx = sb("vctx", [128, 4, 512], BF16)
        K.MB = sb("MB", [128, 2, 8, 8, 64], BF16)
        K.RB = sb("RB", [128, 64])
        K.cktm = [sb("cktm%d" % i, [128, 512]) for i in range(4)]
        K.wslabA = [sb("wslabA%d" % i, [128, 8, 768], BF16) for i in range(2)]
        with Scope(K):
            xTh = sb("xTh", [128, 8, 448])
            xtm = [sb("xtm%d" % i, [128, 1024]) for i in range(3)]
            emit_mod_load(K)
            blocks = []
            for b in range(4):
                blocks.append((I.xp[b * 128:(b + 1) * 128, :], 128, xT, b * 128, 'xT'))
            for b in range(4):
                blocks.append((I.xs[b * 128:(b + 1) * 128, :], 128, xT, 512 + b * 128, 'xT'))
            for b in range(4):
                n = 128 if b < 3 else 64
                blocks.append((I.xh[b * 128:b * 128 + n, :], n, xTh, b * 128, 'xTh'))
            for bi, (src, n, dst, col, dk) in enumerate(blocks):
                t = xtm[bi % 3]; tk = 'xtm%d' % (bi % 3)
                fw.dma('sp', t[0:n, :], src, writes=[tk])
                for half in range(2):
                    ps, pk = bank(K)
                    for c4 in range(4):
                        c = half * 4 + c4
                        fw.op('pe', lambda tt, t=t, n=n, c=c, c4=c4, ps=ps: tt.transpose(
                            ps[:, c4 * 128:c4 * 128 + n], t[0:n, c * 128:(c + 1) * 128], K.ident_f[0:n, 0:n]),
                            reads=[tk, 'ident_f'], writes=[pk])
                    evac(K, 'act' if (bi + half) % 2 == 0 else 'dve', dst[:, half * 4:half * 4 + 4, col:col + n],
                         ps[:].rearrange("p (c t) -> p c t", c=4)[:, :, 0:n], [pk], [(dk, col)])
            dump(K, "xT", xT[:], [128, 8, 1024], xkeys)

            sq = sb("sq", [128, 8, 512], BF16)
            emit_mod_compute(K)
            fw.dma('pool', K.vctx[:], I.cv.rearrange("(b p) c -> p b c", p=128), writes=['vctx'])
            fw.dma('pool', K.MB[:].rearrange("p a b h j -> p (a b h j)"), I.bt[:, :], writes=['MB'])
            fw.dma('sp', K.RB[:], I.rbp[:, :], writes=['RB'])
            for kb in range(4):
                fw.dma('sp', K.cktm[kb][:, :], I.ck[kb * 128:(kb + 1) * 128, :], writes=['cktm%d' % kb])
            K.wslab = K.wslabA
            K.pre_slab = load_w(K, [(I.w_in[:, 0:256], 256), (I.w_in[:, 512:768], 256), (I.w_in[:, 1024:1280], 256)])
            rc = K.rc
            fw.dma('pool', rc['W2b'][:], I.w2.rearrange("d l c -> (d l) c"), writes=['W2b'])
            fw.dma('pool', rc['A2b'][:], I.a2.rearrange("d l c -> (d l) c"), writes=['A2b'])
            fw.dma('pool', rc['G2b'][:], I.g2[:, :], writes=['G2b'])
            fw.dma('pool', rc['mqb'][:].rearrange("p a b -> p (a b)"), I.mq[:, :], writes=['mqb'])
            fw.dma('pool', rc['mbb'][:].rearrange("p a b -> p (a b)"), I.mb[:, :], writes=['mbb'])
            fw.dma('pool', rc['rst'][:], I.rst[:, :], writes=['rst'])
            fw.dma('sp', rc['xi_f'][:], I.xi[:, :], writes=['xi_f'])
            fw.dma('sp', rc['bd1_f'][:], I.bd1[:, :], writes=['bd1_f'])
            rstds = [sb("rstdF%d" % i, [128, 512]) for i in range(3)]
            tmpn = sb("tmpn", [128, 2, 512])
            hkeys = [('xTh', c) for c in range(0, 512, 128)]
            nblocks = ((xT, xkeys, 0, 512, hT, 0, 'hT0', 0), (xT, xkeys, 512, 512, hT, 512, 'hT512', 1), (xTh, hkeys, 0, 448, hTh, 0, 'hT1024', 1))
            for bi_, (src, skeys, scol, n, dst, dcol, okey, cond) in enumerate(nblocks):
                rms_stats(K, sq, rstds[bi_], src, skeys, scol, n, 'rstd%d' % bi_)
            for bi_, (src, skeys, scol, n, dst, dcol, okey, cond) in enumerate(nblocks):
                rms_apply(K, rstds[bi_], tmpn, src, skeys, scol, n, dst, dcol, [okey],
                          lambda c, cond=cond: K.AB[:, 0, c, cond:cond + 1],
                          lambda c, cond=cond: K.MOD[:, c, cond:cond + 1], 'rstd%d' % bi_)
            fw.op('dve', lambda e: e.tensor_copy(out=hT[:, :, 1024:1026], in_=hTh[:, :, 255:257]), reads=['hT1024'], writes=['hTx'])
        with Scope(K):
            emit_attention(K, stage)
        hscope.__exit__(None, None, None)
        with Scope(K):
            emit_rwkv(K, stage)
    if stage <= 3:
        return
    emit_back(K, stage)


def rmsnorm_block(K, sq, rstd, tmpn, src, src_keys, scol, n, dst, dcol, out_keys, Asc, Bsc, rkey='rstd'):
    rms_stats(K, sq, rstd, src, src_keys, scol, n, rkey)
    rms_apply(K, rstd, tmpn, src, src_keys, scol, n, dst, dcol, out_keys, Asc, Bsc, rkey)


def rms_stats(K, sq, rstd, src, src_keys, scol, n, rkey='rstd'):
    fw = K.fw
    for c in range(8):
        fw.op('act', lambda e, c=c: e.activation(out=sq[:, c, 0:n], in_=src[:, c, scol:scol + n], func=AF.Square),
              reads=src_keys, writes=[('sq', c)])
    ps, pk = bank(K)
    for c in range(8):
        fw.op('pe', lambda t, c=c, ps=ps: t.matmul(ps[:, 0:n], K.ones_b[:], sq[:, c, 0:n], start=(c == 0), stop=(c == 7)),
              reads=[('sq', c), 'ones_b'], writes=[pk])
    fw.op('act', lambda e, ps=ps: e.activation(out=rstd[:, 0:n], in_=ps[:, 0:n], func=AF.Ln,
                                               bias=K.epsc[:, 0:1], scale=1.0 / D),
          reads=[pk, 'epsc'], writes=[rkey])
    fw.op('act', lambda e: e.activation(out=rstd[:, 0:n], in_=rstd[:, 0:n], func=AF.Exp, scale=-0.5), reads=[rkey], writes=[rkey])


def rms_apply(K, rstd, tmpn, src, src_keys, scol, n, dst, dcol, out_keys, Asc, Bsc, rkey='rstd'):
    fw = K.fw
    for c in range(8):
        fw.op('dve', lambda e, c=c: e.tensor_tensor(out=tmpn[:, c % 2, 0:n], in0=src[:, c, scol:scol + n], in1=rstd[:, 0:n], op=ALU.mult),
              reads=src_keys + [rkey], writes=[('tmpn', c % 2)])
        if Bsc is not None:
            fw.op('act', lambda e, c=c: e.activation(out=dst[:, c, dcol:dcol + n], in_=tmpn[:, c % 2, 0:n], func=AF.Identity,
                                                     bias=Bsc(c), scale=Asc(c)),
                  reads=[('tmpn', c % 2), 'MOD', 'AB', 'PV'], writes=out_keys)
        else:
            fw.op('act', lambda e, c=c: e.activation(out=dst[:, c, dcol:dcol + n], in_=tmpn[:, c % 2, 0:n], func=AF.Identity,
                                                     scale=Asc(c)),
                  reads=[('tmpn', c % 2), 'MOD', 'AB', 'PV'], writes=out_keys)


def emit_attention(K, stage=99):
    nc, fw, I, O, sb = K.nc, K.fw, K.I, K.O, K.sb
    hT, mixT = K.hT, K.mixT
    hsel = lambda k, col0, n: hT[:, k, col0:col0 + n] if col0 < 1024 else K.hTh[:, k, col0 - 1024:col0 - 1024 + n]
    kctx = sb("kctx", [128, 4, 512], BF16)
    cktm = K.cktm
    vctx, MB, RB = K.vctx, K.MB, K.RB
    K.wslab = K.wslabA
    for kb in range(4):
        t = cktm[kb]; tk = 'cktm%d' % kb
        ps, pk = bank(K)
        for p in range(4):
            fw.op('pe', lambda tt, t=t, p=p, ps=ps: tt.transpose(ps[:, p * 128:(p + 1) * 128], t[:, p * 128:(p + 1) * 128], K.ident_f[:]),
                  reads=[tk, 'ident_f'], writes=[pk])
        evac(K, 'act' if kb % 2 == 0 else 'dve', kctx[:, :, kb * 128:(kb + 1) * 128],
             ps[:].rearrange("p (c t) -> p c t", c=4), [pk], ['kctx'])
    HK = ['hT0', 'hT512', 'hT1024']
    QP = sb("QP", [128, 2, 2, 2, 256], BF16)
    QS = sb("QS", [128, 2, 8, 2, 64], BF16)
    kP = sb("kP", [128, 2, 512], BF16)
    kS = sb("kS", [128, 2, 1536], BF16)
    vP = sb("vP", [128, 4, 256], BF16)
    vS = sb("vS", [128, 12, 256], BF16)
    E = [sb("E%d" % i, [128, 12, 256], BF16) for i in range(2)]
    EP = sb("EP", [128, 2, 2, 512], BF16)
    rec = sb("rec", [128, 512])
    recS = [sb("recS%d" % i, [128, 256]) for i in range(2)]
    stkv = [sb("stkv%d" % i, [128, 512]) for i in range(2)]
    fw.op('act', lambda e: e.activation(out=MB[:].rearrange("p a b h j -> p (a b h j)"),
                                        in_=MB[:].rearrange("p a b h j -> p (a b h j)"), func=AF.Exp),
          reads=['MB'], writes=['MB'])
    fw.op('pool', lambda e: e.memset(QP[:].rearrange("p a b c d -> p (a b c d)"), 0.0), writes=['QP'])
    fw.op('pool', lambda e: e.memset(QS[:].rearrange("p a b c d -> p (a b c d)"), 0.0), writes=['QS'])
    fw.op('pool', lambda e: e.memset(kS[:].rearrange("p a b -> p (a b)"), 0.0), writes=['kS'])
    fw.op('pool', lambda e: e.memset(vS[:].rearrange("p a b -> p (a b)"), 0.0), writes=['vS'])
    ecount = 0
    if stage < 1.15:
        return
    for hh in range(2):
        if hh == 0:
            slab, sk = K.pre_slab
        else:
            slab, sk = load_w(K, [(I.w_in[:, hh * 256:hh * 256 + 256], 256),
                                  (I.w_in[:, 512 + hh * 256:512 + hh * 256 + 256], 256),
                                  (I.w_in[:, 1024 + hh * 256:1024 + hh * 256 + 256], 256)])
        for pi in range(2):
            for (col0, isS) in ((0, False), (512, True)):
                ps, pk = bank(K)
                for k in range(8):
                    fw.op('pe', lambda t, k=k, ps=ps, pi=pi, col0=col0: t.matmul(
                        ps[:, 0:512], slab[:, k, pi * 128:(pi + 1) * 128], hT[:, k, col0:col0 + 512],
                        start=(k == 0), stop=(k == 7)), reads=[sk] + HK, writes=[pk])
                for h2 in range(2):
                    pr = slice(h2 * 64, h2 * 64 + 64)
                    if not isS:
                        evac(K, 'act' if h2 == 0 else 'dve', QP[pr, pi, :, h2, :],
                             ps[pr, :].rearrange("p (s t) -> p s t", s=2), [pk], ['QP'])
                    else:
                        evac(K, 'act' if h2 == 0 else 'dve', QS[pr, pi, :, h2, :],
                             ps[pr, :].rearrange("p (l t) -> p l t", l=8), [pk], ['QS'])
        for pi in range(2):
            for (col0, n, kind) in ((0, 512, 'P'), (512, 512, 'S'), (1024, 448, 'H')):
                ps, pk = bank(K)
                for k in range(8):
                    fw.op('pe', lambda t, k=k, ps=ps, pi=pi, col0=col0, n=n: t.matmul(
                        ps[:, 0:n], slab[:, k, 256 + pi * 128:256 + (pi + 1) * 128], hsel(k, col0, n),
                        start=(k == 0), stop=(k == 7)), reads=[sk] + HK, writes=[pk])
                if kind == 'P':
                    evac(K, 'act', kP[:, pi, :], ps[:, 0:512], [pk], ['kP'])
                elif kind == 'S':
                    evac(K, 'dve', kS[:, pi, 512:1024], ps[:, 0:512], [pk], ['kS'])
                else:
                    evac(K, 'act', kS[:, pi, 256:512], ps[:, 0:256], [pk], ['kS'])
                    evac(K, 'dve', kS[:, pi, 1024:1216], ps[:, 256:448], [pk], ['kS'])
        for tb in range(4):
            ps, pk = bank(K)
            for k in range(8):
                fw.op('pe', lambda t, k=k, ps=ps, tb=tb: t.matmul(
                    ps[:, 0:512], hT[:, k, tb * 128:(tb + 1) * 128], slab[:, k, 256:768],
                    start=(k == 0), stop=(k == 7)), reads=[sk] + HK, writes=[pk])
            st = stkv[tb % 2]; stk = 'stkv%d' % (tb % 2)
            evac(K, 'act', st[:], ps[:, 0:512], [pk], [stk])
            evac(K, 'dve', vP[:, tb, :], ps[:, 256:512], [pk], ['vP'])
            fw.dma('sp', O.nk[tb * 128:(tb + 1) * 128, hh * 256:(hh + 1) * 256], st[:, 0:256], reads=[stk])
            fw.dma('sp', O.nv[tb * 128:(tb + 1) * 128, hh * 256:(hh + 1) * 256], st[:, 256:512], reads=[stk])
        vsrc = [(4 + b, 512 + b * 128, 128) for b in range(4)] + [(2, 1024, 128), (3, 1152, 128), (8, 1280, 128), (9, 1408, 64)]
        for gi in range(0, 8, 2):
            ps, pk = bank(K)
            for gj in range(2):
                blk, col0, n = vsrc[gi + gj]
                for k in range(8):
                    fw.op('pe', lambda t, k=k, ps=ps, gj=gj, col0=col0, n=n: t.matmul(
                        ps[0:n, gj * 256:(gj + 1) * 256], hsel(k, col0, n), slab[:, k, 512:768],
                        start=(k == 0), stop=(k == 7)), reads=[sk] + HK, writes=[pk])
            for gj in range(2):
                blk, col0, n = vsrc[gi + gj]
                evac(K, 'act' if gj == 0 else 'dve', vS[0:n, blk, :], ps[0:n, gj * 256:(gj + 1) * 256], [pk], ['vS'])
        if stage < 1.25:
            continue
        for seq in range(2):
            for pi in range(2):
                p = 2 * hh + pi
                for kb in range(2):
                    ps, pk = bank(K)
                    fw.op('pe', lambda t, ps=ps, pi=pi, seq=seq, kb=kb: t.matmul(
                        ps[:, 0:512], kP[:, pi, seq * 256 + kb * 128:seq * 256 + (kb + 1) * 128],
                        QP[:, pi, seq, :, :].rearrange("p a q -> p (a q)"), start=True, stop=True),
                        reads=['kP', 'QP'], writes=[pk])
                    fw.op('act', lambda e, ps=ps, pi=pi, kb=kb: e.activation(out=EP[:, pi, kb, :], in_=ps[:, 0:512], func=AF.Exp, scale=0.125),
                          reads=[pk], writes=[('EP', pi, kb)])
                psd, pdk = bank(K)
                for kb in range(2):
                    fw.op('pe', lambda t, psd=psd, pi=pi, kb=kb: t.matmul(psd[:, 0:512], K.ones_b[:], EP[:, pi, kb, :], start=(kb == 0), stop=(kb == 1)),
                          reads=[('EP', pi, kb), 'ones_b'], writes=[pdk])
                psn, pnk = bank(K)
                for kb in range(2):
                    fw.op('pe', lambda t, psn=psn, pi=pi, kb=kb, seq=seq: t.matmul(
                        psn[:, 0:512], vP[:, seq * 2 + kb, pi * 128:(pi + 1) * 128], EP[:, pi, kb, :], start=(kb == 0), stop=(kb == 1)),
                        reads=[('EP', pi, kb), 'vP'], writes=[pnk])
                fw.op('act', lambda e, psd=psd: e.activation(out=rec[:, 0:512], in_=psd[:, 0:512], func=AF.Ln), reads=[pdk], writes=['rec'])
                fw.op('act', lambda e: e.activation(out=rec[:, 0:512], in_=rec[:, 0:512], func=AF.Exp, scale=-1.0), reads=['rec'], writes=['rec'])
                for h2 in range(2):
                    pr = slice(h2 * 64, h2 * 64 + 64)
                    fw.op('dve', lambda e, psn=psn, pr=pr, h2=h2, p=p, seq=seq: e.tensor_tensor(
                        out=mixT[pr, p, seq * 256:(seq + 1) * 256], in0=psn[pr, h2 * 256:(h2 + 1) * 256],
                        in1=rec[pr, h2 * 256:(h2 + 1) * 256], op=ALU.mult),
                        reads=[pnk, 'rec'], writes=[('mixT', p)])
        if stage < 1.35:
            continue
        for l in range(8):
            Et = E[ecount % 2]; ek = 'E%d' % (ecount % 2); ecount += 1
            par = l % 2
            b0 = (l + 1) // 2
            for g in range(6):
                ps, pk = bank(K)
                for gj in range(2):
                    blk = 2 * g + gj
                    for pi in range(2):
                        if blk < 8:
                            lhs = kS[:, pi, (b0 + blk) * 128:(b0 + blk + 1) * 128]
                            rk = 'kS'
                        else:
                            lhs = kctx[:, 2 * hh + pi, (blk - 8) * 128:(blk - 7) * 128]
                            rk = 'kctx'
                        fw.op('pe', lambda t, ps=ps, lhs=lhs, gj=gj, pi=pi, l=l: t.matmul(
                            ps[:, gj * 256 + pi * 128:gj * 256 + (pi + 1) * 128], lhs,
                            QS[:, pi, l, :, :].rearrange("p a q -> p (a q)"), start=True, stop=True),
                            reads=[rk, 'QS'], writes=[pk])
                for gj in range(2):
                    blk = 2 * g + gj
                    if blk < 8:
                        fw.op('act', lambda e, ps=ps, gj=gj, blk=blk, l=l, Et=Et: e.activation(
                            out=Et[:, blk, :], in_=ps[:, gj * 256:(gj + 1) * 256], func=AF.Exp, scale=0.125,
                            bias=RB[:, l * 8 + blk:l * 8 + blk + 1]), reads=[pk, 'RB'], writes=[(ek, blk)])
                    else:
                        fw.op('act', lambda e, ps=ps, gj=gj, blk=blk, Et=Et: e.activation(
                            out=Et[:, blk, :], in_=ps[:, gj * 256:(gj + 1) * 256], func=AF.Exp, scale=0.125),
                            reads=[pk], writes=[(ek, blk)])
            ekeys = [(ek, b) for b in range(12)]
            fw.op('dve', lambda e, Et=Et, par=par, hh=hh: e.tensor_tensor(
                out=Et[:, 0:8, :], in0=Et[:, 0:8, :],
                in1=MB[:, par, :, hh * 4:hh * 4 + 4, :].rearrange("p b h j -> p b (h j)"), op=ALU.mult),
                reads=ekeys[0:8] + ['MB'], writes=ekeys[0:8])
            rq = recS[(ecount - 1) % 2]; rqk = 'recS%d' % ((ecount - 1) % 2)
            psd, pdk = bank(K)
            for blk in range(12):
                fw.op('pe', lambda t, psd=psd, blk=blk, Et=Et: t.matmul(psd[:, 0:256], K.ones_b[:], Et[:, blk, :], start=(blk == 0), stop=(blk == 11)),
                      reads=[(ek, blk), 'ones_b'], writes=[pdk])
            psn, pnk = bank(K)
            for pi in range(2):
                for blk in range(12):
                    if blk < 8:
                        lhs = vS[:, b0 + blk, pi * 128:(pi + 1) * 128]; rk = 'vS'
                    else:
                        lhs = vctx[:, blk - 8, hh * 256 + pi * 128:hh * 256 + (pi + 1) * 128]; rk = 'vctx'
                    fw.op('pe', lambda t, psn=psn, lhs=lhs, blk=blk, pi=pi, Et=Et: t.matmul(
                        psn[:, pi * 128:(pi + 1) * 128], lhs, Et[:, blk, pi * 128:(pi + 1) * 128],
                        start=(blk == 0), stop=(blk == 11)), reads=[(ek, blk), rk], writes=[pnk])
            fw.op('act', lambda e, psd=psd: e.activation(out=rq[:, 0:256], in_=psd[:, 0:256], func=AF.Ln), reads=[pdk], writes=[rqk])
            fw.op('act', lambda e: e.activation(out=rq[:, 0:256], in_=rq[:, 0:256], func=AF.Exp, scale=-1.0), reads=[rqk], writes=[rqk])
            for h2 in range(2):
                pr = slice(h2 * 64, h2 * 64 + 64)
                fw.op('dve', lambda e, psn=psn, pr=pr, h2=h2, hh=hh, l=l: e.tensor_tensor(
                    out=mixT[pr, 2 * hh:2 * hh + 2, 512 + l * 64:512 + (l + 1) * 64],
                    in0=psn[pr, 0:256].rearrange("p (a b) -> p a b", a=2)[:, :, h2 * 64:(h2 + 1) * 64],
                    in1=rq[pr, 0:256].rearrange("p (a b) -> p a b", a=2)[:, :, h2 * 64:(h2 + 1) * 64], op=ALU.mult),
                    reads=[pnk, rqk], writes=[('mixT', 2 * hh), ('mixT', 2 * hh + 1)])
    if 'mixA' in K.dbg:
        md = sb("mixdbg", [128, 4, 1024])
        fw.op('dve', lambda e: e.tensor_copy(out=md[:], in_=mixT[:, 0:4, :]), reads=[('mixT', p) for p in range(4)], writes=['mixdbg'])
        dump(K, "mixA", md[:], [128, 4, 1024], ['mixdbg'])


C0 = 0.6065306597126334


def emit_rwkv(K, stage):
    nc, fw, I, O, sb = K.nc, K.fw, K.I, K.O, K.sb
    hT, mixT, PV = K.hT, K.mixT, K.PV
    HK = ['hT0', 'hT512', 'hTx']
    yT = sb("yT", [128, 4, 1024])
    YP = sb("YP", [128, 4, 2, 4, 128], BF16)
    TXW = sb("TXW", [128, 1024], BF16); UXA = sb("UXA", [128, 1024], BF16); SXG = sb("SXG", [128, 1024], BF16)
    CC = sb("CC", [128, 4, 2, 128])
    S0T = sb("S0T", [128, 4, 2, 64])
    rc = K.rc
    W2b, A2b, G2b, mqb, mbb, rst, xi_f, bd1_f = (rc[k] for k in ('W2b', 'A2b', 'G2b', 'mqb', 'mbb', 'rst', 'xi_f', 'bd1_f'))
    OMK = sb("OMK", [128, 4])
    eps12 = sb("eps12", [128, 1])
    K.wslab = [sb("wslabR0", [128, 8, 384], BF16)] * 2
    fw.op('pool', lambda e: e.memset(eps12[:], 1e-12), writes=['eps12'])
    fw.op('dve', lambda e: e.tensor_scalar(out=OMK[:], in0=PV[:, PV_KA:PV_KA + 4], scalar1=-1.0, scalar2=1.0, op0=ALU.mult, op1=ALU.add),
          reads=['PV'], writes=['OMK'])
    with Scope(K):
        Z = [sb("Z%d" % i, [128, 1030], BF16) for i in range(3)]
        TT6 = sb("TT6", [128, 6, 512])
        T = [TT6[:, i, :] for i in range(6)]
        utmp = TT6[:, 0:2, :].rearrange("p a b -> p (a b)")
        Uset = [[sb("U%s%d" % (n, 0), [128, 1024], BF16) for n in ("r", "kr", "v")]] * 2
        RKK = sb("RKK", [128, 1024], BF16)
        Vaug = sb("Vaug", [128, 4, 2, 128], BF16)
        PSET = [dict(TL2=[sb("TL2_%d_%d" % (d, q), [128, 4, 2, 128], BF16) for d in range(2)],
                     TB=[sb("TB_%d_%d" % (d, q), [128, 512], BF16) for d in range(2)],
                     TK=[sb("TK_%d_%d" % (d, q), [128, 512], BF16) for d in range(2)],
                     TBK=[sb("TBK_%d_%d" % (d, q), [128, 4, 2, 128], BF16) for d in range(2)],
                     WC=[sb("WC_%d_%d" % (d, q), [128, 4]) for d in range(2)],
                     ) for q in range(2)]
        KKt1 = sb("KKt", [128, 512]); SQK1 = sb("SQK", [128, 512], BF16)
        for q in range(2):
            PSET[q]['KKt'] = KKt1; PSET[q]['SQK'] = SQK1
        AQ = [sb("AQ_%d" % d, [128, 8, 3, 128], BF16) for d in range(2)]
        NP = [sb("NP_%d" % d, [128, 8, 2, 128], BF16) for d in range(2)]; BB = [sb("BB_%d" % d, [128, 8, 128], BF16) for d in range(2)]
        NN = [NP[d][:, :, 0, :] for d in range(2)]
        PP = [NP[d][:, :, 1, :] for d in range(2)]
        ST32 = [sb("ST32_%d" % d, [128, 2, 128]) for d in range(2)]; STb = [sb("STb_%d" % d, [128, 2, 128], BF16) for d in range(2)]
        STW = [sb("STW_%d" % d, [128, 2, 128]) for d in range(2)]
        Xb = [sb("Xb_%d" % d, [128, 4, 128], BF16) for d in range(2)]; Ub = [sb("Ub_%d" % d, [128, 4, 128], BF16) for d in range(2)]
        stS = sb("stS", [128, 128])
        for zi in range(3):
            fw.op('pool', lambda e, zi=zi: e.memset(Z[zi][:, 0:516].rearrange("p (s t) -> p s t", s=2)[:, :, 0:258:257], 0.0), writes=['Z%d' % zi])

        def proj_mm(slab, sk, scol, zi):
            Zt = Z[zi]; zk = 'Z%d' % zi
            for (col0, kind) in ((0, 'P'), (512, 'S')):
                ps, pk = bank(K)
                for k in range(8):
                    fw.op('pe', lambda t, k=k, ps=ps, col0=col0: t.matmul(
                        ps[:, 0:512], slab[:, k, scol:scol + 128], hT[:, k, col0:col0 + 512], start=(k == 0), stop=(k == 7)),
                        reads=[sk] + HK, writes=[pk])
                if kind == 'P':
                    evac(K, 'act', Zt[:, 0:516].rearrange("p (s t) -> p s t", s=2)[:, :, 1:257],
                         ps[:, 0:512].rearrange("p (s t) -> p s t", s=2), [pk], [zk])
                else:
                    evac(K, 'act', Zt[:, 517:1029], ps[:, 0:512], [pk], [zk])
            ps, pk = bank(K)
            for k in range(8):
                fw.op('pe', lambda t, k=k, ps=ps: t.matmul(ps[:, 0:2], slab[:, k, scol:scol + 128], hT[:, k, 1024:1026], start=(k == 0), stop=(k == 7)),
                      reads=[sk] + HK, writes=[pk])
            fw.op('dve', lambda e, ps=ps: e.tensor_scalar(out=Zt[:, 516:517], in0=ps[:, 0:1], scalar1=K.cst[:, 12:13], scalar2=None, op0=ALU.mult),
                  reads=[pk, 'cst'], writes=[zk])
            fw.op('dve', lambda e, ps=ps: e.tensor_scalar(out=Zt[:, 1029:1030], in0=ps[:, 1:2], scalar1=K.cst[:, 13:14], scalar2=None, op0=ALU.mult),
                  reads=[pk, 'cst'], writes=[zk])

        def conv(zi, ci, dest, dkey, func=None):
            Zt = Z[zi]; zk = 'Z%d' % zi
            w = [PV[:, PV_WTS + j * 15 + ci:PV_WTS + j * 15 + ci + 1] for j in range(3)]
            zP = lambda off: Zt[:, 0:516].rearrange("p (s t) -> p s t", s=2)[:, :, off:off + 256]
            zS = lambda off: Zt[:, 516 + off:516 + off + 512]
            uP = utmp[:, 0:512].rearrange("p (s t) -> p s t", s=2)
            uS = utmp[:, 512:1024]
            final = dest if func is None else utmp
            dP = final[:, 0:512].rearrange("p (s t) -> p s t", s=2)
            dS = final[:, 512:1024]
            fkeys = [dkey] if func is None else ['T0', 'T1']
            for (zv, uv, dv) in ((zP, uP, dP), (zS, uS, dS)):
                fw.op('act', lambda e, zv=zv, uv=uv: e.activation(out=uv, in_=zv(0), func=AF.Identity, scale=w[0]), reads=[zk, 'PV'], writes=['T0', 'T1'])
                fw.op('dve', lambda e, zv=zv, uv=uv: e.scalar_tensor_tensor(out=uv, in0=zv(1), scalar=w[1], in1=uv, op0=ALU.mult, op1=ALU.add),
                      reads=[zk, 'PV', 'T0', 'T1'], writes=['T0', 'T1'])
                fw.op('dve', lambda e, zv=zv, uv=uv, dv=dv: e.scalar_tensor_tensor(out=dv, in0=zv(2), scalar=w[2], in1=uv, op0=ALU.mult, op1=ALU.add),
                      reads=[zk, 'PV', 'T0', 'T1'], writes=fkeys)
            if func is not None:
                fw.op('act', lambda e: e.activation(out=dest[:], in_=utmp[:], func=func), reads=['T0', 'T1'], writes=[dkey])

        slab, sk = load_w(K, [(I.w_in[:, 3072:3456], 384)])
        for zi in range(3):
            proj_mm(slab, sk, zi * 128, zi)
        conv(0, 12, TXW, 'TXW', AF.Tanh)
        conv(1, 13, UXA, 'UXA')
        conv(2, 14, SXG, 'SXG', AF.Sigmoid)

        def load_pair(p):
            return load_w(K, [(I.w_in[:, 1536 + p * 128:1536 + (p + 1) * 128], 128),
                              (I.w_in[:, 2048 + p * 128:2048 + (p + 1) * 128], 128),
                              (I.w_in[:, 2560 + p * 128:2560 + (p + 1) * 128], 128)])

        def proj_pair(slab, sk):
            for zi in range(3):
                proj_mm(slab, sk, zi * 128, zi)

        def conv_pair(p):
            us = Uset[p % 2]
            for zi, nm in enumerate(("r", "kr", "v")):
                conv(zi, zi * 4 + p, us[zi], 'U%s0' % nm)

        slab, sk = load_pair(0)
        proj_pair(slab, sk)
        conv_pair(0)
        Ur, Ukr, Uv = Uset[0]
        ukeys = {'Ur': 'Ur0', 'Ukr': 'Ukr0', 'Uv': 'Uv0'}
        base = dict(T=T, Ur=Ur, Ukr=Ukr, Uv=Uv, RKK=RKK, TXW=TXW, UXA=UXA, W2b=W2b, A2b=A2b, rst=rst, OMK=OMK, PSET=PSET, AQ=AQ, NN=NN, BB=BB, PP=PP, NP=NP,
                    ST32=ST32, STb=STb, STW=STW, Xb=Xb, Ub=Ub, Vaug=Vaug, YP=YP, yT=yT, CC=CC, mqb=mqb, mbb=mbb, xi_f=xi_f, stS=stS, eps12=eps12, ukeys=ukeys)

        def mkL(it):
            L = dict(base)
            hf = 1 - it % 2
            L.update(p=it // 2, half=hf, par=it % 2, hs=slice(hf * 512, hf * 512 + 512))
            return L

        def bonus(p):
            for hb in range(2):
                ps, pk = bank(K)
                fw.op('pe', lambda t, ps=ps, hb=hb: t.matmul(ps[:, 0:512], K.bd1_b[:], RKK[:, hb * 512:(hb + 1) * 512], start=True, stop=True),
                      reads=['RKK', 'bd1_b'], writes=[pk])
                fw.op('dve', lambda e, ps=ps, hb=hb: e.tensor_tensor(out=mixT[:, 4 + p, hb * 512:(hb + 1) * 512], in0=ps[:, 0:512],
                                                                      in1=Uv[:, hb * 512:(hb + 1) * 512], op=ALU.mult),
                      reads=[pk, 'Uv0'], writes=[('mixT', 4 + p)])

        def vaug(half):
            fw.op('pool', lambda e: e.memset(Vaug[:].rearrange("p a b c -> p (a b c)"), 0.0), writes=['Vaug'])
            ps, pk = bank(K)
            psb = ps[:].bitcast(BF16)
            for c in range(4):
                fw.op('pe', lambda t, c=c, psb=psb: t.transpose(psb[:, c * 128:(c + 1) * 128], Uv[:, half * 512 + c * 128:half * 512 + (c + 1) * 128], K.ident_b[:]),
                      reads=['Uv0', 'ident_b'], writes=[pk])
            for h2 in range(2):
                evac(K, 'act' if h2 == 0 else 'dve', Vaug[:, :, h2, h2 * 64:(h2 + 1) * 64],
                     psb[:, 0:512].rearrange("p (c f) -> p c f", c=4)[:, :, h2 * 64:(h2 + 1) * 64], [pk], ['Vaug'])

        ops0 = rwkv_prep(K, mkL(0))
        fw.replay(ops0, len(ops0))
        nslab = nsk = None
        cc_in = nc.dram_tensor("cc_in", [128, 1024], F32)
        cc_out = nc.dram_tensor("cc_out", [512, 1024], F32)
        for it in range(8):
            p, second = it // 2, it % 2
            L = mkL(it)
            if second == 0 and p < 3:
                nslab, nsk = load_pair(p + 1)
                L['hook_prep'] = lambda nslab=nslab, nsk=nsk: proj_pair(nslab, nsk)

            def hook_mid(L=L, it=it, p=p, second=second):
                if second == 1:
                    bonus(p)
                    if p < 3:
                        conv_pair(p + 1)
                return rwkv_prep(K, mkL(it + 1)) if it < 7 else []
            L['hook_mid'] = hook_mid
            L['vaug'] = vaug
            rwkv_core(K, L)
            if it == 6:
                fw.dma('pool', cc_in.ap(), CC[:].rearrange("p a b c -> p (a b c)"), reads=['CC'], writes=['cc_in'])
                fw.async_op('pool', lambda g: g.collective_compute("AllGather", ALU.bypass, replica_groups=[[0, 1, 2, 3], [4, 5, 6, 7]],
                                                             ins=[cc_in.ap().opt()], outs=[cc_out.ap().opt()]), reads=['cc_in'], writes=['cc_out'])
    if stage < 3:
        if 'yT' in K.dbg:
            dump(K, "yT", yT[:], [128, 4, 1024], [('yT%d' % p, t_) for p in range(4) for t_ in range(0, 1024, 128)])
            dump(K, "CC", CC[:], [128, 4, 2, 128], ['CC'])
        return
    rwkv_finish(K, locals())


def rwkv_prep(K, L):
    fw = K.fw
    PV = K.PV
    p, half, hs = L['p'], L['half'], L['hs']
    uk = L['ukeys']
    T, Ur, Ukr, Uv, RKK = L['T'], L['Ur'], L['Ukr'], L['Uv'], L['RKK']
    TXW, UXA, W2b, A2b, rst, OMK = L['TXW'], L['UXA'], L['W2b'], L['A2b'], L['rst'], L['OMK']
    par = L['par']
    PS = L['PSET'][par]
    TL2s, TBs, TKs, TBKs, WCs = PS['TL2'], PS['TB'], PS['TK'], PS['TBK'], PS['WC']
    AQs, NNs, BBs, PPs = L['AQ'], L['NN'], L['BB'], L['PP']
    ST32s, STbs, STWs, Xbs, Ubs = L['ST32'], L['STb'], L['STW'], L['Xb'], L['Ub']
    Vaug, YP, yT, CC, mqb, mbb, xi_f, stS = (L[k] for k in ('Vaug', 'YP', 'yT', 'CC', 'mqb', 'mbb', 'xi_f', 'stS'))
    O = K.O
    SG, LC, LX, AT, TE, T2 = T[0], T[1], T[2], T[3], T[4], T[5]
    kn = lambda base, d: ('%s_%d_%d' % (base, d, par)) if base in ('TL2', 'TB', 'TK', 'TBK', 'WC') else ('%s_%d' % (base, d))
    KKt, SQK = PS['KKt'], PS['SQK']
    kkk = 'KKt'
    eps12 = L['eps12']
    fw.capture = []

    def prep_kk():
        fw.op('dve', lambda e: e.tensor_scalar(out=KKt[:], in0=Ukr[:, hs], scalar1=PV[:, PV_KK + p:PV_KK + p + 1], scalar2=None, op0=ALU.mult),
              reads=[uk['Ukr'], 'PV'], writes=[kkk])
        fw.op('act', lambda e: e.activation(out=SQK[:], in_=KKt[:], func=AF.Square), reads=[kkk], writes=['SQK'])
        lb = LazyBank(K)
        fw.op('pe', lambda t: t.matmul(lb.ps[:, 0:512], K.bd1_b[:], SQK[:], start=True, stop=True), reads=['SQK', 'bd1_b'], writes=[lb])
        fw.op('act', lambda e: e.activation(out=T[5][:], in_=lb.ps[:, 0:512], func=AF.Ln, bias=eps12[:, 0:1], scale=1.0),
              reads=[lb, 'eps12'], writes=['T5'])
        fw.op('act', lambda e: e.activation(out=T[5][:], in_=T[5][:], func=AF.Exp, scale=-0.5), reads=['T5'], writes=['T5'])
        fw.op('dve', lambda e: e.tensor_tensor(out=KKt[:], in0=KKt[:], in1=T[5][:], op=ALU.mult), reads=[kkk, 'T5'], writes=[kkk])

    def prep_dir(d):
        dr = slice(d * 64, d * 64 + 64)
        TL2, TB, TK, WC = TL2s[d], TBs[d], TKs[d], WCs[d]
        lb = LazyBank(K)
        fw.op('pe', lambda t: t.matmul(lb.ps[:, 0:512], W2b[dr, p * 128:(p + 1) * 128], TXW[dr, hs], start=True, stop=True), reads=['W2b', 'TXW'], writes=[lb])
        fw.op('act', lambda e: e.activation(out=SG[:], in_=lb.ps[:, 0:512], func=AF.Sigmoid, bias=PV[:, PV_W0 + d * 4 + p:PV_W0 + d * 4 + p + 1], scale=1.0),
              reads=[lb, 'PV'], writes=['T0'])
        lb2 = LazyBank(K)
        fw.op('pe', lambda t: t.matmul(lb2.ps[:, 0:512], A2b[dr, p * 128:(p + 1) * 128], UXA[dr, hs], start=True, stop=True), reads=['A2b', 'UXA'], writes=[lb2])
        fw.op('act', lambda e: e.activation(out=AT[:], in_=lb2.ps[:, 0:512], func=AF.Sigmoid, bias=PV[:, PV_A0 + d * 4 + p:PV_A0 + d * 4 + p + 1], scale=1.0),
              reads=[lb2, 'PV'], writes=['T3'])
        fw.op('dve', lambda e: e.tensor_tensor_scan(out=LC[:], data0=rst[:], data1=SG[:], initial=0.0, op0=ALU.mult, op1=ALU.add),
              reads=['rst', 'T0'], writes=['T1'])
        if d == 0:
            fw.op('dve', lambda e: e.tensor_tensor(out=LX[:], in0=LC[:], in1=SG[:], op=ALU.subtract), reads=['T1', 'T0'], writes=['T2'])
            LI, lik = LC, 'T1'
        else:
            for c in range(4):
                cs = slice(c * 128, (c + 1) * 128)
                fw.op('dve', lambda e, cs=cs, c=c: e.tensor_scalar(out=LX[:, cs], in0=LC[:, cs], scalar1=-1.0, scalar2=LC[:, c * 128 + 127:c * 128 + 128],
                                                                   op0=ALU.mult, op1=ALU.add), reads=['T1'], writes=['T2'])
            fw.op('dve', lambda e: e.tensor_tensor(out=SG[:], in0=LX[:], in1=SG[:], op=ALU.add), reads=['T2', 'T0'], writes=['T0'])
            LI, lik = SG, 'T0'
        fw.op('act', lambda e: e.activation(out=TE[:], in_=LI[:], func=AF.Exp, scale=-C0), reads=[lik], writes=['T4'])
        wcol = 127 if d == 0 else 0
        fw.op('dve', lambda e: e.tensor_copy(out=WC[:], in_=TE[:, wcol:512:128]), reads=['T4'], writes=[kn('WC', d)])
        fw.op('dve', lambda e: e.tensor_tensor(out=TL2[:, :, 1, :], in0=Ur[:, hs].rearrange("p (c t) -> p c t", c=4),
                                               in1=TE[:].rearrange("p (c t) -> p c t", c=4), op=ALU.mult), reads=[uk['Ur'], 'T4'], writes=[kn('TL2', d)])
        fw.op('act', lambda e: e.activation(out=TE[:], in_=LX[:], func=AF.Exp, scale=-C0), reads=['T2'], writes=['T4'])
        fw.op('dve', lambda e: e.tensor_tensor(out=TL2[:, :, 0, :], in0=KKt[:].rearrange("p (c t) -> p c t", c=4),
                                               in1=TE[:].rearrange("p (c t) -> p c t", c=4), op=ALU.mult), reads=[kkk, 'T4'], writes=[kn('TL2', d)])
        fw.op('act', lambda e: e.activation(out=TE[:], in_=LI[:], func=AF.Exp, scale=C0), reads=[lik], writes=['T4'])
        fw.op('dve', lambda e: e.tensor_tensor(out=T2[:], in0=KKt[:], in1=AT[:], op=ALU.mult), reads=[kkk, 'T3'], writes=['T5'])
        fw.op('dve', lambda e: e.tensor_tensor(out=TB[:], in0=T2[:], in1=TE[:], op=ALU.mult), reads=['T5', 'T4'], writes=[kn('TB', d)])
        fw.op('dve', lambda e: e.tensor_scalar(out=AT[:], in0=AT[:], scalar1=PV[:, PV_KA + p:PV_KA + p + 1], scalar2=OMK[:, p:p + 1], op0=ALU.mult, op1=ALU.add),
              reads=['T3', 'PV', 'OMK'], writes=['T3'])
        fw.op('dve', lambda e: e.tensor_tensor(out=T2[:], in0=Ukr[:, hs], in1=AT[:], op=ALU.mult), reads=[uk['Ukr'], 'T3'], writes=['T5'])
        fw.op('dve', lambda e: e.tensor_tensor(out=TK[:], in0=T2[:], in1=TE[:], op=ALU.mult), reads=['T5', 'T4'], writes=[kn('TK', d)])
        if d == 0:
            fw.op('dve', lambda e: e.scalar_tensor_tensor(out=RKK[:, hs], in0=Ur[:, hs], scalar=PV[:, PV_RK + p:PV_RK + p + 1], in1=T2[:], op0=ALU.mult, op1=ALU.mult),
                  reads=[uk['Ur'], 'PV', 'T5'], writes=['RKK'])
        else:
            fw.op('dve', lambda e: e.scalar_tensor_tensor(out=T2[:], in0=Ur[:, hs], scalar=PV[:, PV_RK + p:PV_RK + p + 1], in1=T2[:], op0=ALU.mult, op1=ALU.mult),
                  reads=[uk['Ur'], 'PV', 'T5'], writes=['T5'])
            fw.op('dve', lambda e: e.tensor_tensor(out=RKK[:, hs], in0=RKK[:, hs], in1=T2[:], op=ALU.add), reads=['RKK', 'T5'], writes=['RKK'])

    prep_kk()
    for d in range(2):
        prep_dir(d)
    ops = fw.capture
    fw.capture = None
    return ops


def rwkv_prepB(K, L):
    fw = K.fw
    PV = K.PV
    p, half, hs = L['p'], L['half'], L['hs']
    uk = L['ukeys']
    T, Ur, Ukr, Uv, RKK = L['T'], L['Ur'], L['Ukr'], L['Uv'], L['RKK']
    TXW, UXA, W2b, A2b, rst, OMK = L['TXW'], L['UXA'], L['W2b'], L['A2b'], L['rst'], L['OMK']
    par = L['par']
    PS = L['PSET'][par]
    TL2s, TBs, TKs, TBKs, WCs = PS['TL2'], PS['TB'], PS['TK'], PS['TBK'], PS['WC']
    AQs, NNs, BBs, PPs = L['AQ'], L['NN'], L['BB'], L['PP']
    ST32s, STbs, STWs, Xbs, Ubs = L['ST32'], L['STb'], L['STW'], L['Xb'], L['Ub']
    Vaug, YP, yT, CC, mqb, mbb, xi_f, stS = (L[k] for k in ('Vaug', 'YP', 'yT', 'CC', 'mqb', 'mbb', 'xi_f', 'stS'))
    O = K.O
    SG, LC, LX, AT, TE, T2 = T[0], T[1], T[2], T[3], T[4], T[5]
    kn = lambda base, d: ('%s_%d_%d' % (base, d, par)) if base in ('TL2', 'TB', 'TK', 'TBK', 'WC') else ('%s_%d' % (base, d))
    for d in range(2):
        TB, TK = TBs[d], TKs[d]
        ps, pk = bank(K)
        psb = ps[:].bitcast(BF16)
        for c in range(4):
            for a, (src, skey) in enumerate(((TB, kn('TB', d)), (TK, kn('TK', d)))):
                fw.op('pe', lambda t, c=c, a=a, src=src: t.transpose(psb[:, (c * 2 + a) * 128:(c * 2 + a + 1) * 128], src[:, c * 128:(c + 1) * 128], K.ident_b[:]),
                      reads=[skey, 'ident_b'], writes=[pk])
        evac(K, 'act', TBKs[d][:].rearrange("p c a t -> p (c a t)"), psb[:, 0:1024], [pk], [kn('TBK', d)])


def rwkv_core(K, L):
    fw = K.fw
    PV = K.PV
    p, half, hs = L['p'], L['half'], L['hs']
    uk = L['ukeys']
    T, Ur, Ukr, Uv, RKK = L['T'], L['Ur'], L['Ukr'], L['Uv'], L['RKK']
    TXW, UXA, W2b, A2b, rst, OMK = L['TXW'], L['UXA'], L['W2b'], L['A2b'], L['rst'], L['OMK']
    par = L['par']
    PS = L['PSET'][par]
    TL2s, TBs, TKs, TBKs, WCs = PS['TL2'], PS['TB'], PS['TK'], PS['TBK'], PS['WC']
    AQs, NNs, BBs, PPs = L['AQ'], L['NN'], L['BB'], L['PP']
    ST32s, STbs, STWs, Xbs, Ubs = L['ST32'], L['STb'], L['STW'], L['Xb'], L['Ub']
    Vaug, YP, yT, CC, mqb, mbb, xi_f, stS = (L[k] for k in ('Vaug', 'YP', 'yT', 'CC', 'mqb', 'mbb', 'xi_f', 'stS'))
    O = K.O
    SG, LC, LX, AT, TE, T2 = T[0], T[1], T[2], T[3], T[4], T[5]
    kn = lambda base, d: ('%s_%d_%d' % (base, d, par)) if base in ('TL2', 'TB', 'TK', 'TBK', 'WC') else ('%s_%d' % (base, d))
    rwkv_prepB(K, L)
    L['vaug'](half)
    if L.get('hook_prep'):
        L['hook_prep']()
    for d in range(2):
        TL2, TB, TK, AQ, NN, BB = TL2s[d], TBs[d], TKs[d], AQs[d], NNs[d], BBs[d]
        tl = TL2[:].rearrange("p c a t -> p c (a t)")
        for c in range(4):
            for h2 in range(2):
                u = c * 2 + h2
                pr = slice(h2 * 64, h2 * 64 + 64)
                ps, pk = bank(K)
                fw.op('pe', lambda t, ps=ps, c=c, pr=pr: t.matmul(ps[:, 0:256], TB[pr, c * 128:(c + 1) * 128], tl[pr, c, :], start=True, stop=True),
                      reads=[kn('TB', d), kn('TL2', d)], writes=[pk])
                fw.op('pe', lambda t, ps=ps, c=c, pr=pr: t.matmul(ps[:, 256:512], TK[pr, c * 128:(c + 1) * 128], tl[pr, c, :], start=True, stop=True),
                      reads=[kn('TK', d), kn('TL2', d)], writes=[pk])
                fw.op('dve', lambda e, ps=ps, u=u: e.tensor_tensor(out=NN[:, u, :], in0=ps[:, 0:128], in1=mqb[:, d, 0:128], op=ALU.mult),
                      reads=[pk, 'mqb'], writes=[(kn('NN', d), u // 4)])
                fw.op('act', lambda e, ps=ps, u=u: e.activation(out=AQ[:, u, :, :].rearrange("p a t -> p (a t)"), in_=ps[:, 128:512], func=AF.Copy),
                      reads=[pk], writes=[(kn('AQ', d), u)])
                fw.op('pool', lambda e, u=u: e.tensor_tensor(out=AQ[:, u, :, :].rearrange("p a t -> p (a t)"), in0=AQ[:, u, :, :].rearrange("p a t -> p (a t)"),
                                                             in1=mqb[:, d, 128:512], op=ALU.mult),
                      reads=[(kn('AQ', d), u), 'mqb'], writes=[(kn('AQ', d), u)])
        for h2 in range(2):
            pr = slice(h2 * 64, h2 * 64 + 64)
            ps, pk = bank(K)
            for c in range(4):
                fw.op('pe', lambda t, ps=ps, c=c, pr=pr: t.matmul(ps[:, c * 128:(c + 1) * 128], TL2[pr, c, 0, :], TB[pr, c * 128:(c + 1) * 128], start=True, stop=True),
                      reads=[kn('TB', d), kn('TL2', d)], writes=[pk])
            fw.op('dve', lambda e, ps=ps, h2=h2: e.tensor_tensor(out=BB[:, h2:8:2, :], in0=ps[:, 0:512].rearrange("p (c t) -> p c t", c=4),
                                                                in1=mbb[:, d:d + 1, :].broadcast_to([128, 4, 128]), op=ALU.mult),
                  reads=[pk, 'mbb'], writes=[(kn('BB', d), 0), (kn('BB', d), 1)])
    if L.get('hook_quad'):
        L['hook_quad']()
    for d in range(2):
        for g in range(2):
            gs = slice(g * 4, g * 4 + 4)
            fw.op('dve', lambda e, gs=gs, d=d: e.tensor_tensor(out=PPs[d][:, gs, :], in0=NNs[d][:, gs, :], in1=K.ident_b[:].unsqueeze(1).broadcast_to([128, 4, 128]), op=ALU.add),
                  reads=[(kn('NN', d), g), 'ident_b'], writes=[(kn('PP', d), g)])
    NPs = L['NP']
    for lvl in range(1, 8):
        for d in range(2):
            NN, BB, PP, NPt = NNs[d], BBs[d], PPs[d], NPs[d]
            for g in range(2):
                nk_, bk_, pk_ = (kn('NN', d), g), (kn('BB', d), g), (kn('PP', d), g)
                gs = slice(g * 4, g * 4 + 4)
                if lvl == 1:
                    psn, pnk = bank(K)
                    for j in range(4):
                        u = g * 4 + j
                        fw.op('pe', lambda t, psn=psn, j=j, u=u: t.matmul(psn[:, j * 128:(j + 1) * 128], BB[:, u, :], NN[:, u, :], start=True, stop=True),
                              reads=[bk_, nk_], writes=[pnk])
                    psb_, pbk = bank(K)
                    for j in range(4):
                        u = g * 4 + j
                        fw.op('pe', lambda t, psb_=psb_, j=j, u=u: t.matmul(psb_[:, j * 128:(j + 1) * 128], NN[:, u, :], BB[:, u, :], start=True, stop=True),
                              reads=[bk_, nk_], writes=[pbk])
                    evac(K, 'act', NN[:, gs, :], psn[:, 0:512].rearrange("p (u t) -> p u t", u=4), [pnk], [nk_])
                    evac(K, 'act', BB[:, gs, :].rearrange("p u t -> p (u t)"), psb_[:, 0:512], [pbk], [bk_])
                elif lvl <= 5:
                    pm = [bank(K), bank(K)]
                    for j in range(4):
                        u = g * 4 + j
                        psm, pmk = pm[j // 2]
                        jj = j % 2
                        fw.op('pe', lambda t, psm=psm, jj=jj, u=u: t.matmul(psm[:, jj * 256:jj * 256 + 256], BB[:, u, :], NPt[:, u, :, :].rearrange("p a t -> p (a t)"),
                                                                            start=True, stop=True), reads=[bk_, nk_, pk_], writes=[pmk])
                    psb_, pbk = bank(K)
                    for j in range(4):
                        u = g * 4 + j
                        fw.op('pe', lambda t, psb_=psb_, j=j, u=u: t.matmul(psb_[:, j * 128:(j + 1) * 128], NN[:, u, :], BB[:, u, :], start=True, stop=True),
                              reads=[bk_, nk_], writes=[pbk])
                    for hb_ in range(2):
                        psm, pmk = pm[hb_]
                        u0 = g * 4 + hb_ * 2
                        v3 = psm[:, 0:512].rearrange("p (j c) -> p j c", j=2)
                        evac(K, 'act', NN[:, u0:u0 + 2, :], v3[:, :, 0:128], [pmk], [nk_])
                        fw.op('dve', lambda e, v3=v3, u0=u0: e.tensor_tensor(out=PP[:, u0:u0 + 2, :], in0=v3[:, :, 128:256], in1=PP[:, u0:u0 + 2, :], op=ALU.add),
                              reads=[pmk, pk_], writes=[pk_])
                    evac(K, 'dve' if (g + d + lvl) % 2 == 0 else 'act', BB[:, gs, :].rearrange("p u t -> p (u t)"), psb_[:, 0:512], [pbk], [bk_])
                else:
                    psp, ppk = bank(K)
                    for j in range(4):
                        u = g * 4 + j
                        fw.op('pe', lambda t, psp=psp, j=j, u=u: t.matmul(psp[:, j * 128:(j + 1) * 128], BB[:, u, :], PP[:, u, :], start=True, stop=True),
                              reads=[bk_, pk_], writes=[ppk])
                    if lvl == 6:
                        psb_, pbk = bank(K)
                        for j in range(4):
                            u = g * 4 + j
                            fw.op('pe', lambda t, psb_=psb_, j=j, u=u: t.matmul(psb_[:, j * 128:(j + 1) * 128], NN[:, u, :], BB[:, u, :], start=True, stop=True),
                                  reads=[bk_, nk_], writes=[pbk])
                    fw.op('dve', lambda e, psp=psp: e.tensor_tensor(out=PP[:, gs, :], in0=psp[:, 0:512].rearrange("p (u t) -> p u t", u=4),
                                                                   in1=PP[:, gs, :], op=ALU.add), reads=[ppk, pk_], writes=[pk_])
                    if lvl == 6:
                        evac(K, 'act', BB[:, gs, :].rearrange("p u t -> p (u t)"), psb_[:, 0:512], [pbk], [bk_])
    filler = L['hook_mid']()
    if getattr(K, 'no_interleave', False):
        fw.replay(filler, len(filler))
    if half == 0:
        base = [[0, 1], [2, 3]]
    else:
        base = [[0, 1, 2, 3]]
    seqs_d = [base, [list(reversed(x)) for x in base]]
    ns = len(base)
    nsteps = len(base[0])
    nfill = -(-len(filler) // (4 * nsteps))
    for d in range(2):
        for si in range(ns):
            fw.op('dve', lambda e, si=si, d=d: e.tensor_copy(out=ST32s[d][:, si, :], in_=xi_f[:]), reads=['xi_f'], writes=[kn('ST32', d)])
            fw.op('act', lambda e, si=si, d=d: e.activation(out=STbs[d][:, si, :], in_=xi_f[:], func=AF.Copy), reads=['xi_f'], writes=[kn('STb', d)])
    for step in range(nsteps):
        XB = {}
        for d in range(2):
            TL2, AQ, STb = TL2s[d], AQs[d], STbs[d]
            AQk = [(kn('AQ', d), u) for u in range(8)]
            XB[d] = [bank(K), bank(K)]
            for si in range(ns):
                c = seqs_d[d][si][step]
                for h2 in range(2):
                    pr = slice(h2 * 64, h2 * 64 + 64)
                    u = c * 2 + h2
                    psx, pxk = XB[d][h2]
                    fw.op('pe', lambda t, psx=psx, si=si, c=c, pr=pr: t.matmul(psx[:, si * 128:(si + 1) * 128], TL2[pr, c, 0, :], STb[pr, si, :], start=True, stop=False),
                          reads=[kn('TL2', d), kn('STb', d)], writes=[pxk])
                    fw.op('pe', lambda t, psx=psx, si=si, c=c, u=u, h2=h2: t.matmul(psx[:, si * 128:(si + 1) * 128], AQ[:, u, 1, :], Vaug[:, c, h2, :], start=False, stop=True),
                          reads=AQk + ['Vaug'], writes=[pxk])
        for d in range(2):
            for h2 in range(2):
                psx, pxk = XB[d][h2]
                fw.op('act' if d == 0 else 'dve',
                      (lambda e, psx=psx, h2=h2, d=d: e.activation(out=Xbs[d][:, h2 * ns:(h2 + 1) * ns, :].rearrange("p s t -> p (s t)"), in_=psx[:, 0:ns * 128],
                                                                   func=AF.Identity, scale=-1.0)) if d == 0 else
                      (lambda e, psx=psx, h2=h2, d=d: e.tensor_scalar(out=Xbs[d][:, h2 * ns:(h2 + 1) * ns, :].rearrange("p s t -> p (s t)"), in0=psx[:, 0:ns * 128],
                                                                      scalar1=-1.0, scalar2=None, op0=ALU.mult)),
                      reads=[pxk], writes=[kn('Xb', d)])
        fw.replay(filler, nfill)
        UB = {}
        for d in range(2):
            PP, Xb = PPs[d], Xbs[d]
            PPk = [(kn('PP', d), 0), (kn('PP', d), 1)]
            UB[d] = bank(K)
            psu, puk = UB[d]
            for si in range(ns):
                c = seqs_d[d][si][step]
                for h2 in range(2):
                    u = c * 2 + h2
                    q = h2 * ns + si
                    fw.op('pe', lambda t, q=q, u=u, psu=psu: t.matmul(psu[:, q * 128:(q + 1) * 128], PP[:, u, :], Xb[:, q, :], start=True, stop=True),
                          reads=PPk + [kn('Xb', d)], writes=[puk])
        for d in range(2):
            psu, puk = UB[d]
            evac(K, 'dve' if d == 0 else 'act', Ubs[d][:, 0:2 * ns, :].rearrange("p s t -> p (s t)"), psu[:, 0:2 * ns * 128], [puk], [kn('Ub', d)])
        fw.replay(filler, nfill)
        YB = {}; MB_ = {}
        for d in range(2):
            TL2, AQ, STb, Ub, TBK = TL2s[d], AQs[d], STbs[d], Ubs[d], TBKs[d]
            AQk = [(kn('AQ', d), u) for u in range(8)]
            YB[d] = [bank(K), bank(K)]
            for si in range(ns):
                c = seqs_d[d][si][step]
                for h2 in range(2):
                    pr = slice(h2 * 64, h2 * 64 + 64)
                    u = c * 2 + h2
                    q = h2 * ns + si
                    psy, pyk = YB[d][h2]
                    fw.op('pe', lambda t, psy=psy, si=si, c=c, pr=pr: t.matmul(psy[:, si * 128:(si + 1) * 128], STb[pr, si, :], TL2[pr, c, 1, :], start=True, stop=False),
                          reads=[kn('TL2', d), kn('STb', d)], writes=[pyk])
                    fw.op('pe', lambda t, psy=psy, si=si, q=q, u=u: t.matmul(psy[:, si * 128:(si + 1) * 128], Ub[:, q, :], AQ[:, u, 0, :], start=False, stop=False),
                          reads=AQk + [kn('Ub', d)], writes=[pyk])
                    fw.op('pe', lambda t, psy=psy, si=si, c=c, u=u, h2=h2: t.matmul(psy[:, si * 128:(si + 1) * 128], Vaug[:, c, h2, :], AQ[:, u, 2, :], start=False, stop=True),
                          reads=AQk + ['Vaug'], writes=[pyk])
            for si in range(ns):
                c = seqs_d[d][si][step]
                fw.op('dve', lambda e, si=si, c=c, d=d: e.tensor_scalar(out=STWs[d][:, si, :], in0=ST32s[d][:, si, :], scalar1=WCs[d][:, c:c + 1], scalar2=None, op0=ALU.mult),
                      reads=[kn('ST32', d), kn('WC', d)], writes=[kn('STW', d)])
            MB_[d] = bank(K)
            psm, pmk = MB_[d]
            for si in range(ns):
                c = seqs_d[d][si][step]
                for h2 in range(2):
                    pr = slice(h2 * 64, h2 * 64 + 64)
                    q = h2 * ns + si
                    fw.op('pe', lambda t, si=si, c=c, q=q, pr=pr, h2=h2, psm=psm: t.matmul(psm[pr, si * 128:(si + 1) * 128], TBK[:, c, 0, h2 * 64:(h2 + 1) * 64], Ub[:, q, :], start=True, stop=False),
                          reads=[kn('TBK', d), kn('Ub', d)], writes=[pmk])
                    fw.op('pe', lambda t, si=si, c=c, pr=pr, h2=h2, psm=psm: t.matmul(psm[pr, si * 128:(si + 1) * 128], TBK[:, c, 1, h2 * 64:(h2 + 1) * 64], Vaug[:, c, h2, :], start=False, stop=True),
                          reads=[kn('TBK', d), 'Vaug'], writes=[pmk])
        for d in range(2):
            for si in range(ns):
                c = seqs_d[d][si][step]
                tok = half * 512 + c * 128
                for h2 in range(2):
                    pr = slice(h2 * 64, h2 * 64 + 64)
                    po = slice((1 - h2) * 64, (1 - h2) * 64 + 64)
                    psy, pyk = YB[d][h2]
                    li = base[si].index(c)
                    first = (d == 0) == (li < len(base[si]) / 2)
                    if first:
                        fw.op('dve', lambda e, psy=psy, si=si, pr=pr, tok=tok: e.tensor_copy(out=yT[pr, p, tok:tok + 128], in_=psy[pr, si * 128:(si + 1) * 128]),
                              reads=[pyk], writes=[('yT%d' % p, tok)])
                    else:
                        fw.op('dve', lambda e, psy=psy, si=si, pr=pr, tok=tok: e.tensor_tensor(out=yT[pr, p, tok:tok + 128], in0=psy[pr, si * 128:(si + 1) * 128],
                                                                                              in1=yT[pr, p, tok:tok + 128], op=ALU.add),
                              reads=[pyk, ('yT%d' % p, tok)], writes=[('yT%d' % p, tok)])
                    if half == 1:
                        fw.op('act', lambda e, psy=psy, si=si, po=po, c=c, d=d: e.activation(out=YP[po, p, d, c, :], in_=psy[po, si * 128:(si + 1) * 128], func=AF.Copy),
                              reads=[pyk], writes=['YP'])
            psm, pmk = MB_[d]
            for si in range(ns):
                c = seqs_d[d][si][step]
                fw.op('dve', lambda e, si=si, c=c, d=d, psm=psm: e.scalar_tensor_tensor(out=ST32s[d][:, si, :], in0=psm[:, si * 128:(si + 1) * 128], scalar=WCs[d][:, c:c + 1],
                                                                                     in1=STWs[d][:, si, :], op0=ALU.mult, op1=ALU.add),
                      reads=[pmk, kn('WC', d), kn('STW', d)], writes=[kn('ST32', d)])
                fw.op('act', lambda e, si=si, d=d: e.activation(out=STbs[d][:, si, :], in_=ST32s[d][:, si, :], func=AF.Copy), reads=[kn('ST32', d)], writes=[kn('STb', d)])
        fw.replay(filler, 2 * nfill)
    fw.replay(filler, len(filler))
    for d in range(2):
        ST32 = ST32s[d]
        if half == 0:
            for si in range(ns):
                ps, pk = bank(K)
                fw.op('pe', lambda t, ps=ps, si=si: t.transpose(ps[:, 0:128], ST32[:, si, :], K.ident_f[:]), reads=[kn('ST32', d), 'ident_f'], writes=[pk])
                evac(K, 'dve', stS[:], ps[:, 0:128], [pk], ['stS'])
                for h2 in range(2):
                    pr = slice(h2 * 64, h2 * 64 + 64)
                    fw.dma('sp', O.ns[si, d, 2 * p + h2, :, :], stS[pr, h2 * 64:(h2 + 1) * 64], reads=['stS'])
        else:
            fw.op('dve', lambda e: e.tensor_copy(out=CC[:, p, d, :], in_=ST32[:, 0, :]), reads=[kn('ST32', d)], writes=['CC'])


def rwkv_finish(K, L):
    nc, fw, I, sb = K.nc, K.fw, K.I, K.sb
    PV = K.PV
    yT, YP, SXG, CC, S0T, G2b, bd1_f, xi_f = (L[k] for k in ('yT', 'YP', 'SXG', 'CC', 'S0T', 'G2b', 'bd1_f', 'xi_f'))
    mixT = K.mixT
    cc_out = L['cc_out']
    with Scope(K):
        CG = sb("CG", [128, 4, 8, 128])
        s0t = sb("s0t", [64, 8, 128])
        for p in range(4):
            for d in range(2):
                pd = p * 2 + d
                fw.dma('sp', s0t[:, pd, :].rearrange("v (h k) -> v h k", h=2), I.s0[d, 2 * p:2 * p + 2].rearrange("h v k -> v h k"), writes=[('s0t', pd)])
        for p in range(4):
            for d in range(2):
                pd = p * 2 + d
                ps, pk = bank(K)
                fw.op('pe', lambda t, ps=ps, pd=pd: t.transpose(ps[:, 0:64], s0t[:, pd, :], K.ident_f[0:64, 0:64]), reads=[('s0t', pd), 'ident_f'], writes=[pk])
                evac(K, 'dve' if pd % 2 == 0 else 'act', S0T[:, p, d, :], ps[:, 0:64], [pk], ['S0T'])
        SWt = sb("SWt", [128, 8, 128])
        XS = sb("XS", [128, 4, 8, 64])
        MIN = sb("MIN", [128, 8, 64]); MINb = sb("MINb", [128, 8, 64], BF16); MINsw = sb("MINsw", [128, 8, 64], BF16)
        xi_b = sb("xi_b", [128, 128], BF16)
        Wall = [[sb("W%d_%d" % (i, q), [128, 512]) for i in range(3)] for q in range(4)]
        fw.dma('pool', xi_b[:], I.xi[:, :], writes=['xi_b'])
        fw.dma('pool', CG[:].rearrange("p r a c -> p r (a c)"), cc_out.ap().rearrange("(r p) f -> p r f", p=128), reads=['cc_out'], writes=['CG'])
        fw.op('dve', lambda e: e.tensor_copy(out=XS[:, 0, 0:8:2, :], in_=S0T[:, :, 0, :]), reads=['S0T'], writes=['XS'])
        fw.op('dve', lambda e: e.tensor_copy(out=XS[:, 3, 1:8:2, :], in_=S0T[:, :, 1, :]), reads=['S0T'], writes=['XS'])
        for i in range(3):
            rank = (i, 3 - i)
            cur = (i, 3 - i)
            nxt = (i + 1, 2 - i)
            sw = [bank(K), bank(K)]
            for pd in range(8):
                d = pd % 2
                ps, pk = sw[pd // 4]
                for hh in range(2):
                    fw.op('pe', lambda t, ps=ps, hh=hh, pd=pd, d=d: t.matmul(ps[hh * 64:(hh + 1) * 64, (pd % 4) * 128:(pd % 4 + 1) * 128],
                                                                           CG[:, rank[d], pd, (1 - hh) * 64:(2 - hh) * 64], K.ident_f[:], start=True, stop=True),
                          reads=['CG', 'ident_f'], writes=[pk])
            for b_ in range(2):
                ps, pk = sw[b_]
                evac(K, 'act' if b_ == 0 else 'dve', SWt[:, b_ * 4:(b_ + 1) * 4, :].rearrange("p a b -> p (a b)"), ps[:, 0:512], [pk], ['SWt'])
            cb = [bank(K), bank(K)]
            for pd in range(8):
                d = pd % 2
                for h2 in range(2):
                    pr = slice(h2 * 64, h2 * 64 + 64)
                    ps2, pk2 = cb[h2]
                    fw.op('pe', lambda t, ps2=ps2, pr=pr, h2=h2, pd=pd, d=d: t.matmul(ps2[pr, pd * 64:(pd + 1) * 64], SWt[pr, pd, h2 * 64:(h2 + 1) * 64], XS[pr, cur[d], pd, :],
                                                                                   start=True, stop=True), reads=['SWt', 'XS'], writes=[pk2])
            for h2 in range(2):
                pr = slice(h2 * 64, h2 * 64 + 64)
                ps2, pk2 = cb[h2]
                for d in range(2):
                    fw.op('dve', lambda e, ps2=ps2, pr=pr, h2=h2, d=d: e.tensor_tensor(
                        out=XS[pr, nxt[d], d:8:2, :], in0=ps2[pr, 0:512].rearrange("p (a b) -> p a b", a=8)[:, d:8:2, :],
                        in1=CG[pr, rank[d], d:8:2, h2 * 64:(h2 + 1) * 64], op=ALU.add), reads=[pk2, 'CG'], writes=['XS'])
        fw.op('dve', lambda e: e.tensor_scalar(out=MIN[:].rearrange("p a b -> p (a b)"), in0=XS[:, 0, :, :].rearrange("p a b -> p (a b)"), scalar1=K.cst[:, 0:1], scalar2=None, op0=ALU.mult),
              reads=['XS', 'cst'], writes=['MIN'])
        for j in range(1, 4):
            fw.op('dve', lambda e, j=j: e.scalar_tensor_tensor(out=MIN[:].rearrange("p a b -> p (a b)"), in0=XS[:, j, :, :].rearrange("p a b -> p (a b)"), scalar=K.cst[:, j:j + 1],
                                                                in1=MIN[:].rearrange("p a b -> p (a b)"), op0=ALU.mult, op1=ALU.add),
                  reads=['XS', 'cst', 'MIN'], writes=['MIN'])
        dump(K, "MIN", MIN[:], [128, 8, 64], ['MIN'])
        fw.op('act', lambda e: e.activation(out=MINb[:], in_=MIN[:], func=AF.Copy), reads=['MIN'], writes=['MINb'])
        ps, pk = bank(K)
        fw.op('pe', lambda t: t.matmul(ps[:, 0:512], xi_b[:], MINb[:].rearrange("p a b -> p (a b)"), start=True, stop=True), reads=['xi_b', 'MINb'], writes=[pk])
        evac(K, 'dve', MINsw[:].rearrange("p a b -> p (a b)"), ps[:, 0:512], [pk], ['MINsw'])
        for p in range(4):
            for d in range(2):
                pd = p * 2 + d
                for h2 in range(2):
                    pr = slice(h2 * 64, h2 * 64 + 64)
                    po = slice((1 - h2) * 64, (1 - h2) * 64 + 64)
                    psc, pck = bank(K)
                    fw.op('pe', lambda t, psc=psc, pr=pr, po=po: t.matmul(psc[pr, 0:512], MINsw[po, pd, :], YP[po, p, d, :, :].rearrange("p c t -> p (c t)"), start=True, stop=True),
                          reads=['MINsw', 'YP'], writes=[pck])
                    fw.op('dve', lambda e, psc=psc, pr=pr: e.tensor_tensor(out=yT[pr, p, 512:1024], in0=psc[pr, 0:512], in1=yT[pr, p, 512:1024], op=ALU.add),
                          reads=[pck] + [('yT%d' % p, 512 + c_ * 128) for c_ in range(4)], writes=[('yT%d' % p, 512 + c_ * 128) for c_ in range(4)])
        dump(K, "yT", yT[:], [128, 4, 1024], [('yT%d' % p, t_) for p in range(4) for t_ in range(0, 1024, 128)])
        for batch in range(2):
            units = [(batch * 2 + (q // 2), q % 2, q) for q in range(4)]
            ykeys = lambda p, hb: [('yT%d' % p, hb * 512 + c_ * 128) for c_ in range(4)]
            hsl = lambda hb: slice(hb * 512, hb * 512 + 512)
            b1 = {}; b2 = {}; b3 = {}
            for (p, hb, q) in units:
                b1[q] = bank(K)
                fw.op('pe', lambda t, p=p, hb=hb, q=q: t.matmul(b1[q][0][:, 0:512], bd1_f[:], yT[:, p, hsl(hb)], start=True, stop=True),
                      reads=['bd1_f'] + ykeys(p, hb), writes=[b1[q][1]])
            for (p, hb, q) in units:
                W = Wall[q]
                fw.op('dve', lambda e, p=p, hb=hb, q=q, W=W: e.scalar_tensor_tensor(out=W[0][:], in0=b1[q][0][:, 0:512], scalar=-1.0 / 64, in1=yT[:, p, hsl(hb)],
                                                                                   op0=ALU.mult, op1=ALU.add), reads=[b1[q][1]] + ykeys(p, hb), writes=['W0_%d' % q])
            for (p, hb, q) in units:
                W = Wall[q]
                fw.op('act', lambda e, W=W: e.activation(out=W[1][:], in_=W[0][:], func=AF.Square), reads=['W0_%d' % q], writes=['W1_%d' % q])
            for (p, hb, q) in units:
                W = Wall[q]
                b2[q] = bank(K)
                fw.op('pe', lambda t, q=q, W=W: t.matmul(b2[q][0][:, 0:512], bd1_f[:], W[1][:], start=True, stop=True), reads=['bd1_f', 'W1_%d' % q], writes=[b2[q][1]])
            for (p, hb, q) in units:
                b3[q] = bank(K)
                fw.op('pe', lambda t, p=p, hb=hb, q=q: t.matmul(b3[q][0][:, 0:512], G2b[:, p * 128:(p + 1) * 128], SXG[:, hsl(hb)], start=True, stop=True),
                      reads=['G2b', 'SXG'], writes=[b3[q][1]])
            for (p, hb, q) in units:
                W = Wall[q]
                fw.op('act', lambda e, q=q, W=W: e.activation(out=W[1][:], in_=b2[q][0][:, 0:512], func=AF.Ln, bias=K.epsc[:, 1:2], scale=1.0 / 64),
                      reads=[b2[q][1], 'epsc'], writes=['W1_%d' % q])
            for (p, hb, q) in units:
                W = Wall[q]
                fw.op('act', lambda e, W=W: e.activation(out=W[1][:], in_=W[1][:], func=AF.Exp, scale=-0.5), reads=['W1_%d' % q], writes=['W1_%d' % q])
            for (p, hb, q) in units:
                W = Wall[q]
                fw.op('dve', lambda e, W=W: e.tensor_tensor(out=W[0][:], in0=W[0][:], in1=W[1][:], op=ALU.mult), reads=['W0_%d' % q, 'W1_%d' % q], writes=['W0_%d' % q])
            for (p, hb, q) in units:
                W = Wall[q]
                fw.op('act', lambda e, p=p, W=W: e.activation(out=W[2][:], in_=W[0][:], func=AF.Identity, bias=PV[:, PV_LNB + p:PV_LNB + p + 1], scale=PV[:, PV_LNG + p:PV_LNG + p + 1]),
                      reads=['W0_%d' % q, 'PV'], writes=['W2_%d' % q])
            for (p, hb, q) in units:
                W = Wall[q]
                fw.op('dve', lambda e, p=p, hb=hb, W=W: e.tensor_tensor(out=W[2][:], in0=W[2][:], in1=mixT[:, 4 + p, hsl(hb)], op=ALU.add),
                      reads=['W2_%d' % q, ('mixT', 4 + p)], writes=['W2_%d' % q])
            for (p, hb, q) in units:
                W = Wall[q]
                fw.op('dve', lambda e, p=p, hb=hb, q=q, W=W: e.tensor_tensor(out=mixT[:, 4 + p, hsl(hb)], in0=b3[q][0][:, 0:512], in1=W[2][:], op=ALU.mult),
                      reads=[b3[q][1], 'W2_%d' % q], writes=[('mixT', 4 + p)])
        if 'mixR' in K.dbg:
            md = sb("mixdbgR", [128, 4, 1024])
            fw.op('dve', lambda e: e.tensor_copy(out=md[:], in_=mixT[:, 4:8, :]), reads=[('mixT', 4 + p) for p in range(4)], writes=['mixdbgR'])
            dump(K, "mixR", md[:], [128, 4, 1024], ['mixdbgR'])


def emit_back(K, stage):
    nc, fw, I, O, sb = K.nc, K.fw, K.I, K.O, K.sb
    xT, mixT, PV, MOD, AB = K.xT, K.mixT, K.PV, K.MOD, K.AB
    xkeys = K.xkeys
    xk = lambda mo, hb: ('xT', hb * 512)
    XK = [('xT', c) for c in range(0, 1024, 128)]
    XKh = [[('xT', c) for c in range(0, 512, 128)], [('xT', c) for c in range(512, 1024, 128)]]
    hc_in = nc.dram_tensor("hc_in", [128, 16], F32)
    hc_out = nc.dram_tensor("hc_out", [512, 16], F32)
    XH = sb("XH", [128, 8, 2]); HG = sb("HG", [128, 4, 8, 2]); XHh = sb("XHh", [128, 8, 2])

    def emit_halo_select():
        for (dst, src, sel0) in ((0, 1, 4), (1, 0, 8)):
            fw.op('dve', lambda e: e.tensor_scalar(out=XHh[:, :, dst], in0=HG[:, 0, :, src], scalar1=K.cst[:, sel0:sel0 + 1], scalar2=None, op0=ALU.mult),
                  reads=['HG', 'cst'], writes=['XHh'])
            for r in range(1, 4):
                fw.op('dve', lambda e, r=r: e.scalar_tensor_tensor(out=XHh[:, :, dst], in0=HG[:, r, :, src], scalar=K.cst[:, sel0 + r:sel0 + r + 1],
                                                                    in1=XHh[:, :, dst], op0=ALU.mult, op1=ALU.add), reads=['HG', 'cst', 'XHh'], writes=['XHh'])

    with Scope(K):
        h2T = sb("h2T", [128, 8, 1026], BF16)
        sq = sb("sq2", [128, 8, 512], BF16); rstd = sb("rstd2", [128, 512]); rstdS = sb("rstd2S", [128, 512]); tmpn = sb("tmpn2", [128, 2, 512])
        K.wslab = [sb("wslabF%d" % i, [128, 8, 512], BF16) for i in range(2)]
        slabs = [load_w(K, [(I.w_out[:, g * 512:(g + 1) * 512], 512)]) for g in range(2)]
        A2 = lambda cond: (lambda c: AB[:, 1, c, cond:cond + 1])
        B2 = lambda cond: (lambda c: MOD[:, 24 + c, cond:cond + 1])
        for hb in range(2):
            for g in range(2):
                slab, sk = slabs[g]
                for m in range(4):
                    mo = g * 4 + m
                    ps, pk = bank(K)
                    for k in range(8):
                        fw.op('pe', lambda t, ps=ps, k=k, m=m, hb=hb, slab=slab: t.matmul(ps[:, 0:512], slab[:, k, m * 128:(m + 1) * 128], mixT[:, k, hb * 512:(hb + 1) * 512],
                                                                                       start=(k == 0), stop=(k == 7)), reads=[sk] + [('mixT', q) for q in range(8)], writes=[pk])
                    fw.op('dve', lambda e, ps=ps, mo=mo, hb=hb: e.scalar_tensor_tensor(
                        out=xT[:, mo, hb * 512:(hb + 1) * 512], in0=ps[:, 0:512], scalar=MOD[:, 16 + mo, hb:hb + 1],
                        in1=xT[:, mo, hb * 512:(hb + 1) * 512], op0=ALU.mult, op1=ALU.add), reads=[pk, 'MOD'] + XKh[hb], writes=XKh[hb])
            if hb == 0:
                rms_stats(K, sq, rstd, xT, XKh[0], 0, 512, 'rstd2')
        dump(K, "xmid", xT[:], [128, 8, 1024], XK)
        fw.op('dve', lambda e: e.tensor_copy(out=XH[:, :, 0], in_=xT[:, :, 512]), reads=XKh[1], writes=['XH'])
        fw.op('dve', lambda e: e.tensor_copy(out=XH[:, :, 1], in_=xT[:, :, 1023]), reads=XKh[1], writes=['XH'])
        fw.dma('pool', hc_in.ap(), XH[:].rearrange("p a b -> p (a b)"), reads=['XH'], writes=['hc_in'])
        fw.async_op('pool', lambda g: g.collective_compute("AllGather", ALU.bypass, replica_groups=[[0, 1, 2, 3], [4, 5, 6, 7]],
                                                           ins=[hc_in.ap().opt()], outs=[hc_out.ap().opt()]), reads=['hc_in'], writes=['hc_out'])
        fw.dma('pool', HG[:].rearrange("p r a b -> p r (a b)"), hc_out.ap().rearrange("(r p) f -> p r f", p=128), reads=['hc_out'], writes=['HG'])
        rms_apply(K, rstd, tmpn, xT, XKh[0], 0, 512, h2T, 0, ['h2T0'], A2(0), B2(0), 'rstd2')
        rms_stats(K, sq, rstdS, xT, XKh[1], 512, 512, 'rstd2S')
        rms_apply(K, rstdS, tmpn, xT, XKh[1], 512, 512, h2T, 512, ['h2T512'], A2(1), B2(1), 'rstd2S')
        emit_halo_select()
        rmsnorm_block(K, sq, rstd, tmpn, XHh, ['XHh'], 0, 2, h2T, 1024, ['h2T1024'], A2(1), B2(1), 'rstd2')
        H2K = ['h2T0', 'h2T512', 'h2T1024']
        HM = sb("HM", [128, 22, 1024], BF16)
        W2f = sb("W2f", [128, 22, 1024], BF16)

        def load_w2(q):
            fw.dma('pool', W2f[:, q * 6:min(22, (q + 1) * 6), :], I.wf2[q * 768:min(2816, (q + 1) * 768), :].rearrange("(k p) c -> p k c", p=128), writes=['W2f'])
        Zf = [sb("Zf%d" % i, [128, 1030], BF16) for i in range(2)]
        uf = sb("uf", [128, 1024]); sf = sb("sf", [128, 1024])
        for zi in range(2):
            fw.op('pool', lambda e, zi=zi: e.memset(Zf[zi][:, 0:516].rearrange("p (s t) -> p s t", s=2)[:, :, 0:258:257], 0.0), writes=['Zf%d' % zi])
        for g in range(11):
            slab, sk = load_w(K, [(I.w1[:, g * 256:(g + 1) * 256], 256), (I.w3[:, g * 256:(g + 1) * 256], 256)])
            if g in (2, 4, 6, 8):
                load_w2(g // 2 - 1)
            for m in range(2):
                f = g * 2 + m
                Zt = Zf[f % 2]; zk = 'Zf%d' % (f % 2)
                for (col0, kind) in ((0, 'P'), (512, 'S')):
                    ps, pk = bank(K)
                    for k in range(8):
                        fw.op('pe', lambda t, ps=ps, k=k, m=m, col0=col0: t.matmul(ps[:, 0:512], slab[:, k, m * 128:(m + 1) * 128], h2T[:, k, col0:col0 + 512],
                                                                                start=(k == 0), stop=(k == 7)), reads=[sk] + H2K, writes=[pk])
                    if kind == 'P':
                        evac(K, 'act', Zt[:, 0:516].rearrange("p (s t) -> p s t", s=2)[:, :, 1:257], ps[:, 0:512].rearrange("p (s t) -> p s t", s=2), [pk], [zk])
                    else:
                        evac(K, 'act', Zt[:, 517:1029], ps[:, 0:512], [pk], [zk])
                ps, pk = bank(K)
                for k in range(8):
                    fw.op('pe', lambda t, ps=ps, k=k, m=m: t.matmul(ps[:, 0:2], slab[:, k, m * 128:(m + 1) * 128], h2T[:, k, 1024:1026], start=(k == 0), stop=(k == 7)),
                          reads=[sk] + H2K, writes=[pk])
                fw.op('dve', lambda e, ps=ps: e.tensor_scalar(out=Zt[:, 516:517], in0=ps[:, 0:1], scalar1=K.cst[:, 12:13], scalar2=None, op0=ALU.mult),
                      reads=[pk, 'cst'], writes=[zk])
                fw.op('dve', lambda e, ps=ps: e.tensor_scalar(out=Zt[:, 1029:1030], in0=ps[:, 1:2], scalar1=K.cst[:, 13:14], scalar2=None, op0=ALU.mult),
                      reads=[pk, 'cst'], writes=[zk])
                w = [PV[:, PV_WC + j * 22 + f:PV_WC + j * 22 + f + 1] for j in range(3)]
                zP = lambda off: Zt[:, 0:516].rearrange("p (s t) -> p s t", s=2)[:, :, off:off + 256]
                zS = lambda off: Zt[:, 516 + off:516 + off + 512]
                uP = uf[:, 0:512].rearrange("p (s t) -> p s t", s=2)
                uS = uf[:, 512:1024]
                for (zv, uv) in ((zP, uP), (zS, uS)):
                    fw.op('act', lambda e, zv=zv, uv=uv: e.activation(out=uv, in_=zv(0), func=AF.Identity, scale=w[0]), reads=[zk, 'PV'], writes=['uf'])
                    fw.op('dve', lambda e, zv=zv, uv=uv: e.scalar_tensor_tensor(out=uv, in0=zv(1), scalar=w[1], in1=uv, op0=ALU.mult, op1=ALU.add),
                          reads=[zk, 'PV', 'uf'], writes=['uf'])
                    fw.op('dve', lambda e, zv=zv, uv=uv: e.scalar_tensor_tensor(out=uv, in0=zv(2), scalar=w[2], in1=uv, op0=ALU.mult, op1=ALU.add),
                          reads=[zk, 'PV', 'uf'], writes=['uf'])
                fw.op('act', lambda e: e.activation(out=sf[:], in_=uf[:], func=AF.Silu), reads=['uf'], writes=['sf'])
                for hb in range(2):
                    ps, pk = bank(K)
                    for k in range(8):
                        fw.op('pe', lambda t, ps=ps, k=k, m=m, hb=hb: t.matmul(ps[:, 0:512], slab[:, k, 256 + m * 128:256 + (m + 1) * 128], h2T[:, k, hb * 512:(hb + 1) * 512],
                                                                            start=(k == 0), stop=(k == 7)), reads=[sk] + H2K, writes=[pk])
                    fw.op('dve', lambda e, ps=ps, hb=hb, f=f: e.tensor_tensor(out=HM[:, f, hb * 512:(hb + 1) * 512], in0=ps[:, 0:512], in1=sf[:, hb * 512:(hb + 1) * 512], op=ALU.mult),
                          reads=[pk, 'sf'], writes=[('HM', f)])
        HMK = [('HM', f) for f in range(22)]
        for g in range(2):
            for m in range(4):
                mo = g * 4 + m
                for hb in range(2):
                    ps, pk = bank(K)
                    for f in range(22):
                        fw.op('pe', lambda t, ps=ps, f=f, mo=mo, hb=hb: t.matmul(ps[:, 0:512], W2f[:, f, mo * 128:(mo + 1) * 128], HM[:, f, hb * 512:(hb + 1) * 512],
                                                                            start=(f == 0), stop=(f == 21)), reads=['W2f'] + HMK, writes=[pk])
                    fw.op('dve', lambda e, ps=ps, mo=mo, hb=hb: e.scalar_tensor_tensor(
                        out=xT[:, mo, hb * 512:(hb + 1) * 512], in0=ps[:, 0:512], scalar=MOD[:, 40 + mo, hb:hb + 1],
                        in1=xT[:, mo, hb * 512:(hb + 1) * 512], op0=ALU.mult, op1=ALU.add), reads=[pk, 'MOD'] + XK, writes=XK)
    with Scope(K):
        sq = sb("sq3", [128, 8, 512], BF16); rstd = sb("rstd3", [128, 512]); tmpn = sb("tmpn3", [128, 2, 512])
        YFs = [sb("YF%d" % i, [128, 8, 512]) for i in range(2)]
        rstds3 = [rstd, sb("rstd3b", [128, 512])]
        OT = [sb("OT%d" % i, [128, 1024]) for i in range(2)]
        oc = 0
        for hb in range(2):
            rms_stats(K, sq, rstds3[hb], xT, XK, hb * 512, 512, 'rstd3_%d' % hb)
        for hb in range(2):
            rms_apply(K, rstds3[hb], tmpn, xT, XK, hb * 512, 512, YFs[hb], 0, ['YF%d' % hb],
                      lambda c: PV[:, PV_NORMF + c:PV_NORMF + c + 1], None, 'rstd3_%d' % hb)
        for hb, dst in ((0, O.yp), (1, O.ys)):
            YF = YFs[hb]
            for tb in range(4):
                ot = OT[oc % 2]; otk = 'OT%d' % (oc % 2); oc += 1
                for half in range(2):
                    ps, pk = bank(K)
                    for c4 in range(4):
                        c = half * 4 + c4
                        fw.op('pe', lambda t, ps=ps, c=c, c4=c4, tb=tb: t.transpose(ps[:, c4 * 128:(c4 + 1) * 128], YF[:, c, tb * 128:(tb + 1) * 128], K.ident_f[:]),
                              reads=['YF%d' % hb, 'ident_f'], writes=[pk])
                    evac(K, 'act' if half == 0 else 'dve', ot[:, half * 512:(half + 1) * 512], ps[:, 0:512], [pk], [otk])
                fw.dma('sp', dst[tb * 128:(tb + 1) * 128, :], ot[:], reads=[otk])


def _prep_inputs(inp):
    f = lambda a: np.ascontiguousarray(np.asarray(a, dtype=np.float32))
    x_prompt = f(inp['x_prompt']); x_sample = f(inp['x_sample'])
    shared = {}
    w_ada = f(inp['w_ada'][0]); b_ada = f(inp['b_ada'][0]); shared['w_in'] = f(inp['w_in'][0])
    shared['w2'] = f(inp['w2'][0]); shared['a2'] = f(inp['a2'][0]); shared['g2'] = f(inp['g2'][0])
    shared['w_out'] = f(inp['w_out'][0]); shared['w1'] = f(inp['w_ffn1'][0]); shared['w3'] = f(inp['w_ffn3'][0])
    shared['wf2'] = f(inp['w_ffn2'][0])
    shared['ident'] = np.eye(128, dtype=np.float32)
    bd = np.zeros((128, 128), np.float32); bd[:64, :64] = 1; bd[64:, 64:] = 1
    shared['bd1'] = bd
    s = np.arange(128)[:, None]; t = np.arange(128)[None, :]
    mq = np.zeros((128, 2, 4, 128), np.float32)
    for d, (strict, incl) in enumerate((((s < t), (s <= t)), ((s > t), (s >= t)))):
        mq[:, d, 0] = -1.0 * strict
        mq[:, d, 1] = 1.0 * incl
        mq[:, d, 2] = 1.0 * strict
        mq[:, d, 3] = 1.0 * incl
    shared['mq'] = mq.reshape(128, 1024)
    mb = np.zeros((128, 2, 128), np.float32)
    mb[:, 0] = -1.0 * (t < s)
    mb[:, 1] = -1.0 * (t > s)
    shared['mb'] = mb.reshape(128, 256)
    rst = np.ones((128, 512), np.float32); rst[:, ::128] = 0
    shared['rst'] = rst
    xi = np.zeros((128, 128), np.float32); xi[np.arange(128), (np.arange(128) + 64) % 128] = 1
    shared['xi'] = xi
    rpb = f(inp['rpb'][0])
    bt = np.full((128, 2, 8, 8, 64), NEG, np.float32)
    jj = np.arange(64)
    cs_ = np.clip(jj - 8, 0, 48)
    for par in range(2):
        for blk in range(8):
            for p in range(128):
                rel = (-8 if par == 0 else -7) + 2 * blk + p // 64
                ap = rel + 7
                kc = p % 64
                if ap < 0 or ap > 14:
                    continue
                ok = (kc >= cs_) & (kc < cs_ + 16)
                co = kc - jj + 15
                vals = rpb[:, ap, np.clip(co, 0, 30)]
                bt[p, par, blk] = np.where(ok[None, :], vals, NEG)
    shared['bt'] = bt.reshape(128, -1)
    pv_common = np.zeros((PV_ROWS, 128), np.float32)
    pv_common[PV_NORM1:PV_NORM1 + 8] = f(inp['norm1'][0]).reshape(8, 128)
    pv_common[PV_NORM2:PV_NORM2 + 8] = f(inp['norm2'][0]).reshape(8, 128)
    pv_common[PV_NORMF:PV_NORMF + 8] = f(inp['norm_f']).reshape(8, 128)
    pv_common[PV_WTS:PV_WTS + 45] = f(inp['w_ts'][0]).reshape(45, 128)
    pv_common[PV_W0:PV_W0 + 8] = f(inp['w0'][0]).reshape(8, 128)
    pv_common[PV_A0:PV_A0 + 8] = f(inp['a0'][0]).reshape(8, 128)
    pv_common[PV_KK:PV_KK + 4] = f(inp['k_k'][0]).reshape(4, 128)
    pv_common[PV_KA:PV_KA + 4] = f(inp['k_a'][0]).reshape(4, 128)
    pv_common[PV_RK:PV_RK + 4] = f(inp['r_k'][0]).reshape(4, 128)
    pv_common[PV_LNG:PV_LNG + 4] = f(inp['ln_x_g'][0]).reshape(4, 128)
    pv_common[PV_LNB:PV_LNB + 4] = f(inp['ln_x_b'][0]).reshape(4, 128)
    pv_common[PV_WC:PV_WC + 66] = f(inp['w_ffn_conv'][0]).reshape(66, 128)
    cache_k = f(inp['cache_k']); cache_v = f(inp['cache_v']); st = f(inp['state_rwkv'])
    cvec = f(inp['c']); c_ctx = f(inp['c_ctx'])
    in_maps = []
    for c in range(8):
        b, j = c // 4, c % 4
        m = dict(shared)
        m['xp'] = x_prompt[2 * c:2 * c + 2].reshape(512, D)
        m['xs'] = x_sample[b, 512 * j:512 * j + 512]
        xh = np.zeros((448, D), np.float32)
        lo = 512 * j - 256
        if lo >= 0:
            xh[0:256] = x_sample[b, lo:lo + 256]
        hi = 512 * j + 512
        if hi + 192 <= 2048:
            xh[256:448] = x_sample[b, hi:hi + 192]
        m['xh'] = xh
        m['ck'] = cache_k[b, 0].reshape(512, 512)
        m['cv'] = cache_v[b, 0].reshape(512, 512)
        m['s0'] = st[b, 0]
        pv = pv_common.copy()
        pv[PV_C:PV_C + 8] = c_ctx.reshape(8, 128)
        pv[PV_C + 8:PV_C + 16] = cvec[0].reshape(8, 128)
        pv[PV_C + 16:PV_C + 24] = cvec[1].reshape(8, 128)
        pv[PV_BADA:PV_BADA + 12] = b_ada[j * 1536:(j + 1) * 1536].reshape(12, 128)
        m['w_ada_sh'] = np.ascontiguousarray(w_ada[:, j * 1536:(j + 1) * 1536])
        m['pvec'] = pv
        rb = np.full((128, 8, 8), NEG, np.float32)
        for l in range(8):
            i = 8 * j + l
            si = min(max(i - 4, 0), 24)
            par = l % 2
            for blk in range(8):
                for half in range(2):
                    rel = (-8 if par == 0 else -7) + 2 * blk + half
                    kr = i + rel
                    if si <= kr < si + 8:
                        rb[half * 64:(half + 1) * 64, l, blk] = 0.0
        m['rbp'] = rb.reshape(128, 64)
        cst = np.zeros((128, 16), np.float32)
        cst[:, 0 + j] = 1.0
        cst[:, 14 + b] = 1.0
        if j > 0:
            cst[:, 4 + (j - 1)] = 1.0; cst[:, 12] = 1.0
        if j < 3:
            cst[:, 8 + (j + 1)] = 1.0; cst[:, 13] = 1.0
        m['cst'] = cst
        in_maps.append(m)
    return in_maps


_NC_CACHE = {}


def kernel(**inputs):
    in_maps = _prep_inputs(inputs)
    if 'nc' not in _NC_CACHE:
        _NC_CACHE['nc'] = build_nc()[0]
    nc = _NC_CACHE['nc']
    res = run_bass_kernel_spmd(nc, in_maps, core_ids=list(range(8)))
    R = res.results
    y_prompt = np.concatenate([R[c]['yp'].reshape(2, 256, D) for c in range(8)], 0)
    y_sample = np.stack([np.concatenate([R[b * 4 + j]['ys'] for j in range(4)], 0) for b in range(2)], 0)
    nk = np.concatenate([R[c]['nk'].reshape(2, 1, 256, 8, 64) for c in range(8)], 0)
    nv = np.concatenate([R[c]['nv'].reshape(2, 1, 256, 8, 64) for c in range(8)], 0)
    ns = np.concatenate([R[c]['ns'].reshape(2, 1, 2, 8, 64, 64) for c in range(8)], 0)
    return (y_prompt.astype(np.float32), y_sample.astype(np.float32), nk.astype(np.float32),
            nv.astype(np.float32), ns.astype(np.float32))
```
